# Optimizing a Trainium2 kernel written in Bass

```python
import math
import jax
import jax.numpy as jnp
from jax import lax
import numpy as np

D_MODEL = 1024
BATCH = 1
SEQ = 16384
DEPTH = 2
DEC_BATCH = 16
DEC_SEQ = 16
PAST_LEN = 4096

CHUNK = 64
D_MIX = 1024
HEAD_DIM = 64
A_HEADS = 8
A_WIDTH = 512
A_PREV_CHUNKS = 8
REL_CLIP = 128
B_WIDTH = 256
CONV_WIDTH = 3
C_HEADS = 4
C_QK_DIM = 32
C_V_DIM = 64
C_WIDTH = 256
ROPE_DIMS = 8
ROPE_THETA = 500000.0
Q_BLOCK = 128
NORM_EPS = 1e-6
SUBLN_EPS = 1e-5
SPLITS = (512, 512, 512, 512, 256, 256, 256, 256, 256, 256, 256, 256)
D_IN_PROJ = 4096

kernel_name = 'chunk_hybrid_stream_encoder_step'


def rmsnorm(x, g, eps=NORM_EPS):
    xf = x.astype(jnp.float32)
    y = xf * lax.rsqrt(jnp.mean(xf * xf, axis=-1, keepdims=True) + eps)
    return (y * g.astype(jnp.float32)).astype(x.dtype)


def split_projection(h, w_in):
    z = jnp.einsum('bsd,de->bse', h, w_in)
    bounds = np.cumsum(np.array(SPLITS))[:-1].tolist()
    return jnp.split(z, bounds, axis=-1)


def partial_rope(x, pos):
    half = ROPE_DIMS // 2
    inv_freq = ROPE_THETA ** (-jnp.arange(half, dtype=jnp.float32) * (2.0 / ROPE_DIMS))
    ang = pos.astype(jnp.float32)[:, None] * inv_freq[None, :]
    cos = jnp.cos(ang)[:, None, None, :]
    sin = jnp.sin(ang)[:, None, None, :]
    xr = x[..., :ROPE_DIMS].astype(jnp.float32)
    x1, x2 = xr[..., :half], xr[..., half:]
    rot = jnp.concatenate([x1 * cos - x2 * sin, x2 * cos + x1 * sin], axis=-1).astype(x.dtype)
    return jnp.concatenate([rot, x[..., ROPE_DIMS:]], axis=-1)


def rel_bias_lookup(table, dist):
    idx = jnp.clip(dist, -REL_CLIP, REL_CLIP) + REL_CLIP
    return table.astype(jnp.float32)[:, idx]


def band_attention_prompt(q, k, v, rel_bias):
    B, S, H, Dh = q.shape
    nc = S // CHUNK
    band = (A_PREV_CHUNKS + 1) * CHUNK
    pad = ((0, 0), (A_PREV_CHUNKS * CHUNK, 0), (0, 0), (0, 0))
    kp = jnp.pad(k, pad).reshape(B, nc + A_PREV_CHUNKS, CHUNK, H, Dh)
    vp = jnp.pad(v, pad).reshape(B, nc + A_PREV_CHUNKS, CHUNK, H, Dh)
    kb = jnp.concatenate([kp[:, j:j + nc] for j in range(A_PREV_CHUNKS + 1)], axis=2)
    vb = jnp.concatenate([vp[:, j:j + nc] for j in range(A_PREV_CHUNKS + 1)], axis=2)
    qc = q.reshape(B, nc, CHUNK, H, Dh)
    s = jnp.einsum('bcqhd,bckhd->bchqk', qc, kb).astype(jnp.float32) * (Dh ** -0.5)
    r = jnp.arange(band)
    dist = jnp.arange(CHUNK)[:, None] + A_PREV_CHUNKS * CHUNK - r[None, :]
    s = s + rel_bias_lookup(rel_bias, dist)[None, None]
    key_pos = jnp.arange(nc)[:, None] * CHUNK - A_PREV_CHUNKS * CHUNK + r[None, :]
    s = jnp.where((key_pos >= 0)[None, :, None, None, :], s, -jnp.inf)
    p = jax.nn.softmax(s, axis=-1).astype(v.dtype)
    o = jnp.einsum('bchqk,bckhd->bcqhd', p, vb)
    return o.reshape(B, S, H * Dh)


def band_attention_sample(q, k, v, cache_k, cache_v, rel_bias):
    B, T, H, Dh = q.shape
    W = cache_k.shape[1]
    kk = jnp.concatenate([cache_k, k], axis=1)
    vv = jnp.concatenate([cache_v, v], axis=1)
    s = jnp.einsum('bqhd,bkhd->bhqk', q, kk).astype(jnp.float32) * (Dh ** -0.5)
    dist = jnp.arange(T)[:, None] + W - jnp.arange(W + T)[None, :]
    s = s + rel_bias_lookup(rel_bias, dist)[None]
    p = jax.nn.softmax(s, axis=-1).astype(v.dtype)
    o = jnp.einsum('bhqk,bkhd->bqhd', p, vv)
    return o.reshape(B, T, H * Dh)


def causal_conv(up, w, T):
    out = up[:, 0:T] * w[0]
    for j in range(1, CONV_WIDTH):
        out = out + up[:, j:j + T] * w[j]
    return out


def diff_lambda(lq1, lk1, lq2, lk2, lam_init):
    f = lambda a: a.astype(jnp.float32)
    return jnp.exp(jnp.sum(f(lq1) * f(lk1))) - jnp.exp(jnp.sum(f(lq2) * f(lk2))) + lam_init


def diff_combine(s, lam, v):
    p = jax.nn.softmax(s, axis=-1)
    a = p[:, :, 0] - lam * p[:, :, 1]
    return jnp.einsum('bhqk,bkhe->bqhe', a.astype(v.dtype), v)


def diff_attention_prompt(q, k, v, lam):
    B, S, H, _, d = q.shape
    nb = S // Q_BLOCK
    q_blocks = jnp.moveaxis(q.reshape(B, nb, Q_BLOCK, H, 2, d), 1, 0)
    key_chunk = jnp.arange(S) // CHUNK

    def one_block(args):
        qb, bi = args
        q_chunk = (bi * Q_BLOCK + jnp.arange(Q_BLOCK)) // CHUNK
        s = jnp.einsum('bqhmd,bkhmd->bhmqk', qb, k).astype(jnp.float32) * (d ** -0.5)
        s = jnp.where((key_chunk[None, :] <= q_chunk[:, None])[None, None, None], s, -jnp.inf)
        return diff_combine(s, lam, v)

    o = lax.map(one_block, (q_blocks, jnp.arange(nb)))
    return jnp.moveaxis(o, 0, 1).reshape(B, S, H, -1)


def diff_attention_sample(q, k, v, cache_k, cache_v, lam):
    d = q.shape[-1]
    kk = jnp.concatenate([cache_k, k], axis=1)
    vv = jnp.concatenate([cache_v, v], axis=1)
    s = jnp.einsum('bqhmd,bkhmd->bhmqk', q, kk).astype(jnp.float32) * (d ** -0.5)
    return diff_combine(s, lam, vv)


def merge_branches(o_a, a_g, o_b, b_g, o_c, c_g, w_out):
    y = jnp.concatenate([o_a * jax.nn.silu(a_g), o_b * jax.nn.silu(b_g), o_c * jax.nn.silu(c_g)], axis=-1)
    return jnp.einsum('bse,ed->bsd', y, w_out)


def prompt_layer(x, g, w_in, w_out, rel_bias, conv_w, subln_g, lam, lam_init):
    B, S, _ = x.shape
    h = rmsnorm(x, g)
    a_q, a_k, a_v, a_g, b_b, b_c, b_h, b_g, c_q, c_k, c_v, c_g = split_projection(h, w_in)
    k = a_k.reshape(B, S, A_HEADS, HEAD_DIM)
    v = a_v.reshape(B, S, A_HEADS, HEAD_DIM)
    o_a = band_attention_prompt(a_q.reshape(B, S, A_HEADS, HEAD_DIM), k, v, rel_bias)
    keep = min(A_PREV_CHUNKS * CHUNK, S)
    up = jnp.pad(b_c * b_h, ((0, 0), (CONV_WIDTH - 1, 0), (0, 0)))
    o_b = b_b * causal_conv(up, conv_w, S)
    pos = jnp.arange(S)
    cq = partial_rope(c_q.reshape(B, S, C_HEADS, 2, C_QK_DIM), pos)
    ck = partial_rope(c_k.reshape(B, S, C_HEADS, 2, C_QK_DIM), pos)
    cv = c_v.reshape(B, S, C_HEADS, C_V_DIM)
    o = diff_attention_prompt(cq, ck, cv, lam)
    o_c = (rmsnorm(o, subln_g, SUBLN_EPS) * (1.0 - lam_init)).reshape(B, S, C_WIDTH)
    y = x + merge_branches(o_a, a_g, o_b, b_g, o_c, c_g, w_out)
    return y, (k[:, S - keep:], v[:, S - keep:], up[:, -(CONV_WIDTH - 1):], ck, cv)


def sample_layer(x, ca_k, ca_v, c_conv, cc_k, cc_v, g, w_in, w_out, rel_bias, conv_w, subln_g, lam, lam_init):
    B, T, _ = x.shape
    past = cc_k.shape[1]
    h = rmsnorm(x, g)
    a_q, a_k, a_v, a_g, b_b, b_c, b_h, b_g, c_q, c_k, c_v, c_g = split_projection(h, w_in)
    k = a_k.reshape(B, T, A_HEADS, HEAD_DIM)
    v = a_v.reshape(B, T, A_HEADS, HEAD_DIM)
    o_a = band_attention_sample(a_q.reshape(B, T, A_HEADS, HEAD_DIM), k, v, ca_k, ca_v, rel_bias)
    up = jnp.concatenate([c_conv, b_c * b_h], axis=1)
    o_b = b_b * causal_conv(up, conv_w, T)
    pos = past + jnp.arange(T)
    cq = partial_rope(c_q.reshape(B, T, C_HEADS, 2, C_QK_DIM), pos)
    ck = partial_rope(c_k.reshape(B, T, C_HEADS, 2, C_QK_DIM), pos)
    cv = c_v.reshape(B, T, C_HEADS, C_V_DIM)
    o = diff_attention_sample(cq, ck, cv, cc_k, cc_v, lam)
    o_c = (rmsnorm(o, subln_g, SUBLN_EPS) * (1.0 - lam_init)).reshape(B, T, C_WIDTH)
    y = x + merge_branches(o_a, a_g, o_b, b_g, o_c, c_g, w_out)
    return y, (k, v, up[:, -(CONV_WIDTH - 1):], ck, cv)


def setup_inputs(seed: int = 0) -> dict:
    key = jax.random.key(seed)
    ks = jax.random.split(key, 20)
    a_win = min(A_PREV_CHUNKS * CHUNK, PAST_LEN)

    def nrm(k, shape, scale=1.0):
        return scale * jax.random.normal(k, shape, dtype=jnp.float32)

    return {
        'x_prompt': nrm(ks[0], (BATCH, SEQ, D_MODEL)),
        'x_sample': nrm(ks[1], (DEC_BATCH, DEC_SEQ, D_MODEL)),
        'cache_a_k': nrm(ks[2], (DEPTH, DEC_BATCH, a_win, A_HEADS, HEAD_DIM)),
        'cache_a_v': nrm(ks[3], (DEPTH, DEC_BATCH, a_win, A_HEADS, HEAD_DIM)),
        'state_conv': nrm(ks[4], (DEPTH, DEC_BATCH, CONV_WIDTH - 1, B_WIDTH)),
        'cache_c_k': nrm(ks[5], (DEPTH, DEC_BATCH, PAST_LEN, C_HEADS, 2, C_QK_DIM)),
        'cache_c_v': nrm(ks[6], (DEPTH, DEC_BATCH, PAST_LEN, C_HEADS, C_V_DIM)),
        'norm_g': 1.0 + nrm(ks[7], (DEPTH, D_MODEL), 0.05),
        'w_in': nrm(ks[8], (DEPTH, D_MODEL, D_IN_PROJ), D_MODEL ** -0.5),
        'w_out': nrm(ks[9], (DEPTH, D_MIX, D_MODEL), D_MIX ** -0.5),
        'rel_bias': nrm(ks[10], (DEPTH, A_HEADS, 2 * REL_CLIP + 1), 0.2),
        'conv_w': nrm(ks[11], (DEPTH, CONV_WIDTH, B_WIDTH), CONV_WIDTH ** -0.5),
        'lam_q1': nrm(ks[12], (DEPTH, C_QK_DIM), 0.1),
        'lam_k1': nrm(ks[13], (DEPTH, C_QK_DIM), 0.1),
        'lam_q2': nrm(ks[14], (DEPTH, C_QK_DIM), 0.1),
        'lam_k2': nrm(ks[15], (DEPTH, C_QK_DIM), 0.1),
        'subln_g': 1.0 + nrm(ks[16], (DEPTH, C_V_DIM), 0.05),
        'final_g': 1.0 + nrm(ks[17], (D_MODEL,), 0.05),
    }


def reference(x_prompt, x_sample, cache_a_k, cache_a_v, state_conv, cache_c_k, cache_c_v,
              norm_g, w_in, w_out, rel_bias, conv_w, lam_q1, lam_k1, lam_q2, lam_k2, subln_g, final_g):
    xp, xs = x_prompt, x_sample
    p_ak, p_av, p_conv, p_ck, p_cv = [], [], [], [], []
    s_ak, s_av, s_conv, s_ck, s_cv = [], [], [], [], []
    for l in range(DEPTH):
        lam_init = 0.8 - 0.6 * math.exp(-0.3 * l)
        lam = diff_lambda(lam_q1[l], lam_k1[l], lam_q2[l], lam_k2[l], lam_init)
        xp, (ak, av, cs, ck, cv) = prompt_layer(xp, norm_g[l], w_in[l], w_out[l], rel_bias[l],
                                                conv_w[l], subln_g[l], lam, lam_init)
        p_ak.append(ak); p_av.append(av); p_conv.append(cs); p_ck.append(ck); p_cv.append(cv)
        xs, (ak, av, cs, ck, cv) = sample_layer(xs, cache_a_k[l], cache_a_v[l], state_conv[l],
                                                cache_c_k[l], cache_c_v[l], norm_g[l], w_in[l], w_out[l],
                                                rel_bias[l], conv_w[l], subln_g[l], lam, lam_init)
        s_ak.append(ak); s_av.append(av); s_conv.append(cs); s_ck.append(ck); s_cv.append(cv)
    y_prompt = rmsnorm(xp, final_g)
    y_sample = rmsnorm(xs, final_g)
    return (y_prompt, y_sample,
            jnp.stack(p_ak), jnp.stack(p_av), jnp.stack(p_conv), jnp.stack(p_ck), jnp.stack(p_cv),
            jnp.stack(s_ak), jnp.stack(s_av), jnp.stack(s_conv), jnp.stack(s_ck), jnp.stack(s_cv))
```

```python
import math
from contextlib import ExitStack
import numpy as np
import concourse.bass as bass
import concourse.mybir as mybir
from concourse.bass_utils import run_bass_kernel_spmd

F32 = mybir.dt.float32
BF16 = mybir.dt.bfloat16
AF = mybir.ActivationFunctionType
ALU = mybir.AluOpType
AX = mybir.AxisListType

S_P = 16384
NT_P = 128
T_S = 16
PAST = 4096
AWIN = 512
NEG = -30000.0
C_SCALE = 32 ** -0.5
NCORES = 8
NSEQ = 2
NSLOT = 4


class _Rec:
    def __init__(self):
        self.calls = []

    def __getattr__(self, name):
        def call(*a, **k):
            self.calls.append((name, a, k))
            return self
        return call


class Tracker:
    def __init__(self, nc, es):
        self.nc = nc
        self.ops = {e: [] for e in ("pe", "act", "dve", "pool", "sp")}
        self.csem = {e: es.enter_context(nc.semaphore("c_" + e)) for e in ("pe", "act", "dve", "pool")}
        self.ccnt = {e: 0 for e in self.csem}
        self.dsem = {q: [es.enter_context(nc.semaphore("d_%s%d" % (q, i))) for i in range(8)] for q in ("sp", "pool")}
        self.dcnt = {q: 0 for q in self.dsem}
        self.seen = {e: {} for e in self.ops}
        self.lastw = {}
        self.readers = {}
        self.out_tokens = []

    def _deps(self, reads, writes):
        deps = []
        for k in list(reads) + list(writes):
            if k in self.lastw:
                deps.append(self.lastw[k])
        for k in writes:
            deps.extend(self.readers.get(k, []))
        return deps

    def _update(self, tok, reads, writes):
        for k in writes:
            self.lastw[k] = tok
            self.readers[k] = []
        for k in reads:
            self.readers.setdefault(k, []).append(tok)

    def _waits(self, eng, deps, skip_self_sem=None):
        need = {}
        for (sem, val, seng) in deps:
            if skip_self_sem is not None and seng == skip_self_sem:
                continue
            key = id(sem)
            if self.seen[eng].get(key, 0) >= val:
                continue
            if key not in need or need[key][1] < val:
                need[key] = (sem, val)
        for key, (sem, val) in need.items():
            self.seen[eng][key] = val
        return list(need.values())

    def op(self, eng, fn, reads=(), writes=()):
        deps = self._deps(reads, writes)
        waits = self._waits(eng, deps, skip_self_sem=("pe" if eng == "pe" else None))
        self.ccnt[eng] += 1
        sem, val = self.csem[eng], self.ccnt[eng]

        rec = _Rec()
        fn(rec)
        name, a, k = rec.calls[0]

        def emit(e, waits=waits, name=name, a=a, k=k, sem=sem):
            for (s, v) in waits:
                e.wait_ge(s, v)
            getattr(e, name)(*a, **k).then_inc(sem, 1)
        self.ops[eng].append(emit)
        self._update((sem, val, eng), reads, writes)

    def dma(self, q, out, in_, reads=(), writes=(), is_output=False, percore=False):
        deps = self._deps(reads, writes)
        i = self.dcnt[q]
        self.dcnt[q] += 1
        sem = self.dsem[q][i % 8]
        val = (i // 8 + 1) * 16
        if i >= 8:
            deps.append((sem, val - 16, "dq"))
        waits = self._waits(q, deps)

        def emit(e, c=None, waits=waits, sem=sem, out=out, in_=in_):
            for (s, v) in waits:
                e.wait_ge(s, v)
            i_ = in_(c) if callable(in_) else in_
            e.dma_start(out=out, in_=i_).then_inc(sem, 16)
        emit.percore = percore
        self.ops[q].append(emit)
        tok = (sem, val, "dq")
        self._update(tok, reads, writes)
        if is_output:
            self.out_tokens.append(tok)

    def finish(self):
        for q in ("sp", "pool"):
            toks = []
            n = self.dcnt[q]
            for s in range(min(8, n)):
                last_i = ((n - 1 - s) // 8) * 8 + s
                toks.append((self.dsem[q][s], (last_i // 8 + 1) * 16, "dq"))
            if q == "sp":
                toks = toks + self.out_tokens
            waits = self._waits(q, toks)

            def emit(e, waits=waits):
                for (s, v) in waits:
                    e.wait_ge(s, v)
            self.ops[q].append(emit)


def build_nc():
    nc = bass.Bass("TRN2", target_bir_lowering=False)

    def din(name, shape, dt=F32):
        return nc.dram_tensor(name, list(shape), dt, kind="ExternalInput").ap()

    def dout(name, shape):
        return nc.dram_tensor(name, list(shape), F32, kind="ExternalOutput").ap()

    def dscr(name, shape, dt):
        return nc.dram_tensor(name, list(shape), dt).ap()

    I = dict(
        xp=din("xp", [S_P, 1024]), xs=din("xs", [NSEQ, T_S, 1024]),
        cak=din("cak", [2, NSEQ, AWIN, 512]), cav=din("cav", [2, NSEQ, AWIN, 512]),
        scv=din("scv", [2, NSEQ, 2, 256]),
        cck=din("cck", [2, NSEQ, PAST, 256]), ccv=din("ccv", [2, NSEQ, PAST, 256]),
        ng=din("ng", [2, 128, 8]), win=din("win", [2, 1024, 4096]), wout=din("wout", [2, 1024, 1024]),
        biasp=din("biasp", [2, 128, 8, 5, 128]), biass=din("biass", [2, 128, 8, 5, T_S]),
        convw=din("convw", [2, 3, 128, 256]),
        lam=din("lam", [2, 4, 128, 32]), subg=din("subg", [2, 128, 64]), fing=din("fing", [128, 1024]),
        ropep=din("ropep", [S_P, 128]), ropes=din("ropes", [T_S, 128]),
        idb=din("idb", [128, 128], BF16), idf=din("idf", [128, 128]),
        cmask=din("cmask", [NSLOT, 32, 2, 512], BF16),
        padb=din("padb", [128, NSLOT * 4 * 5]), lmc=din("lmc", [2, 128], BF16),
    )
    O = dict(
        yp=dout("yp", [NSLOT * 512, 1024]), ys=dout("ys", [NSEQ, T_S, 1024]),
        pak=dout("pak", [2, 512, 512]), pav=dout("pav", [2, 512, 512]), pcv_=dout("pconv", [2, 2, 256]),
        pck=dout("pck", [2, S_P, 256]), pcv=dout("pcv", [2, S_P, 256]),
        sak=dout("sak", [2, NSEQ, T_S, 512]), sav=dout("sav", [2, NSEQ, T_S, 512]),
        scvo=dout("sconv", [2, NSEQ, 2, 256]),
        sck=dout("sck", [2, NSEQ, T_S, 256]), scv2=dout("scv2", [2, NSEQ, T_S, 256]),
    )

    class Seq:
        pass

    seqs = []
    for si in range(1 + NSEQ):
        s = Seq()
        s.name = "p" if si == 0 else "s%d" % (si - 1)
        s.prompt = si == 0
        s.T = S_P if s.prompt else T_S
        s.nt = 128 if s.prompt else T_S
        s.ntiles = NT_P if s.prompt else 1
        s.ca_tiles = 4
        s.cc_tiles = 0 if s.prompt else PAST // 128
        s.KA = s.ca_tiles * 128 + s.T
        s.KC = s.cc_tiles * 128 + s.T
        s.x_in = I["xp"] if s.prompt else I["xs"][si - 1]
        s.rope = I["ropep"] if s.prompt else I["ropes"]
        s.si = si - 1
        n = s.name
        s.X1 = dscr("X1" + n, [s.nt, s.ntiles * 1024], F32)
        s.U = dscr("U" + n, [s.T + 2, 256], F32)
        s.AQT = dscr("AQT" + n, [512, s.T], BF16)
        s.AKT = dscr("AKT" + n, [512, s.KA], BF16)
        s.AV1 = dscr("AV1" + n, [128, (s.ca_tiles + s.ntiles) * 520], BF16)
        s.CQT = dscr("CQT" + n, [256, s.T], BF16)
        s.CKT = dscr("CKT" + n, [256, s.KC], BF16)
        s.CV1 = dscr("CV1" + n, [s.KC, 260], BF16)
        s.GA = dscr("GA" + n, [s.nt, s.ntiles * 512], BF16)
        s.GC = dscr("GC" + n, [s.nt, s.ntiles * 256], BF16)
        s.OB = dscr("OB" + n, [s.nt, s.ntiles * 256], BF16)
        s.YC = dscr("YC" + n, [s.T, 256], BF16)
        s.YC2 = dscr("YC2" + n, [NSLOT * 512, 256], BF16)
        seqs.append(s)

    es = ExitStack()
    with es:
        def sb(name, shape, dt=F32):
            return es.enter_context(nc.sbuf_tensor("sb_" + name, list(shape), dt))

        def ps(name, shape, dt=F32):
            return es.enter_context(nc.psum_tensor("ps_" + name, list(shape), dt))

        BIG = sb("BIG", [128, 33152], BF16)
        zbuf = [sb("z0", [128, 4096]), sb("z1", [128, 4096])]
        z = zbuf[0]
        ztile = [0]
        xts = [sb("xt%d" % i, [128, 1024]) for i in range(2)]; xn = sb("xn", [128, 1024], BF16)
        hT = sb("hT", [128, 8, 128], BF16)
        st = sb("st", [128, 8]); rs = sb("rs", [128, 2]); gcol = sb("gcol", [128, 16])
        ropes = [sb("rope%d" % i, [128, 128]) for i in range(2)]; rt = sb("rt", [128, 4, 64])
        stg = sb("stg", [128, 1536], BF16); tT = sb("tT", [128, 8, 128], BF16)
        av1 = sb("av1", [128, 8, 65], BF16); cv1 = sb("cv1", [128, 4, 65], BF16)
        ga = sb("ga", [128, 512], BF16); gc = sb("gc", [128, 256], BF16)
        u = sb("u", [128, 256]); um1 = sb("um1", [128, 256]); um2 = sb("um2", [128, 256]); cvt = sb("cvt", [128, 256])
        sg = sb("sg", [128, 256]); sgt = sb("sgt", [128, 1024]); ob = sb("ob", [128, 256], BF16)
        cw = sb("cw", [128, 3, 256]); zero2 = sb("zero2", [2, 256])
        lamt = sb("lamt", [128, 4, 32]); lamj = sb("lamj", [128, 32]); lamv = sb("lamv", [128, 8])
        subg = sb("subg", [128, 64]); fing = sb("fing", [128, 1024])
        idb = sb("idb", [128, 128], BF16); idf = sb("idf", [128, 128])
        Qz = [[sb("Qz%d_%d" % (i, hh), [128, 2, 512], BF16) for hh in range(2)] for i in range(2)]
        Pm = [sb("Pm%d" % i, [128, 2, 512], BF16) for i in range(3)]
        oT = sb("oT", [128, 2, 512])
        rr = sb("rr", [128, 16]); o1 = sb("o1", [128, 64]); o2 = sb("o2", [128, 64]); ssq = sb("ssq", [128, 4])
        gct = sb("gct", [128, 256], BF16); yct = sb("yct", [128, 256], BF16)
        KTbs = [sb("KTb%d" % i, [128, 640], BF16) for i in range(2)]; QTas = [sb("QTa%d" % i, [128, 128], BF16) for i in range(2)]
        biasp = sb("biasp", [128, 8, 5, 128], BF16); biass = sb("biass", [128, 8, 5, T_S], BF16)
        Pas = [sb("Pa%d" % i, [128, 5, 128], BF16) for i in range(2)]; av1bs = [sb("av1b%d" % i, [128, 5, 520], BF16) for i in range(2)]
        ra = sb("ra", [128, 8]); gats = [sb("gat%d" % i, [128, 512], BF16) for i in range(2)]
        y = sb("y", [128, 1024], BF16); yT = sb("yT", [128, 8, 128], BF16)
        xr = sb("xr", [128, 1024]); sq = sb("sq", [128, 1024]); xo = xr
        FA = ps("FA", [128, 1024]); FB = ps("FB", [128, 1024]); FC = ps("FC", [128, 1024])
        T0 = ps("T0", [128, 8, 128], BF16); T1 = ps("T1", [128, 8, 128], BF16)

        zt = sb("zt", [128, 640], BF16); padb = sb("padb", [128, NSLOT * 20]); lmc = sb("lmc", [2, 128], BF16)
        rms = [sb("rm%d" % i, [2, 512], BF16) for i in range(4)]
        tr = Tracker(nc, es)
        blk_of = lambda c, j: 8 * j + (c if j % 2 == 0 else 7 - c)
        g_of = lambda c, lt: 4 * blk_of(c, lt // 4) + lt % 4
        op, dma = tr.op, tr.dma
        alt = [0]

        def evac(out, in_, reads, writes):
            alt[0] ^= 1
            if alt[0]:
                op("act", lambda e: e.activation(out=out, in_=in_, func=AF.Copy), reads, writes)
            else:
                op("dve", lambda e: e.tensor_copy(out=out, in_=in_), reads, writes)

        def c_epilogue(s, h, q0, nq, split=False, slot=0):
            evac(oT[0:65, :, :nq], FC[0:65, :].rearrange("p (m q) -> p m q", m=2)[:, :, :nq], ["FC"], ["oT"])
            for qt in range(max(1, nq // 128)):
                qn = min(128, nq)
                tok0 = q0 + qt * 128
                for m in range(2):
                    op("pe", lambda e, m=m: e.transpose(
                        FA[:qn, m * 128: m * 128 + 65], oT[0:65, m, qt * 128: qt * 128 + qn], idf[0:65, 0:65]),
                        ["oT", "idf"], ["FA0"])
                if split:
                    dma("sp", gct[:qn, :], lambda c, s=s, lt_=slot * 4 + qt: s.GC[:, g_of(c, lt_) * 256:(g_of(c, lt_) + 1) * 256],
                        reads=[("GC", s.name, i) for i in range(s.ntiles)], writes=["gct"], percore=True)
                else:
                    dma("sp", gct[:qn, :], s.GC[:qn, (tok0 // s.nt) * 256:(tok0 // s.nt + 1) * 256], reads=[("GC", s.name, tok0 // s.nt)], writes=["gct"])
                op("dve", lambda e: e.reciprocal(out=rr[:qn, 0:1], in_=FA[:qn, 64:65]), ["FA0"], ["rr"])
                op("dve", lambda e: e.reciprocal(out=rr[:qn, 1:2], in_=FA[:qn, 128 + 64:128 + 65]), ["FA0"], ["rr"])
                op("dve", lambda e: e.tensor_tensor(out=rr[:qn, 1:2], in0=rr[:qn, 1:2], in1=lamv[:qn, 2:3], op=ALU.mult),
                   ["rr", "lamv"], ["rr"])
                op("dve", lambda e: e.tensor_scalar(out=o1[:qn, :], in0=FA[:qn, 0:64], scalar1=rr[:qn, 0:1], scalar2=None,
                                                    op0=ALU.mult), ["FA0", "rr"], ["o1"])
                op("dve", lambda e: e.scalar_tensor_tensor(out=o1[:qn, :], in0=FA[:qn, 128:192], scalar=rr[:qn, 1:2],
                                                           in1=o1[:qn, :], op0=ALU.mult, op1=ALU.add),
                   ["FA0", "rr", "o1"], ["o1"])
                op("dve", lambda e: e.tensor_tensor(out=o2[:qn, :], in0=o1[:qn, :], in1=o1[:qn, :], op=ALU.mult), ["o1"], ["o2"])
                op("dve", lambda e: e.reduce_sum(out=ssq[:qn, 0:1], in_=o2[:qn, :], axis=AX.X), ["o2"], ["ssq"])
                op("dve", lambda e: e.tensor_scalar(out=ssq[:qn, 0:1], in0=ssq[:qn, 0:1], scalar1=1.0 / 64, scalar2=1e-5,
                                                    op0=ALU.mult, op1=ALU.add), ["ssq"], ["ssq"])
                op("act", lambda e: e.activation(out=ssq[:qn, 0:1], in_=ssq[:qn, 0:1], func=AF.Ln), ["ssq"], ["ssq"])
                op("act", lambda e: e.activation(out=ssq[:qn, 0:1], in_=ssq[:qn, 0:1], func=AF.Exp, scale=-0.5), ["ssq"], ["ssq"])
                op("dve", lambda e: e.scalar_tensor_tensor(out=o1[:qn, :], in0=o1[:qn, :], scalar=ssq[:qn, 0:1],
                                                           in1=subg[:qn, :], op0=ALU.mult, op1=ALU.mult),
                   ["o1", "ssq", "subg"], ["o1"])
                op("dve", lambda e: e.tensor_tensor(out=yct[:qn, 0:64], in0=o1[:qn, :], in1=gct[:qn, h * 64:(h + 1) * 64],
                                                    op=ALU.mult), ["o1", "gct"], ["yct"])
                dma("sp", (s.YC2 if split else s.YC)[tok0:tok0 + qn, h * 64:(h + 1) * 64], yct[:qn, 0:64], reads=["yct"],
                    writes=[("YC2" if split else "YC", s.name, tok0 // s.nt, h)])

        dma("sp", idb[:], I["idb"][:, :], writes=["idb"])
        dma("sp", idf[:], I["idf"][:, :], writes=["idf"])
        dma("sp", fing[:], I["fing"][:, :], writes=["fing"])
        op("pool", lambda e: e.memset(zero2[:], 0.0), writes=["zero2"])
        op("pool", lambda e: e.memset(zt[:], 0.0), writes=["zt"])
        dma("sp", padb[:], I["padb"][:, :], writes=["padb"])
        dma("sp", lmc[:], I["lmc"][:, :], writes=["lmc"])
        op("pool", lambda e: e.memset(av1[:], 1.0), writes=["av1"])
        op("pool", lambda e: e.memset(cv1[:], 1.0), writes=["cv1"])
        for i in range(2):
            for hh in range(2):
                op("pool", lambda e: e.memset(Qz[i][hh][:], 0.0), writes=["Qz%d" % i])
        op("pool", lambda e: e.memset(BIG[:, 33024:33152], 0.0), writes=["BIG"])

        for l in range(2):
            lam_init = 0.8 - 0.6 * math.exp(-0.3 * l)
            dma("sp", gcol[:, 0:8], I["ng"][l], writes=["gcol"])
            for k in range(8):
                for hf in range(2):
                    zk = "z%d" % hf
                    dma("sp", zbuf[0][:, hf * 2048:(hf + 1) * 2048], I["win"][l, k * 128:(k + 1) * 128, hf * 2048:(hf + 1) * 2048],
                        writes=["zb0c%d" % i for i in range(hf * 4, hf * 4 + 4)])
                    dst = BIG[:, k * 4096 + hf * 2048: k * 4096 + (hf + 1) * 2048]
                    if hf:
                        op("dve", lambda e, dst=dst, hf=hf, k=k: e.tensor_scalar(out=dst, in0=zbuf[0][:, hf * 2048:(hf + 1) * 2048],
                                                                                 scalar1=gcol[:, k:k + 1], scalar2=None, op0=ALU.mult),
                           reads=["zb0c%d" % i for i in range(hf * 4, hf * 4 + 4)] + ["gcol"], writes=["BIG"])
                    else:
                        op("act", lambda e, dst=dst, hf=hf, k=k: e.activation(out=dst, in_=zbuf[0][:, hf * 2048:(hf + 1) * 2048], func=AF.Copy,
                                                                              scale=gcol[:, k:k + 1]),
                           reads=["zb0c%d" % i for i in range(hf * 4, hf * 4 + 4)] + ["gcol"], writes=["BIG"])
            dma("pool", biasp[:].rearrange("p a b c -> p (a b c)"), I["biasp"][l].rearrange("p a b c -> p (a b c)"), writes=["biasp"])
            dma("pool", biass[:].rearrange("p a b c -> p (a b c)"), I["biass"][l].rearrange("p a b c -> p (a b c)"), writes=["biass"])
            dma("sp", cw[:], I["convw"][l].rearrange("j p c -> p j c"), writes=["cw"])
            dma("sp", lamt[:], I["lam"][l].rearrange("j p c -> p j c"), writes=["lamt"])
            dma("sp", subg[:], I["subg"][l], writes=["subg"])
            op("dve", lambda e: e.tensor_scalar(out=subg[:], in0=subg[:], scalar1=float(1.0 - lam_init), scalar2=None, op0=ALU.mult),
               reads=["subg"], writes=["subg"])
            for j in range(2):
                op("dve", lambda e, j=j: e.tensor_tensor(out=lamj[:], in0=lamt[:, 2 * j, :], in1=lamt[:, 2 * j + 1, :], op=ALU.mult),
                   reads=["lamt"], writes=["lamj"])
                op("dve", lambda e, j=j: e.reduce_sum(out=lamv[:, j:j + 1], in_=lamj[:], axis=AX.X), reads=["lamj"], writes=["lamv"])
            op("act", lambda e: e.activation(out=lamv[:, 0:2], in_=lamv[:, 0:2], func=AF.Exp), reads=["lamv"], writes=["lamv"])
            op("dve", lambda e: e.tensor_tensor(out=lamv[:, 2:3], in0=lamv[:, 1:2], in1=lamv[:, 0:1], op=ALU.subtract),
               reads=["lamv"], writes=["lamv"])
            op("dve", lambda e: e.tensor_scalar(out=lamv[:, 2:3], in0=lamv[:, 2:3], scalar1=float(-lam_init), scalar2=None, op0=ALU.add),
               reads=["lamv"], writes=["lamv"])

            for s in seqs:
                nt = s.nt
                if s.prompt:
                    dma("sp", s.U[0:2, :], zero2[:], reads=["zero2"], writes=[("U", s.name, -1)])
                    if l == 0:
                        for p4 in range(4):
                            dma("sp", s.AKT[p4 * 128:(p4 + 1) * 128, 0:512], zt[:, 0:512], reads=["zt"], writes=[("AKT", s.name, i) for i in range(4)])
                            dma("sp", s.AV1[:, p4 * 520:(p4 + 1) * 520], zt[:, 0:520], reads=["zt"], writes=[("AV1", s.name, p4)])
                else:
                    dma("sp", um1[0:2, :], I["scv"][l, s.si], writes=["um1"])
                    dma("sp", s.U[0:2, :], um1[0:2, :], reads=["um1"], writes=[("U", s.name, -1)])
                    for kt in range(s.cc_tiles):
                        dma("pool", stg[:, 0:256], I["cck"][l, s.si, kt * 128:(kt + 1) * 128, :], writes=["stg"])
                        for b in range(2):
                            op("pe", lambda e, b=b: e.transpose(T1[:, b, :], stg[:, b * 128:(b + 1) * 128], idb[:]),
                               reads=["stg", "idb"], writes=["T1"])
                        evac(tT[:, 0:2, :], T1[:, 0:2, :], ["T1"], ["tT"])
                        dma("sp", s.CKT.rearrange("(b p) t -> p b t", p=128)[:, :, kt * 128:(kt + 1) * 128], tT[:, 0:2, :],
                            reads=["tT"], writes=[("CKT", s.name, kt)])
                        dma("pool", cv1[:, :, 0:64], I["ccv"][l, s.si, kt * 128:(kt + 1) * 128, :].rearrange("t (h e) -> t h e", e=64),
                            writes=["cv1"])
                        dma("sp", s.CV1[kt * 128:(kt + 1) * 128, :], cv1[:].rearrange("p h e -> p (h e)"), reads=["cv1"],
                            writes=[("CV1", s.name, kt)])
                    for kt in range(0 if s.prompt else s.ca_tiles):
                        dma("pool", stg[:, 0:512], I["cak"][l, s.si, kt * 128:(kt + 1) * 128, :], writes=["stg"])
                        for b in range(4):
                            op("pe", lambda e, b=b: e.transpose(T1[:, b, :], stg[:, b * 128:(b + 1) * 128], idb[:]),
                               reads=["stg", "idb"], writes=["T1"])
                        evac(tT[:, 0:4, :], T1[:, 0:4, :], ["T1"], ["tT"])
                        dma("sp", s.AKT.rearrange("(b p) t -> p b t", p=128)[:, :, kt * 128:(kt + 1) * 128], tT[:, 0:4, :],
                            reads=["tT"], writes=[("AKT", s.name, kt)])
                        dma("pool", av1[:, :, 0:64], I["cav"][l, s.si, kt * 128:(kt + 1) * 128, :].rearrange("t (h e) -> t h e", e=64),
                            writes=["av1"])
                        dma("sp", s.AV1[:, kt * 520:(kt + 1) * 520], av1[:].rearrange("p h e -> p (h e)"), reads=["av1"],
                            writes=[("AV1", s.name, kt)])

                zbase = ztile[0]

                def front(t, mid=None):
                    r0 = t * nt
                    ka0 = s.ca_tiles * 128 + r0
                    kc0 = s.cc_tiles * 128 + r0
                    zi = (zbase + t) % 2
                    z = zbuf[zi]
                    zc0, zc1, zc2, zc3, zc4, zc5, zc6, zc7 = ["zb%dc%d" % (zi, i) for i in range(8)]
                    xt, rope = xts[zi], ropes[zi]
                    xtk, ropek = "xt%d" % zi, "rope%d" % zi
                    dma("pool", xt[:nt, :], s.x_in[r0:r0 + nt, :] if l == 0 else s.X1[:nt, t * 1024:(t + 1) * 1024],
                        reads=[("X1", s.name, t)] if l else [], writes=[xtk])
                    dma("pool", rope[:nt, :], s.rope[r0:r0 + nt, :], writes=[ropek])
                    op("act", lambda e: e.activation(out=xn[:nt, :], in_=xt[:nt, :], func=AF.Copy), [xtk], ["xn"])
                    for k in range(8):
                        op("pe", lambda e, k=k: e.transpose(T0[:, k, :nt], xn[:nt, k * 128:(k + 1) * 128], idb[:nt, :nt]),
                           ["xn", "idb"], ["T0"])
                    op("act", lambda e: e.activation(out=hT[:, :, :nt], in_=T0[:, :, :nt], func=AF.Copy), ["T0"], ["hT"])
                    for cb in range(8):
                        pk = "FA%d" % (cb % 2)
                        pt = FA[:nt, (cb % 2) * 512:(cb % 2 + 1) * 512]
                        for k in range(8):
                            op("pe", lambda e, k=k, pt=pt, cb=cb: e.matmul(pt, lhsT=hT[:, k, :nt],
                                                                           rhs=BIG[:, k * 4096 + cb * 512: k * 4096 + (cb + 1) * 512],
                                                                           start=(k == 0), stop=(k == 7)),
                               ["hT", "BIG"], [pk])
                        op("act", lambda e: e.activation(out=z[:nt, cb * 512:(cb + 1) * 512], in_=pt, func=AF.Copy),
                           [pk], ["zb%dc%d" % (zi, cb)])
                        if cb == 3 and mid is not None:
                            mid()

                def epiA(t):
                    r0 = t * nt
                    ka0 = s.ca_tiles * 128 + r0
                    kc0 = s.cc_tiles * 128 + r0
                    zi = (zbase + t) % 2
                    z = zbuf[zi]
                    zc0, zc1, zc2, zc3, zc4, zc5, zc6, zc7 = ["zb%dc%d" % (zi, i) for i in range(8)]
                    xt, rope = xts[zi], ropes[zi]
                    xtk, ropek = "xt%d" % zi, "rope%d" % zi
                    rsk = "rs%d" % zi
                    op("dve", lambda e: e.tensor_tensor(out=sq[:nt, :], in0=xt[:nt, :], in1=xt[:nt, :], op=ALU.mult), [xtk], ["sq"])
                    op("dve", lambda e: e.reduce_sum(out=rs[:nt, zi:zi + 1], in_=sq[:nt, :], axis=AX.X), ["sq"], [rsk])
                    op("dve", lambda e: e.tensor_scalar(out=rs[:nt, zi:zi + 1], in0=rs[:nt, zi:zi + 1], scalar1=1.0 / 1024, scalar2=1e-6,
                                                         op0=ALU.mult, op1=ALU.add), [rsk], [rsk])
                    op("act", lambda e: e.activation(out=rs[:nt, zi:zi + 1], in_=rs[:nt, zi:zi + 1], func=AF.Ln), [rsk], [rsk])
                    op("act", lambda e: e.activation(out=rs[:nt, zi:zi + 1], in_=rs[:nt, zi:zi + 1], func=AF.Exp, scale=-0.5), [rsk], [rsk])
                    for cb in range(8):
                        op("dve", lambda e, cb=cb: e.tensor_scalar(out=z[:nt, cb * 512:(cb + 1) * 512], in0=z[:nt, cb * 512:(cb + 1) * 512],
                                                                   scalar1=rs[:nt, zi:zi + 1], scalar2=None, op0=ALU.mult),
                           ["zb%dc%d" % (zi, cb), rsk], ["zb%dc%d" % (zi, cb)])
                    zz = z[:nt, 3072:3584].rearrange("p (g d) -> p g d", d=32)
                    x1, x2 = zz[:, :, 0:4], zz[:, :, 4:8]
                    cs = rope[:nt, 0:64].rearrange("p (g d) -> p g d", d=4)
                    sn = rope[:nt, 64:128].rearrange("p (g d) -> p g d", d=4)
                    rv = [rt[:nt, i, :].rearrange("p (g d) -> p g d", d=4) for i in range(4)]
                    op("dve", lambda e: e.tensor_tensor(out=rv[0], in0=x1, in1=cs, op=ALU.mult), [zc6, ropek], ["rt0"])
                    op("dve", lambda e: e.tensor_tensor(out=rv[1], in0=x2, in1=sn, op=ALU.mult), [zc6, ropek], ["rt1"])
                    op("dve", lambda e: e.tensor_tensor(out=rv[2], in0=x2, in1=cs, op=ALU.mult), [zc6, ropek], ["rt2"])
                    op("dve", lambda e: e.tensor_tensor(out=rv[3], in0=x1, in1=sn, op=ALU.mult), [zc6, ropek], ["rt3"])
                    op("dve", lambda e: e.tensor_tensor(out=x1, in0=rv[0], in1=rv[1], op=ALU.subtract), ["rt0", "rt1", "rt3"], [zc6])
                    op("dve", lambda e: e.tensor_tensor(out=x2, in0=rv[2], in1=rv[3], op=ALU.add), ["rt2", "rt3"], [zc6])
                    if s.prompt:
                        dma("sp", O["pck"][l, r0:r0 + nt, :], z[:nt, 3328:3584], reads=[zc6], is_output=True)
                        dma("sp", O["pcv"][l, r0:r0 + nt, :], z[:nt, 3584:3840], reads=[zc7], is_output=True)
                        if t >= NT_P - 4:
                            rr0 = (t - (NT_P - 4)) * 128
                            dma("sp", O["pak"][l, rr0:rr0 + 128, :], z[:nt, 512:1024], reads=[zc1], is_output=True)
                            dma("sp", O["pav"][l, rr0:rr0 + 128, :], z[:nt, 1024:1536], reads=[zc2], is_output=True)
                    else:
                        dma("sp", O["sck"][l, s.si], z[:nt, 3328:3584], reads=[zc6], is_output=True)
                        dma("sp", O["scv2"][l, s.si], z[:nt, 3584:3840], reads=[zc7], is_output=True)
                        dma("sp", O["sak"][l, s.si], z[:nt, 512:1024], reads=[zc1], is_output=True)
                        dma("sp", O["sav"][l, s.si], z[:nt, 1024:1536], reads=[zc2], is_output=True)
                    op("act", lambda e: e.activation(out=stg[:nt, 0:512], in_=z[:nt, 0:512], func=AF.Copy, scale=0.125), [zc0], ["stg"])
                    op("dve", lambda e: e.tensor_copy(out=stg[:nt, 512:1024], in_=z[:nt, 512:1024]), [zc1], ["stg"])
                    op("dve", lambda e: e.tensor_copy(out=stg[:nt, 1024:1536], in_=z[:nt, 3072:3584]), [zc6], ["stg"])
                    op("act", lambda e: e.activation(out=sgt[:nt, 0:512], in_=z[:nt, 1536:2048], func=AF.Exp, scale=-1.0), [zc3], ["sgtA"])
                    op("act", lambda e: e.activation(out=sgt[:nt, 512:768], in_=z[:nt, 3840:4096], func=AF.Exp, scale=-1.0), [zc7], ["sgtC"])
                    op("act", lambda e: e.activation(out=sgt[:nt, 768:1024], in_=z[:nt, 2816:3072], func=AF.Exp, scale=-1.0), [zc5], ["sgtB"])
                    op("dve", lambda e: e.tensor_tensor(out=u[:nt, :], in0=z[:nt, 2304:2560], in1=z[:nt, 2560:2816], op=ALU.mult),
                       [zc4, zc5], ["u"])
                    dma("sp", s.U[2 + r0:2 + r0 + nt, :], u[:nt, :], reads=["u"], writes=[("U", s.name, t)])
                    dma("sp", um1[:nt, :], s.U[1 + r0:1 + r0 + nt, :], reads=[("U", s.name, t), ("U", s.name, t - 1)], writes=["um1"])
                    dma("sp", um2[:nt, :], s.U[r0:r0 + nt, :], reads=[("U", s.name, t), ("U", s.name, t - 1)], writes=["um2"])
                    if t == s.ntiles - 1:
                        dst = O["pcv_"][l] if s.prompt else O["scvo"][l, s.si]
                        dma("sp", dst, s.U[s.T:s.T + 2, :], reads=[("U", s.name, t)], is_output=True)

                def epiT(t):
                    r0 = t * nt
                    ka0 = s.ca_tiles * 128 + r0
                    kc0 = s.cc_tiles * 128 + r0
                    zi = (zbase + t) % 2
                    z = zbuf[zi]
                    zc0, zc1, zc2, zc3, zc4, zc5, zc6, zc7 = ["zb%dc%d" % (zi, i) for i in range(8)]
                    xt, rope = xts[zi], ropes[zi]
                    xtk, ropek = "xt%d" % zi, "rope%d" % zi
                    for b in range(8):
                        op("pe", lambda e, b=b: e.transpose(T1[:, b, :nt], stg[:nt, b * 128:(b + 1) * 128], idb[:nt, :nt]),
                           ["stg", "idb"], ["T1"])
                    op("dve", lambda e: e.tensor_copy(out=tT[:, :, :nt], in_=T1[:, :, :nt]), ["T1"], ["tT"])
                    dma("sp", s.AQT.rearrange("(b p) t -> p b t", p=128)[:, :, r0:r0 + nt], tT[:, 0:4, :nt], reads=["tT"],
                        writes=[("AQT", s.name, t)])
                    dma("sp", s.AKT.rearrange("(b p) t -> p b t", p=128)[:, :, ka0:ka0 + nt], tT[:, 4:8, :nt], reads=["tT"],
                        writes=[("AKT", s.name, s.ca_tiles + t)])
                    for b in range(4):
                        op("pe", lambda e, b=b: e.transpose(T1[:, b, :nt], stg[:nt, 1024 + b * 128:1024 + (b + 1) * 128], idb[:nt, :nt]),
                           ["stg", "idb"], ["T1"])
                    op("dve", lambda e: e.tensor_copy(out=tT[:, 0:4, :nt], in_=T1[:, 0:4, :nt]), ["T1"], ["tT"])
                    dma("sp", s.CQT.rearrange("(b p) t -> p b t", p=128)[:, :, r0:r0 + nt], tT[:, 0:2, :nt], reads=["tT"],
                        writes=[("CQT", s.name, t)])
                    dma("sp", s.CKT.rearrange("(b p) t -> p b t", p=128)[:, :, kc0:kc0 + nt], tT[:, 2:4, :nt], reads=["tT"],
                        writes=[("CKT", s.name, s.cc_tiles + t)])

                def epiB(t):
                    r0 = t * nt
                    ka0 = s.ca_tiles * 128 + r0
                    kc0 = s.cc_tiles * 128 + r0
                    zi = (zbase + t) % 2
                    z = zbuf[zi]
                    zc0, zc1, zc2, zc3, zc4, zc5, zc6, zc7 = ["zb%dc%d" % (zi, i) for i in range(8)]
                    xt, rope = xts[zi], ropes[zi]
                    xtk, ropek = "xt%d" % zi, "rope%d" % zi
                    op("dve", lambda e: e.tensor_copy(out=av1[:nt, :, 0:64], in_=z[:nt, 1024:1536].rearrange("p (h e) -> p h e", e=64)),
                       [zc2], ["av1"])
                    dma("sp", s.AV1[:nt, (s.ca_tiles + t) * 520:(s.ca_tiles + t + 1) * 520], av1[:nt].rearrange("p h e -> p (h e)"), reads=["av1"],
                        writes=[("AV1", s.name, s.ca_tiles + t)])
                    op("dve", lambda e: e.tensor_copy(out=cv1[:nt, :, 0:64], in_=z[:nt, 3584:3840].rearrange("p (h e) -> p h e", e=64)),
                       [zc7], ["cv1"])
                    dma("sp", s.CV1[kc0:kc0 + nt, :], cv1[:nt].rearrange("p h e -> p (h e)"), reads=["cv1"],
                        writes=[("CV1", s.name, s.cc_tiles + t)])
                    op("dve", lambda e: e.tensor_scalar(out=sgt[:nt, 0:512], in0=sgt[:nt, 0:512], scalar1=1.0, scalar2=None, op0=ALU.add), ["sgtA"], ["sgtA"])
                    op("dve", lambda e: e.reciprocal(out=sgt[:nt, 0:512], in_=sgt[:nt, 0:512]), ["sgtA"], ["sgtA"])
                    op("dve", lambda e: e.tensor_tensor(out=ga[:nt, :], in0=z[:nt, 1536:2048], in1=sgt[:nt, 0:512], op=ALU.mult), [zc3, "sgtA"], ["ga"])
                    dma("sp", s.GA[:nt, t * 512:(t + 1) * 512], ga[:nt, :], reads=["ga"], writes=[("GA", s.name, t)])
                    op("dve", lambda e: e.tensor_scalar(out=sgt[:nt, 512:768], in0=sgt[:nt, 512:768], scalar1=1.0, scalar2=None, op0=ALU.add), ["sgtC"], ["sgtC"])
                    op("dve", lambda e: e.reciprocal(out=sgt[:nt, 512:768], in_=sgt[:nt, 512:768]), ["sgtC"], ["sgtC"])
                    op("dve", lambda e: e.tensor_tensor(out=gc[:nt, :], in0=z[:nt, 3840:4096], in1=sgt[:nt, 512:768], op=ALU.mult), [zc7, "sgtC"], ["gc"])
                    dma("sp", s.GC[:nt, t * 256:(t + 1) * 256], gc[:nt, :], reads=["gc"], writes=[("GC", s.name, t)])
                    op("dve", lambda e: e.tensor_tensor(out=cvt[:nt, :], in0=u[:nt, :], in1=cw[:nt, 2, :], op=ALU.mult), ["u", "cw"], ["cvt"])
                    op("dve", lambda e: e.tensor_tensor(out=um1[:nt, :], in0=um1[:nt, :], in1=cw[:nt, 1, :], op=ALU.mult), ["um1", "cw"], ["um1"])
                    op("dve", lambda e: e.tensor_tensor(out=um2[:nt, :], in0=um2[:nt, :], in1=cw[:nt, 0, :], op=ALU.mult), ["um2", "cw"], ["um2"])
                    op("dve", lambda e: e.tensor_tensor(out=cvt[:nt, :], in0=cvt[:nt, :], in1=um1[:nt, :], op=ALU.add), ["cvt", "um1"], ["cvt"])
                    op("dve", lambda e: e.tensor_tensor(out=cvt[:nt, :], in0=cvt[:nt, :], in1=um2[:nt, :], op=ALU.add), ["cvt", "um2"], ["cvt"])
                    op("dve", lambda e: e.tensor_scalar(out=sgt[:nt, 768:1024], in0=sgt[:nt, 768:1024], scalar1=1.0, scalar2=None, op0=ALU.add), ["sgtB"], ["sgtB"])
                    op("dve", lambda e: e.reciprocal(out=sgt[:nt, 768:1024], in_=sgt[:nt, 768:1024]), ["sgtB"], ["sgtB"])
                    op("dve", lambda e: e.tensor_tensor(out=sg[:nt, :], in0=z[:nt, 2816:3072], in1=sgt[:nt, 768:1024], op=ALU.mult), [zc5, "sgtB"], ["sg"])
                    op("dve", lambda e: e.tensor_tensor(out=cvt[:nt, :], in0=cvt[:nt, :], in1=z[:nt, 2048:2304], op=ALU.mult), ["cvt", zc4], ["cvt"])
                    op("dve", lambda e: e.tensor_tensor(out=ob[:nt, :], in0=cvt[:nt, :], in1=sg[:nt, :], op=ALU.mult), ["cvt", "sg"], ["ob"])
                    dma("sp", s.OB[:nt, t * 256:(t + 1) * 256], ob[:nt, :], reads=["ob"], writes=[("OB", s.name, t)])

                front(0)
                for t in range(s.ntiles):
                    epiA(t)
                    if t + 1 < s.ntiles:
                        front(t + 1, mid=lambda t=t: epiT(t))
                    else:
                        epiT(t)
                    epiB(t)
                ztile[0] += s.ntiles

            KT = BIG[:, 0:16384]
            V1 = BIG[:, 16384:16384 + 128 * 130].rearrange("p (t h e) -> p t h e", h=2, e=65)
            for s in seqs:
                nkt_all = (s.KC + 127) // 128
                nq = 512 if s.prompt else T_S
                split = s.prompt and l == 1
                nqb = NSLOT if split else s.T // nq
                allq = [("CQT", s.name, i) for i in range(s.ntiles)]
                allg = [("GC", s.name, i) for i in range(s.ntiles)]
                for hp in range(2):
                    dma("sp", KT[:, 0:s.KC], s.CKT[hp * 128:(hp + 1) * 128, :],
                        reads=[("CKT", s.name, i) for i in range(nkt_all)], writes=["BIG"])
                    for kt in range(nkt_all):
                        kn = min(128, s.KC - kt * 128)
                        dma("sp", V1[:kn, kt, :, :],
                            s.CV1[kt * 128:kt * 128 + kn, hp * 130:(hp + 1) * 130].rearrange("t (h e) -> t h e", e=65),
                            reads=[("CV1", s.name, kt)], writes=["BIG"])
                    pend = []

                    def flush_pv(keep=1):
                        while len(pend) > keep:
                            pend.pop(0)()
                    ucount = 0
                    for qb in range(nqb):
                        q0 = qb * nq
                        qz = Qz[qb % 2]
                        qk = "Qz%d" % (qb % 2)
                        for hh in range(2):
                            for m in range(2):
                                rws = slice(64 * hh + 32 * m, 64 * hh + 32 * m + 32)
                                if split:
                                    dma("sp", qz[hh][rws, m, :nq],
                                        lambda c, s=s, r_=hp * 128 + rws.start, qb=qb: s.CQT[r_: r_ + 32, blk_of(c, qb) * 512:(blk_of(c, qb) + 1) * 512],
                                        reads=allq, writes=[qk], percore=True)
                                else:
                                    dma("sp", qz[hh][rws, m, :nq], s.CQT[hp * 128 + rws.start: hp * 128 + rws.stop, q0:q0 + nq],
                                        reads=[("CQT", s.name, i) for i in range(q0 // s.nt, (q0 + nq) // s.nt)], writes=[qk])
                        for hh in range(2):
                            h = 2 * hp + hh
                            pb = slice(64 * hh, 64 * hh + 64)
                            nkt = (4 * qb + 4) if s.prompt else nkt_all
                            if split:
                                nkt = min(32 * (qb + 1), nkt_all)
                            for kt in range(nkt):
                                kn = min(128, s.KC - kt * 128)
                                sI = kt - 4 * qb if (s.prompt and not split) else -1
                                um = (kt - 32 * qb) if split else -1
                                if um >= 0:
                                    rm = rms[ucount % 4]
                                    rmk = "rm%d" % (ucount % 4)
                                    dma("sp", rm[:], I["cmask"][qb, um], writes=[rmk])
                                c0 = 128 * sI if sI > 0 else 0
                                par = ucount % 2
                                ucount += 1
                                sc = FA if par == 0 else FB
                                sck = ["FA0", "FA1"] if par == 0 else ["FB"]
                                pm = Pm[(ucount - 1) % 3]
                                pmk = "Pm%d" % ((ucount - 1) % 3)
                                for m in range(2):
                                    op("pe", lambda e, m=m: e.matmul(
                                        sc[:kn, m * 512 + c0: m * 512 + nq], lhsT=KT[:, kt * 128: kt * 128 + kn],
                                        rhs=qz[hh][:, m, c0:nq], start=True, stop=(um < 0)), ["BIG", qk], sck)
                                    if um >= 0:
                                        op("pe", lambda e, m=m: e.matmul(sc[:kn, m * 512: m * 512 + nq], lhsT=lmc[0:2, :kn], rhs=rm[0:2, :nq],
                                                                         start=False, stop=True), ["lmc", rmk], sck)
                                op("act", lambda e: e.activation(
                                    out=pm[:kn, :, c0:nq], in_=sc[:kn, :].rearrange("p (m q) -> p m q", m=2)[:, :, c0:nq],
                                    func=AF.Exp, scale=float(C_SCALE)), sck, [pmk])
                                if sI >= 0:
                                    op("pool", lambda e: e.memset(pm[64:128, :, c0:c0 + 64], 0.0), [pmk], [pmk])
                                flush_pv()

                                def pv(kt=kt, kn=kn, c0=c0, pm=pm, pmk=pmk, hh=hh, nkt=nkt, h=h, q0=q0, nq=nq, split=split):
                                    for m in range(2):
                                        op("pe", lambda e, m=m: e.matmul(
                                            FC[:, m * 512 + c0: m * 512 + nq],
                                            lhsT=BIG[:kn, 16384 + (kt * 2 + hh) * 65: 16384 + (kt * 2 + hh) * 65 + 128], rhs=pm[:kn, m, c0:nq],
                                            start=(kt == 0), stop=(kt == nkt - 1)), ["BIG", pmk], ["FC"])
                                    if kt == nkt - 1:
                                        c_epilogue(s, h, q0, nq, split, q0 // 512)
                                pend.append(pv)
                    flush_pv(0)

            WO = BIG[:, 0:8192].rearrange("p (k c) -> p k c", c=1024)
            for k in range(8):
                dma("sp", zbuf[0][:, 0:1024], I["wout"][l, k * 128:(k + 1) * 128, :], writes=["zb0c0", "zb0c1"])
                op("dve", lambda e, k=k: e.tensor_copy(out=WO[:, k, :], in_=zbuf[0][:, 0:1024]), reads=["zb0c0", "zb0c1"], writes=["BIG"])
            for s in seqs:
                nt = s.nt
                bias = biasp if s.prompt else biass
                split = s.prompt and l == 1
                tiles = [(sl, qt) for sl in range(NSLOT) for qt in range(4)] if split else [(None, t) for t in range(s.ntiles)]
                allt = lambda nm, n_=None: [(nm, s.name, i) for i in range(n_ if n_ is not None else s.ntiles)]
                for ti, (sl, t) in enumerate(tiles):
                    r0 = t * nt
                    lt = ti
                    pad = 4 if s.prompt else 0
                    gk = s.ca_tiles + t
                    k_lo = max(pad, gk - 4)
                    nk = 5 if split else gk - k_lo + 1
                    j0 = 0 if split else 4 - (gk - k_lo)
                    kcols = (nk - 1) * 128 + nt
                    av1b = av1bs[ti % 2]
                    avk = "av1b%d" % (ti % 2)
                    gat = gats[ti % 2]
                    gak = "gat%d" % (ti % 2)
                    if split:
                        dma("sp", av1b[:, 0:5, :].rearrange("p t c -> p (t c)"), lambda c, s=s, lt=lt: s.AV1[:, g_of(c, lt) * 520:(g_of(c, lt) + 5) * 520],
                            reads=allt("AV1", s.ntiles + 4), writes=[avk], percore=True)
                        dma("sp", gat[:nt, :], lambda c, s=s, lt=lt: s.GA[:, g_of(c, lt) * 512:(g_of(c, lt) + 1) * 512], reads=allt("GA"), writes=[gak], percore=True)
                    else:
                        if nk > 1:
                            dma("sp", av1b[:, 0:nk - 1, :].rearrange("p t c -> p (t c)"), s.AV1[:, k_lo * 520:(k_lo + nk - 1) * 520],
                                reads=[("AV1", s.name, i) for i in range(k_lo, gk)], writes=[avk])
                        dma("sp", av1b[:nt, nk - 1, :], s.AV1[:nt, gk * 520:(gk + 1) * 520], reads=[("AV1", s.name, gk)], writes=[avk])
                        dma("sp", gat[:nt, :], s.GA[:nt, t * 512:(t + 1) * 512], reads=[("GA", s.name, t)], writes=[gak])
                    pend = [None]
                    for p in range(4):
                        KTb, QTa = KTbs[p % 2], QTas[p % 2]
                        kq = "KQ%d" % (p % 2)
                        if split:
                            dma("sp", KTb[:, 0:640], lambda c, s=s, p=p, lt=lt: s.AKT[p * 128:(p + 1) * 128, g_of(c, lt) * 128:(g_of(c, lt) + 5) * 128],
                                reads=allt("AKT", s.ntiles + 4), writes=[kq], percore=True)
                            dma("sp", QTa[:, :nt], lambda c, s=s, p=p, lt=lt: s.AQT[p * 128:(p + 1) * 128, g_of(c, lt) * 128:(g_of(c, lt) + 1) * 128],
                                reads=allt("AQT"), writes=[kq], percore=True)
                        else:
                            dma("sp", KTb[:, 0:kcols], s.AKT[p * 128:(p + 1) * 128, k_lo * 128:k_lo * 128 + kcols],
                                reads=[("AKT", s.name, i) for i in range(k_lo, gk + 1)], writes=[kq])
                            dma("sp", QTa[:, :nt], s.AQT[p * 128:(p + 1) * 128, r0:r0 + nt], reads=[("AQT", s.name, t)], writes=[kq])
                        for hh in range(2):
                            h = 2 * p + hh
                            pb = slice(64 * hh, 64 * hh + 64)
                            sc = FA if h % 2 == 0 else FB
                            sck = ["FA0", "FA1"] if h % 2 == 0 else ["FB"]
                            Pa = Pas[h % 2]
                            pak = "Pa%d" % (h % 2)
                            for j in range(nk):
                                kn = 128 if j < nk - 1 else nt
                                op("pe", lambda e: e.matmul(sc[:kn, j * 128: j * 128 + nt], lhsT=KTb[pb, j * 128: j * 128 + kn],
                                                            rhs=QTa[pb, :nt], start=True, stop=False), [kq], sck)
                                op("pe", lambda e: e.matmul(sc[:kn, j * 128: j * 128 + nt], lhsT=idb[:kn, :kn],
                                                            rhs=bias[:kn, h, j0 + j, :nt], start=False, stop=True),
                                   ["idb", "biasp", "biass"], sck)
                            if split:
                                for j in range(nk):
                                    op("act", lambda e: e.activation(out=Pa[:, j, :], in_=sc[:, j * 128:(j + 1) * 128], func=AF.Exp,
                                                                     bias=padb[:, lt * 5 + j: lt * 5 + j + 1]), sck + ["padb"], [pak])
                            elif nt == 128:
                                op("act", lambda e: e.activation(out=Pa[:, 0:nk, :].rearrange("p j q -> p (j q)"), in_=sc[:, 0:nk * 128], func=AF.Exp),
                                   sck, [pak])
                            else:
                                if nk > 1:
                                    op("act", lambda e: e.activation(out=Pa[:, 0:nk - 1, :nt],
                                                                     in_=sc[:, 0:(nk - 1) * 128].rearrange("p (j q) -> p j q", q=128)[:, :, :nt],
                                                                     func=AF.Exp), sck, [pak])
                                op("act", lambda e: e.activation(out=Pa[:nt, nk - 1, :nt], in_=sc[:nt, (nk - 1) * 128:(nk - 1) * 128 + nt], func=AF.Exp),
                                   sck, [pak])
                            if pend[0] is not None:
                                pend[0]()

                            def pv(h=h, Pa=Pa, pak=pak, nk=nk, nt=nt, av1b=av1b, avk=avk):
                                for j in range(nk):
                                    kn = 128 if j < nk - 1 else nt
                                    op("pe", lambda e: e.matmul(FC[:nt, h * 128: h * 128 + 65], lhsT=Pa[:kn, j, :nt],
                                                                rhs=av1b[:kn, j, h * 65:(h + 1) * 65], start=(j == 0), stop=(j == nk - 1)),
                                       [pak, avk], ["FC"])
                            pend[0] = pv
                    pend[0]()
                    op("dve", lambda e: e.reciprocal(out=ra[:nt, :], in_=FC[:nt, :].rearrange("p (h c) -> p h c", c=128)[:, :, 64]), ["FC"], ["ra"])
                    for h in range(8):
                        op("dve", lambda e, h=h: e.scalar_tensor_tensor(out=y[:nt, h * 64:(h + 1) * 64], in0=FC[:nt, h * 128: h * 128 + 64],
                                                                       scalar=ra[:nt, h:h + 1], in1=gat[:nt, h * 64:(h + 1) * 64],
                                                                       op0=ALU.mult, op1=ALU.mult), ["FC", "ra", gak], ["y"])
                    xsrc = s.x_in if l == 0 else s.X1
                    if split:
                        dma("sp", y[:nt, 512:768], lambda c, s=s, lt=lt: s.OB[:, g_of(c, lt) * 256:(g_of(c, lt) + 1) * 256], reads=allt("OB"), writes=["y"], percore=True)
                        dma("sp", y[:nt, 768:1024], s.YC2[lt * 128:(lt + 1) * 128, :], reads=[("YC2", s.name, lt, h) for h in range(4)], writes=["y"])
                        dma("sp", xr[:nt, :], lambda c, s=s, lt=lt: s.X1[:, g_of(c, lt) * 1024:(g_of(c, lt) + 1) * 1024], reads=allt("X1"), writes=["xr"], percore=True)
                    else:
                        dma("sp", y[:nt, 512:768], s.OB[:nt, t * 256:(t + 1) * 256], reads=[("OB", s.name, t)], writes=["y"])
                        dma("sp", y[:nt, 768:1024], s.YC[r0:r0 + nt, :], reads=[("YC", s.name, t, h) for h in range(4)], writes=["y"])
                        dma("sp", xr[:nt, :], s.x_in[r0:r0 + nt, :] if l == 0 else s.X1[:nt, t * 1024:(t + 1) * 1024],
                            reads=[("X1", s.name, t)] if l else [], writes=["xr"])
                    for k in range(8):
                        op("pe", lambda e, k=k: e.transpose(T0[:, k, :nt], y[:nt, k * 128:(k + 1) * 128], idb[:nt, :nt]), ["y", "idb"], ["T0"])
                    evac(yT[:, :, :nt], T0[:, :, :nt], ["T0"], ["yT"])
                    for cb in range(2):
                        for k in range(8):
                            op("pe", lambda e, k=k, cb=cb: e.matmul(FA[:nt, cb * 512:(cb + 1) * 512], lhsT=yT[:, k, :nt],
                                                                    rhs=WO[:, k, cb * 512:(cb + 1) * 512], start=(k == 0), stop=(k == 7)),
                               ["yT", "BIG"], ["FA%d" % cb])
                        op("dve", lambda e, cb=cb: e.tensor_tensor(out=xo[:nt, cb * 512:(cb + 1) * 512], in0=FA[:nt, cb * 512:(cb + 1) * 512],
                                                                   in1=xr[:nt, cb * 512:(cb + 1) * 512], op=ALU.add), ["FA%d" % cb, "xr"], ["xr"])
                    if l == 0:
                        dma("sp", s.X1[:nt, t * 1024:(t + 1) * 1024], xo[:nt, :], reads=["xr"], writes=[("X1", s.name, t)])
                    else:
                        op("pool", lambda e: e.tensor_tensor(out=sq[:nt, :], in0=xo[:nt, :], in1=xo[:nt, :], op=ALU.mult), ["xr"], ["sq"])
                        op("dve", lambda e: e.reduce_sum(out=st[:nt, 1:2], in_=sq[:nt, :], axis=AX.X), ["sq"], ["st1"])
                        op("dve", lambda e: e.tensor_scalar(out=st[:nt, 1:2], in0=st[:nt, 1:2], scalar1=1.0 / 1024, scalar2=1e-6,
                                                            op0=ALU.mult, op1=ALU.add), ["st1"], ["st1"])
                        op("act", lambda e: e.activation(out=st[:nt, 1:2], in_=st[:nt, 1:2], func=AF.Ln), ["st1"], ["st1"])
                        op("act", lambda e: e.activation(out=st[:nt, 1:2], in_=st[:nt, 1:2], func=AF.Exp, scale=-0.5), ["st1"], ["st1"])
                        op("dve", lambda e: e.scalar_tensor_tensor(out=xo[:nt, :], in0=xo[:nt, :], scalar=st[:nt, 1:2], in1=fing[:nt, :],
                                                                   op0=ALU.mult, op1=ALU.mult), ["xr", "st1", "fing"], ["xr"])
                        dst = O["yp"][lt * 128:(lt + 1) * 128, :] if s.prompt else O["ys"][s.si]
                        dma("sp", dst, xo[:nt, :], reads=["xr"], is_output=True)

        tr.finish()
        with nc.Block() as block:
            @block.tensor
            def _(e):
                for f in tr.ops["pe"]:
                    f(e)

            @block.scalar
            def _(e):
                for f in tr.ops["act"]:
                    f(e)

            @block.vector
            def _(e):
                for f in tr.ops["dve"]:
                    f(e)

            @block.gpsimd
            def _(e):
                for f in tr.ops["pool"]:
                    f(e)

            @block.sync
            def _(e):
                ops_sp = tr.ops["sp"]
                i_ = 0
                while i_ < len(ops_sp):
                    if getattr(ops_sp[i_], "percore", False):
                        j_ = i_
                        while j_ < len(ops_sp) and getattr(ops_sp[j_], "percore", False):
                            j_ += 1
                        for arm in e.switch_core_id(n=128):
                            for f in ops_sp[i_:j_]:
                                f(e, arm.logical % 8)
                        i_ = j_
                    else:
                        ops_sp[i_](e)
                        i_ += 1
    return nc


def _host_consts(rel_bias):
    import ml_dtypes
    kk = np.arange(128)[:, None]
    biasp = np.zeros((2, 128, 8, 5, 128), np.float32)
    for j in range(5):
        toff = j - 4
        q = np.arange(128)[None, :]
        dist = q - kk - 128 * toff
        idx = np.clip(dist, -128, 128) + 128
        vals = rel_bias[:, :, idx]
        qc, kc = (q // 64), ((kk + 128 * toff) // 64)
        ok = (kc <= qc) & (kc >= qc - 8)
        vals = np.where(ok[None, None], vals, np.float32(NEG))
        biasp[:, :, :, j, :] = vals.transpose(0, 2, 1, 3)
    biass = np.zeros((2, 128, 8, 5, T_S), np.float32)
    for j in range(5):
        r = j * 128 + kk
        q = np.arange(T_S)[None, :]
        dist = q + AWIN - r
        idx = np.clip(dist, -128, 128) + 128
        vals = rel_bias[:, :, idx]
        biass[:, :, :, j, :] = vals.transpose(0, 2, 1, 3)
    half = 4
    inv = (500000.0 ** (-np.arange(half, dtype=np.float32) * np.float32(2.0 / 8))).astype(np.float32)

    def rope_tab(pos):
        ang = pos.astype(np.float32)[:, None] * inv[None, :]
        c, s_ = np.cos(ang).astype(np.float32), np.sin(ang).astype(np.float32)
        return np.concatenate([np.tile(c, (1, 16)), np.tile(s_, (1, 16))], axis=1).astype(np.float32)
    lmc = np.zeros((2, 128), np.float32); lmc[0, :] = 1.0; lmc[1, 64:] = 1.0
    return dict(lmc=lmc.astype(ml_dtypes.bfloat16), biasp=biasp, biass=biass, ropep=rope_tab(np.arange(S_P)), ropes=rope_tab(PAST + np.arange(T_S)),
                idb=np.eye(128, dtype=np.float32).astype(ml_dtypes.bfloat16), idf=np.eye(128, dtype=np.float32))


def kernel(x_prompt, x_sample, cache_a_k, cache_a_v, state_conv, cache_c_k, cache_c_v,
           norm_g, w_in, w_out, rel_bias, conv_w, lam_q1, lam_k1, lam_q2, lam_k2, subln_g, final_g):
    f = lambda a: np.ascontiguousarray(np.asarray(a, dtype=np.float32))
    x_prompt, x_sample = f(x_prompt), f(x_sample)
    consts = _host_consts(f(rel_bias))
    shared = dict(
        xp=f(x_prompt[0]),
        ng=f(f(norm_g).reshape(2, 8, 128).transpose(0, 2, 1)),
        win=f(w_in), wout=f(w_out),
        convw=f(np.broadcast_to(f(conv_w)[:, :, None, :], (2, 3, 128, 256))),
        lam=f(np.broadcast_to(np.stack([f(lam_q1), f(lam_k1), f(lam_q2), f(lam_k2)], axis=1)[:, :, None, :], (2, 4, 128, 32))),
        subg=f(np.broadcast_to(f(subln_g)[:, None, :], (2, 128, 64))),
        fing=f(np.broadcast_to(f(final_g)[None, :], (128, 1024))),
        **consts,
    )
    import ml_dtypes
    in_maps = []
    blocks_of = []
    for c in range(NCORES):
        blk = [8 * j + (c if j % 2 == 0 else 7 - c) for j in range(NSLOT)]
        blocks_of.append(blk)
        cmask = np.zeros((NSLOT, 32, 2, 512), np.float32)
        padb = np.zeros((128, NSLOT * 20), np.float32)
        for j, b in enumerate(blk):
            for u in range(32):
                kt = 32 * j + u
                for sq_ in range(4):
                    gq = 4 * b + sq_
                    cols = slice(sq_ * 128, (sq_ + 1) * 128)
                    if kt > gq:
                        cmask[j, u, 0, cols] = NEG
                    elif kt == gq:
                        cmask[j, u, 1, sq_ * 128: sq_ * 128 + 64] = NEG
            for qt in range(4):
                g = 4 * b + qt
                for jj in range(5):
                    if g + jj - 4 < 0:
                        padb[:, (j * 4 + qt) * 5 + jj] = NEG
        shared_c = dict(cmask=cmask.astype(ml_dtypes.bfloat16), padb=padb)
        sl = slice(NSEQ * c, NSEQ * (c + 1))
        m = dict(shared)
        m.update(shared_c)
        m.update(
            xs=f(x_sample[sl]),
            cak=f(f(cache_a_k)[:, sl].reshape(2, NSEQ, AWIN, 512)), cav=f(f(cache_a_v)[:, sl].reshape(2, NSEQ, AWIN, 512)),
            scv=f(f(state_conv)[:, sl]),
            cck=f(f(cache_c_k)[:, sl].reshape(2, NSEQ, PAST, 256)), ccv=f(f(cache_c_v)[:, sl].reshape(2, NSEQ, PAST, 256)),
        )
        in_maps.append(m)
    nc = build_nc()
    res = run_bass_kernel_spmd(nc, in_maps, core_ids=list(range(NCORES)))
    R = res.results
    cat = lambda k, ax: np.concatenate([R[c][k] for c in range(NCORES)], axis=ax)
    r0 = R[0]
    yp_full = np.zeros((1, S_P, 1024), np.float32)
    for c in range(NCORES):
        for j, b in enumerate(blocks_of[c]):
            yp_full[0, b * 512:(b + 1) * 512] = R[c]["yp"][j * 512:(j + 1) * 512]
    return (
        yp_full,
        cat("ys", 0).reshape(16, T_S, 1024).astype(np.float32),
        r0["pak"].reshape(2, 1, 512, 8, 64), r0["pav"].reshape(2, 1, 512, 8, 64),
        r0["pconv"].reshape(2, 1, 2, 256),
        r0["pck"].reshape(2, 1, S_P, 4, 2, 32), r0["pcv"].reshape(2, 1, S_P, 4, 64),
        cat("sak", 1).reshape(2, 16, T_S, 8, 64), cat("sav", 1).reshape(2, 16, T_S, 8, 64),
        cat("sconv", 1).reshape(2, 16, 2, 256),
        cat("sck", 1).reshape(2, 16, T_S, 4, 2, 32), cat("scv2", 1).reshape(2, 16, T_S, 4, 64),
    )
```

```python
import math
from contextlib import ExitStack
import numpy as np
import concourse.bass as bass
import concourse.mybir as mybir
from concourse.bass_utils import run_bass_kernel_spmd

F32 = mybir.dt.float32
BF16 = mybir.dt.bfloat16
AF = mybir.ActivationFunctionType
ALU = mybir.AluOpType
AX = mybir.AxisListType

S_P = 16384
NT_P = 128
T_S = 16
PAST = 4096
AWIN = 512
NEG = -30000.0
C_SCALE = 32 ** -0.5
NCORES = 8
NSEQ = 2
NSLOT = 4


class _Rec:
    def __init__(self):
        self.calls = []

    def __getattr__(self, name):
        def call(*a, **k):
            self.calls.append((name, a, k))
            return self
        return call


class Tracker:
    def __init__(self, nc, es):
        self.nc = nc
        self.ops = {e: [] for e in ("pe", "act", "dve", "pool", "sp")}
        self.csem = {e: es.enter_context(nc.semaphore("c_" + e)) for e in ("pe", "act", "dve", "pool")}
        self.ccnt = {e: 0 for e in self.csem}
        self.dsem = {q: [es.enter_context(nc.semaphore("d_%s%d" % (q, i))) for i in range(8)] for q in ("sp", "pool")}
        self.dcnt = {q: 0 for q in self.dsem}
        self.seen = {e: {} for e in self.ops}
        self.lastw = {}
        self.readers = {}
        self.out_tokens = []

    def _deps(self, reads, writes):
        deps = []
        for k in list(reads) + list(writes):
            if k in self.lastw:
                deps.append(self.lastw[k])
        for k in writes:
            deps.extend(self.readers.get(k, []))
        return deps

    def _update(self, tok, reads, writes):
        for k in writes:
            self.lastw[k] = tok
            self.readers[k] = []
        for k in reads:
            self.readers.setdefault(k, []).append(tok)

    def _waits(self, eng, deps, skip_self_sem=None):
        need = {}
        for (sem, val, seng) in deps:
            if skip_self_sem is not None and seng == skip_self_sem:
                continue
            key = id(sem)
            if self.seen[eng].get(key, 0) >= val:
                continue
            if key not in need or need[key][1] < val:
                need[key] = (sem, val)
        for key, (sem, val) in need.items():
            self.seen[eng][key] = val
        return list(need.values())

    def op(self, eng, fn, reads=(), writes=()):
        deps = self._deps(reads, writes)
        waits = self._waits(eng, deps, skip_self_sem=("pe" if eng == "pe" else None))
        self.ccnt[eng] += 1
        sem, val = self.csem[eng], self.ccnt[eng]

        rec = _Rec()
        fn(rec)
        name, a, k = rec.calls[0]

        def emit(e, waits=waits, name=name, a=a, k=k, sem=sem):
            for (s, v) in waits:
                e.wait_ge(s, v)
            getattr(e, name)(*a, **k).then_inc(sem, 1)
        self.ops[eng].append(emit)
        self._update((sem, val, eng), reads, writes)

    def dma(self, q, out, in_, reads=(), writes=(), is_output=False, percore=False):
        deps = self._deps(reads, writes)
        i = self.dcnt[q]
        self.dcnt[q] += 1
        sem = self.dsem[q][i % 8]
        val = (i // 8 + 1) * 16
        if i >= 8:
            deps.append((sem, val - 16, "dq"))
        waits = self._waits(q, deps)

        def emit(e, c=None, waits=waits, sem=sem, out=out, in_=in_):
            for (s, v) in waits:
                e.wait_ge(s, v)
            i_ = in_(c) if callable(in_) else in_
            e.dma_start(out=out, in_=i_).then_inc(sem, 16)
        emit.percore = percore
        self.ops[q].append(emit)
        tok = (sem, val, "dq")
        self._update(tok, reads, writes)
        if is_output:
            self.out_tokens.append(tok)

    def finish(self):
        for q in ("sp", "pool"):
            toks = []
            n = self.dcnt[q]
            for s in range(min(8, n)):
                last_i = ((n - 1 - s) // 8) * 8 + s
                toks.append((self.dsem[q][s], (last_i // 8 + 1) * 16, "dq"))
            if q == "sp":
                toks = toks + self.out_tokens
            waits = self._waits(q, toks)

            def emit(e, waits=waits):
                for (s, v) in waits:
                    e.wait_ge(s, v)
            self.ops[q].append(emit)


def build_nc():
    nc = bass.Bass("TRN2", target_bir_lowering=False)

    def din(name, shape, dt=F32):
        return nc.dram_tensor(name, list(shape), dt, kind="ExternalInput").ap()

    def dout(name, shape):
        return nc.dram_tensor(name, list(shape), F32, kind="ExternalOutput").ap()

    def dscr(name, shape, dt):
        return nc.dram_tensor(name, list(shape), dt).ap()

    I = dict(
        xp=din("xp", [S_P, 1024]), xs=din("xs", [NSEQ, T_S, 1024]),
        cak=din("cak", [2, NSEQ, AWIN, 512]), cav=din("cav", [2, NSEQ, AWIN, 512]),
        scv=din("scv", [2, NSEQ, 2, 256]),
        cck=din("cck", [2, NSEQ, PAST, 256]), ccv=din("ccv", [2, NSEQ, PAST, 256]),
        ng=din("ng", [2, 128, 8]), win=din("win", [2, 1024, 4096]), wout=din("wout", [2, 1024, 1024]),
        biasp=din("biasp", [2, 128, 8, 5, 128]), biass=din("biass", [2, 128, 8, 5, T_S]),
        convw=din("convw", [2, 3, 128, 256]),
        lam=din("lam", [2, 4, 128, 32]), subg=din("subg", [2, 128, 64]), fing=din("fing", [128, 1024]),
        ropep=din("ropep", [S_P, 128]), ropes=din("ropes", [T_S, 128]),
        idb=din("idb", [128, 128], BF16), idf=din("idf", [128, 128]),
        cmask=din("cmask", [NSLOT, 32, 2, 512], BF16),
        padb=din("padb", [128, NSLOT * 4 * 5]), lmc=din("lmc", [2, 128], BF16),
    )
    O = dict(
        yp=dout("yp", [NSLOT * 512, 1024]), ys=dout("ys", [NSEQ, T_S, 1024]),
        pak=dout("pak", [2, 512, 512]), pav=dout("pav", [2, 512, 512]), pcv_=dout("pconv", [2, 2, 256]),
        pck=dout("pck", [2, S_P, 256]), pcv=dout("pcv", [2, S_P, 256]),
        sak=dout("sak", [2, NSEQ, T_S, 512]), sav=dout("sav", [2, NSEQ, T_S, 512]),
        scvo=dout("sconv", [2, NSEQ, 2, 256]),
        sck=dout("sck", [2, NSEQ, T_S, 256]), scv2=dout("scv2", [2, NSEQ, T_S, 256]),
    )

    class Seq:
        pass

    seqs = []
    for si in range(1 + NSEQ):
        s = Seq()
        s.name = "p" if si == 0 else "s%d" % (si - 1)
        s.prompt = si == 0
        s.T = S_P if s.prompt else T_S
        s.nt = 128 if s.prompt else T_S
        s.ntiles = NT_P if s.prompt else 1
        s.ca_tiles = 4
        s.cc_tiles = 0 if s.prompt else PAST // 128
        s.KA = s.ca_tiles * 128 + s.T
        s.KC = s.cc_tiles * 128 + s.T
        s.x_in = I["xp"] if s.prompt else I["xs"][si - 1]
        s.rope = I["ropep"] if s.prompt else I["ropes"]
        s.si = si - 1
        n = s.name
        s.X1 = dscr("X1" + n, [s.nt, s.ntiles * 1024], F32)
        s.U = dscr("U" + n, [s.T + 2, 256], F32)
        s.AQT = dscr("AQT" + n, [512, s.T], BF16)
        s.AKT = dscr("AKT" + n, [512, s.KA], BF16)
        s.AV1 = dscr("AV1" + n, [128, (s.ca_tiles + s.ntiles) * 520], BF16)
        s.CQT = dscr("CQT" + n, [256, s.T], BF16)
        s.CKT = dscr("CKT" + n, [256, s.KC], BF16)
        s.CV1 = dscr("CV1" + n, [s.KC, 260], BF16)
        s.GA = dscr("GA" + n, [s.nt, s.ntiles * 512], BF16)
        s.GC = dscr("GC" + n, [s.nt, s.ntiles * 256], BF16)
        s.OB = dscr("OB" + n, [s.nt, s.ntiles * 256], BF16)
        s.YC = dscr("YC" + n, [s.T, 256], BF16)
        s.YC2 = dscr("YC2" + n, [NSLOT * 512, 256], BF16)
        seqs.append(s)

    es = ExitStack()
    with es:
        def sb(name, shape, dt=F32):
            return es.enter_context(nc.sbuf_tensor("sb_" + name, list(shape), dt))

        def ps(name, shape, dt=F32):
            return es.enter_context(nc.psum_tensor("ps_" + name, list(shape), dt))

        BIG = sb("BIG", [128, 33152], BF16)
        zbuf = [sb("z0", [128, 4096]), sb("z1", [128, 4096])]
        z = zbuf[0]
        ztile = [0]
        xts = [sb("xt%d" % i, [128, 1024]) for i in range(2)]; xn = sb("xn", [128, 1024], BF16)
        hT = sb("hT", [128, 8, 128], BF16)
        st = sb("st", [128, 8]); rs = sb("rs", [128, 4]); gcol = sb("gcol", [128, 16])
        ropes = [sb("rope%d" % i, [128, 128]) for i in range(2)]; rt = sb("rt", [128, 4, 64])
        stg = sb("stg", [128, 1536], BF16); tT = sb("tT", [128, 8, 128], BF16)
        av1 = sb("av1", [128, 8, 65], BF16); cv1 = sb("cv1", [128, 4, 65], BF16)
        ga = sb("ga", [128, 512], BF16); gc = sb("gc", [128, 256], BF16)
        u = sb("u", [128, 256]); um1 = sb("um1", [128, 256]); um2 = sb("um2", [128, 256]); cvt = sb("cvt", [128, 256])
        sg = sb("sg", [128, 256]); sgt = sb("sgt", [128, 1024]); ob = sb("ob", [128, 256], BF16)
        cw = sb("cw", [128, 3, 256]); zero2 = sb("zero2", [2, 256])
        lamt = sb("lamt", [128, 4, 32]); lamj = sb("lamj", [128, 32]); lamv = sb("lamv", [128, 8])
        subg = sb("subg", [128, 64]); fing = sb("fing", [128, 1024])
        idb = sb("idb", [128, 128], BF16); idf = sb("idf", [128, 128])
        Qz = [[sb("Qz%d_%d" % (i, hh), [128, 2, 512], BF16) for hh in range(2)] for i in range(2)]
        Pm = [sb("Pm%d" % i, [128, 2, 512], BF16) for i in range(3)]
        oT = sb("oT", [128, 2, 512])
        rr = sb("rr", [128, 16]); o1 = sb("o1", [128, 64]); o2 = sb("o2", [128, 64]); ssq = sb("ssq", [128, 4])
        gct = sb("gct", [128, 256], BF16); yct = sb("yct", [128, 256], BF16)
        KTbs = [sb("KTb%d" % i, [128, 640], BF16) for i in range(2)]; QTas = [sb("QTa%d" % i, [128, 128], BF16) for i in range(2)]
        biasp = sb("biasp", [128, 8, 5, 128], BF16); biass = sb("biass", [128, 8, 5, T_S], BF16)
        Pas = [sb("Pa%d" % i, [128, 5, 128], BF16) for i in range(2)]; av1bs = [sb("av1b%d" % i, [128, 5, 520], BF16) for i in range(2)]
        ra = sb("ra", [128, 8]); gats = [sb("gat%d" % i, [128, 512], BF16) for i in range(2)]
        y = sb("y", [128, 1024], BF16); yT = sb("yT", [128, 8, 128], BF16)
        xr = sb("xr", [128, 1024]); sq = sb("sq", [128, 1024]); xo = xr
        FA = ps("FA", [128, 1024]); FB = ps("FB", [128, 1024]); FC = ps("FC", [128, 1024])
        T0 = ps("T0", [128, 8, 128], BF16); T1 = ps("T1", [128, 8, 128], BF16)

        zt = sb("zt", [128, 640], BF16); padb = sb("padb", [128, NSLOT * 20]); lmc = sb("lmc", [2, 128], BF16)
        rms = [sb("rm%d" % i, [2, 512], BF16) for i in range(4)]
        tr = Tracker(nc, es)
        blk_of = lambda c, j: 8 * j + (c if j % 2 == 0 else 7 - c)
        g_of = lambda c, lt: 4 * blk_of(c, lt // 4) + lt % 4
        op, dma = tr.op, tr.dma
        alt = [0]

        def evac(out, in_, reads, writes):
            alt[0] ^= 1
            if alt[0]:
                op("act", lambda e: e.activation(out=out, in_=in_, func=AF.Copy), reads, writes)
            else:
                op("dve", lambda e: e.tensor_copy(out=out, in_=in_), reads, writes)

        def c_epilogue(s, h, q0, nq, split=False, slot=0):
            evac(oT[0:65, :, :nq], FC[0:65, :].rearrange("p (m q) -> p m q", m=2)[:, :, :nq], ["FC"], ["oT"])
            for qt in range(max(1, nq // 128)):
                qn = min(128, nq)
                tok0 = q0 + qt * 128
                for m in range(2):
                    op("pe", lambda e, m=m: e.transpose(
                        FA[:qn, m * 128: m * 128 + 65], oT[0:65, m, qt * 128: qt * 128 + qn], idf[0:65, 0:65]),
                        ["oT", "idf"], ["FA0"])
                if split:
                    dma("sp", gct[:qn, :], lambda c, s=s, lt_=slot * 4 + qt: s.GC[:, g_of(c, lt_) * 256:(g_of(c, lt_) + 1) * 256],
                        reads=[("GC", s.name, i) for i in range(s.ntiles)], writes=["gct"], percore=True)
                else:
                    dma("sp", gct[:qn, :], s.GC[:qn, (tok0 // s.nt) * 256:(tok0 // s.nt + 1) * 256], reads=[("GC", s.name, tok0 // s.nt)], writes=["gct"])
                op("dve", lambda e: e.reciprocal(out=rr[:qn, 0:1], in_=FA[:qn, 64:65]), ["FA0"], ["rr"])
                op("dve", lambda e: e.reciprocal(out=rr[:qn, 1:2], in_=FA[:qn, 128 + 64:128 + 65]), ["FA0"], ["rr"])
                op("dve", lambda e: e.tensor_tensor(out=rr[:qn, 1:2], in0=rr[:qn, 1:2], in1=lamv[:qn, 2:3], op=ALU.mult),
                   ["rr", "lamv"], ["rr"])
                op("dve", lambda e: e.tensor_scalar(out=o1[:qn, :], in0=FA[:qn, 0:64], scalar1=rr[:qn, 0:1], scalar2=None,
                                                    op0=ALU.mult), ["FA0", "rr"], ["o1"])
                op("dve", lambda e: e.scalar_tensor_tensor(out=o1[:qn, :], in0=FA[:qn, 128:192], scalar=rr[:qn, 1:2],
                                                           in1=o1[:qn, :], op0=ALU.mult, op1=ALU.add),
                   ["FA0", "rr", "o1"], ["o1"])
                op("dve", lambda e: e.tensor_tensor(out=o2[:qn, :], in0=o1[:qn, :], in1=o1[:qn, :], op=ALU.mult), ["o1"], ["o2"])
                op("dve", lambda e: e.reduce_sum(out=ssq[:qn, 0:1], in_=o2[:qn, :], axis=AX.X), ["o2"], ["ssq"])
                op("dve", lambda e: e.tensor_scalar(out=ssq[:qn, 0:1], in0=ssq[:qn, 0:1], scalar1=1.0 / 64, scalar2=1e-5,
                                                    op0=ALU.mult, op1=ALU.add), ["ssq"], ["ssq"])
                op("act", lambda e: e.activation(out=ssq[:qn, 0:1], in_=ssq[:qn, 0:1], func=AF.Ln), ["ssq"], ["ssq"])
                op("act", lambda e: e.activation(out=ssq[:qn, 0:1], in_=ssq[:qn, 0:1], func=AF.Exp, scale=-0.5), ["ssq"], ["ssq"])
                op("dve", lambda e: e.scalar_tensor_tensor(out=o1[:qn, :], in0=o1[:qn, :], scalar=ssq[:qn, 0:1],
                                                           in1=subg[:qn, :], op0=ALU.mult, op1=ALU.mult),
                   ["o1", "ssq", "subg"], ["o1"])
                op("dve", lambda e: e.tensor_tensor(out=yct[:qn, 0:64], in0=o1[:qn, :], in1=gct[:qn, h * 64:(h + 1) * 64],
                                                    op=ALU.mult), ["o1", "gct"], ["yct"])
                dma("sp", (s.YC2 if split else s.YC)[tok0:tok0 + qn, h * 64:(h + 1) * 64], yct[:qn, 0:64], reads=["yct"],
                    writes=[("YC2" if split else "YC", s.name, tok0 // s.nt, h)])

        dma("sp", idb[:], I["idb"][:, :], writes=["idb"])
        dma("sp", idf[:], I["idf"][:, :], writes=["idf"])
        dma("sp", fing[:], I["fing"][:, :], writes=["fing"])
        op("pool", lambda e: e.memset(zero2[:], 0.0), writes=["zero2"])
        op("pool", lambda e: e.memset(zt[:], 0.0), writes=["zt"])
        dma("sp", padb[:], I["padb"][:, :], writes=["padb"])
        dma("sp", lmc[:], I["lmc"][:, :], writes=["lmc"])
        op("pool", lambda e: e.memset(av1[:], 1.0), writes=["av1"])
        op("pool", lambda e: e.memset(cv1[:], 1.0), writes=["cv1"])
        for i in range(2):
            for hh in range(2):
                op("pool", lambda e: e.memset(Qz[i][hh][:], 0.0), writes=["Qz%d" % i])
        op("pool", lambda e: e.memset(BIG[:, 33024:33152], 0.0), writes=["BIG"])

        for l in range(2):
            lam_init = 0.8 - 0.6 * math.exp(-0.3 * l)
            dma("sp", gcol[:, 0:8], I["ng"][l], writes=["gcol"])
            for k in range(8):
                for hf in range(2):
                    zk = "z%d" % hf
                    dma("sp", zbuf[0][:, hf * 2048:(hf + 1) * 2048], I["win"][l, k * 128:(k + 1) * 128, hf * 2048:(hf + 1) * 2048],
                        writes=["zb0c%d" % i for i in range(hf * 4, hf * 4 + 4)])
                    dst = BIG[:, k * 4096 + hf * 2048: k * 4096 + (hf + 1) * 2048]
                    if hf:
                        op("dve", lambda e, dst=dst, hf=hf, k=k: e.tensor_scalar(out=dst, in0=zbuf[0][:, hf * 2048:(hf + 1) * 2048],
                                                                                 scalar1=gcol[:, k:k + 1], scalar2=None, op0=ALU.mult),
                           reads=["zb0c%d" % i for i in range(hf * 4, hf * 4 + 4)] + ["gcol"], writes=["BIG"])
                    else:
                        op("act", lambda e, dst=dst, hf=hf, k=k: e.activation(out=dst, in_=zbuf[0][:, hf * 2048:(hf + 1) * 2048], func=AF.Copy,
                                                                              scale=gcol[:, k:k + 1]),
                           reads=["zb0c%d" % i for i in range(hf * 4, hf * 4 + 4)] + ["gcol"], writes=["BIG"])
            dma("pool", biasp[:].rearrange("p a b c -> p (a b c)"), I["biasp"][l].rearrange("p a b c -> p (a b c)"), writes=["biasp"])
            dma("pool", biass[:].rearrange("p a b c -> p (a b c)"), I["biass"][l].rearrange("p a b c -> p (a b c)"), writes=["biass"])
            dma("sp", cw[:], I["convw"][l].rearrange("j p c -> p j c"), writes=["cw"])
            dma("sp", lamt[:], I["lam"][l].rearrange("j p c -> p j c"), writes=["lamt"])
            dma("sp", subg[:], I["subg"][l], writes=["subg"])
            op("dve", lambda e: e.tensor_scalar(out=subg[:], in0=subg[:], scalar1=float(1.0 - lam_init), scalar2=None, op0=ALU.mult),
               reads=["subg"], writes=["subg"])
            for j in range(2):
                op("dve", lambda e, j=j: e.tensor_tensor(out=lamj[:], in0=lamt[:, 2 * j, :], in1=lamt[:, 2 * j + 1, :], op=ALU.mult),
                   reads=["lamt"], writes=["lamj"])
                op("dve", lambda e, j=j: e.reduce_sum(out=lamv[:, j:j + 1], in_=lamj[:], axis=AX.X), reads=["lamj"], writes=["lamv"])
            op("act", lambda e: e.activation(out=lamv[:, 0:2], in_=lamv[:, 0:2], func=AF.Exp), reads=["lamv"], writes=["lamv"])
            op("dve", lambda e: e.tensor_tensor(out=lamv[:, 2:3], in0=lamv[:, 1:2], in1=lamv[:, 0:1], op=ALU.subtract),
               reads=["lamv"], writes=["lamv"])
            op("dve", lambda e: e.tensor_scalar(out=lamv[:, 2:3], in0=lamv[:, 2:3], scalar1=float(-lam_init), scalar2=None, op0=ALU.add),
               reads=["lamv"], writes=["lamv"])

            for s in seqs:
                nt = s.nt
                if s.prompt:
                    dma("sp", s.U[0:2, :], zero2[:], reads=["zero2"], writes=[("U", s.name, -1)])
                    if l == 0:
                        for p4 in range(4):
                            dma("sp", s.AKT[p4 * 128:(p4 + 1) * 128, 0:512], zt[:, 0:512], reads=["zt"], writes=[("AKT", s.name, i) for i in range(4)])
                            dma("sp", s.AV1[:, p4 * 520:(p4 + 1) * 520], zt[:, 0:520], reads=["zt"], writes=[("AV1", s.name, p4)])
                else:
                    dma("sp", um1[0:2, :], I["scv"][l, s.si], writes=["um1"])
                    dma("sp", s.U[0:2, :], um1[0:2, :], reads=["um1"], writes=[("U", s.name, -1)])
                    for kt in range(s.cc_tiles):
                        dma("pool", stg[:, 0:256], I["cck"][l, s.si, kt * 128:(kt + 1) * 128, :], writes=["stg"])
                        for b in range(2):
                            op("pe", lambda e, b=b: e.transpose(T1[:, b, :], stg[:, b * 128:(b + 1) * 128], idb[:]),
                               reads=["stg", "idb"], writes=["T1"])
                        evac(tT[:, 0:2, :], T1[:, 0:2, :], ["T1"], ["tT"])
                        dma("sp", s.CKT.rearrange("(b p) t -> p b t", p=128)[:, :, kt * 128:(kt + 1) * 128], tT[:, 0:2, :],
                            reads=["tT"], writes=[("CKT", s.name, kt)])
                        dma("pool", cv1[:, :, 0:64], I["ccv"][l, s.si, kt * 128:(kt + 1) * 128, :].rearrange("t (h e) -> t h e", e=64),
                            writes=["cv1"])
                        dma("sp", s.CV1[kt * 128:(kt + 1) * 128, :], cv1[:].rearrange("p h e -> p (h e)"), reads=["cv1"],
                            writes=[("CV1", s.name, kt)])
                    for kt in range(0 if s.prompt else s.ca_tiles):
                        dma("pool", stg[:, 0:512], I["cak"][l, s.si, kt * 128:(kt + 1) * 128, :], writes=["stg"])
                        for b in range(4):
                            op("pe", lambda e, b=b: e.transpose(T1[:, b, :], stg[:, b * 128:(b + 1) * 128], idb[:]),
                               reads=["stg", "idb"], writes=["T1"])
                        evac(tT[:, 0:4, :], T1[:, 0:4, :], ["T1"], ["tT"])
                        dma("sp", s.AKT.rearrange("(b p) t -> p b t", p=128)[:, :, kt * 128:(kt + 1) * 128], tT[:, 0:4, :],
                            reads=["tT"], writes=[("AKT", s.name, kt)])
                        dma("pool", av1[:, :, 0:64], I["cav"][l, s.si, kt * 128:(kt + 1) * 128, :].rearrange("t (h e) -> t h e", e=64),
                            writes=["av1"])
                        dma("sp", s.AV1[:, kt * 520:(kt + 1) * 520], av1[:].rearrange("p h e -> p (h e)"), reads=["av1"],
                            writes=[("AV1", s.name, kt)])

                zbase = ztile[0]

                def frontA(t):
                    r0 = t * nt
                    ka0 = s.ca_tiles * 128 + r0
                    kc0 = s.cc_tiles * 128 + r0
                    zi = (zbase + t) % 2
                    z = zbuf[zi]
                    zc0, zc1, zc2, zc3, zc4, zc5, zc6, zc7 = ["zb%dc%d" % (zi, i) for i in range(8)]
                    xt, rope = xts[zi], ropes[zi]
                    xtk, ropek = "xt%d" % zi, "rope%d" % zi
                    dma("pool", xt[:nt, :], s.x_in[r0:r0 + nt, :] if l == 0 else s.X1[:nt, t * 1024:(t + 1) * 1024],
                        reads=[("X1", s.name, t)] if l else [], writes=[xtk])
                    dma("pool", rope[:nt, :], s.rope[r0:r0 + nt, :], writes=[ropek])
                    op("act", lambda e: e.activation(out=xn[:nt, :], in_=xt[:nt, :], func=AF.Copy), [xtk], ["xn"])
                    for k in range(8):
                        op("pe", lambda e, k=k: e.transpose(T0[:, k, :nt], xn[:nt, k * 128:(k + 1) * 128], idb[:nt, :nt]),
                           ["xn", "idb"], ["T0"])
                    op("act", lambda e: e.activation(out=hT[:, :, :nt], in_=T0[:, :, :nt], func=AF.Copy), ["T0"], ["hT"])

                def frontB(t, mid=None):
                    r0 = t * nt
                    ka0 = s.ca_tiles * 128 + r0
                    kc0 = s.cc_tiles * 128 + r0
                    zi = (zbase + t) % 2
                    z = zbuf[zi]
                    zc0, zc1, zc2, zc3, zc4, zc5, zc6, zc7 = ["zb%dc%d" % (zi, i) for i in range(8)]
                    xt, rope = xts[zi], ropes[zi]
                    xtk, ropek = "xt%d" % zi, "rope%d" % zi
                    for cb in range(8):
                        pk = "FA%d" % (cb % 2)
                        pt = FA[:nt, (cb % 2) * 512:(cb % 2 + 1) * 512]
                        for k in range(8):
                            op("pe", lambda e, k=k, pt=pt, cb=cb: e.matmul(pt, lhsT=hT[:, k, :nt],
                                                                           rhs=BIG[:, k * 4096 + cb * 512: k * 4096 + (cb + 1) * 512],
                                                                           start=(k == 0), stop=(k == 7)),
                               ["hT", "BIG"], [pk])
                        op("act", lambda e: e.activation(out=z[:nt, cb * 512:(cb + 1) * 512], in_=pt, func=AF.Copy),
                           [pk], ["zb%dc%d" % (zi, cb)])
                        if cb == 3 and mid is not None:
                            mid()

                def epiA(t):
                    r0 = t * nt
                    ka0 = s.ca_tiles * 128 + r0
                    kc0 = s.cc_tiles * 128 + r0
                    zi = (zbase + t) % 2
                    z = zbuf[zi]
                    zc0, zc1, zc2, zc3, zc4, zc5, zc6, zc7 = ["zb%dc%d" % (zi, i) for i in range(8)]
                    xt, rope = xts[zi], ropes[zi]
                    xtk, ropek = "xt%d" % zi, "rope%d" % zi
                    rsk = "rs%d" % zi
                    op("dve", lambda e: e.tensor_tensor(out=sq[:nt, :], in0=xt[:nt, :], in1=xt[:nt, :], op=ALU.mult), [xtk], ["sq"])
                    op("dve", lambda e: e.reduce_sum(out=rs[:nt, zi:zi + 1], in_=sq[:nt, :], axis=AX.X), ["sq"], [rsk])
                    op("dve", lambda e: e.tensor_scalar(out=rs[:nt, zi:zi + 1], in0=rs[:nt, zi:zi + 1], scalar1=1.0 / 1024, scalar2=1e-6,
                                                         op0=ALU.mult, op1=ALU.add), [rsk], [rsk])
                    op("act", lambda e: e.activation(out=rs[:nt, zi:zi + 1], in_=rs[:nt, zi:zi + 1], func=AF.Ln), [rsk], [rsk])
                    op("act", lambda e: e.activation(out=rs[:nt, zi:zi + 1], in_=rs[:nt, zi:zi + 1], func=AF.Exp, scale=-0.5), [rsk], [rsk])
                    op("dve", lambda e: e.tensor_scalar(out=rs[:nt, 2 + zi:3 + zi], in0=rs[:nt, zi:zi + 1], scalar1=-1.0, scalar2=None, op0=ALU.mult),
                       [rsk], [rsk + "n"])
                    op("act", lambda e: e.activation(out=sgt[:nt, 0:512], in_=z[:nt, 1536:2048], func=AF.Exp, scale=rs[:nt, 2 + zi:3 + zi]), [zc3, rsk + "n"], ["sgtA"])
                    op("act", lambda e: e.activation(out=sgt[:nt, 512:768], in_=z[:nt, 3840:4096], func=AF.Exp, scale=rs[:nt, 2 + zi:3 + zi]), [zc7, rsk + "n"], ["sgtC"])
                    op("act", lambda e: e.activation(out=sgt[:nt, 768:1024], in_=z[:nt, 2816:3072], func=AF.Exp, scale=rs[:nt, 2 + zi:3 + zi]), [zc5, rsk + "n"], ["sgtB"])
                    for cb in range(8):
                        op("dve", lambda e, cb=cb: e.tensor_scalar(out=z[:nt, cb * 512:(cb + 1) * 512], in0=z[:nt, cb * 512:(cb + 1) * 512],
                                                                   scalar1=rs[:nt, zi:zi + 1], scalar2=None, op0=ALU.mult),
                           ["zb%dc%d" % (zi, cb), rsk], ["zb%dc%d" % (zi, cb)])
                    zz = z[:nt, 3072:3584].rearrange("p (g d) -> p g d", d=32)
                    x1, x2 = zz[:, :, 0:4], zz[:, :, 4:8]
                    cs = rope[:nt, 0:64].rearrange("p (g d) -> p g d", d=4)
                    sn = rope[:nt, 64:128].rearrange("p (g d) -> p g d", d=4)
                    rv = [rt[:nt, i, :].rearrange("p (g d) -> p g d", d=4) for i in range(4)]
                    op("dve", lambda e: e.tensor_tensor(out=rv[0], in0=x1, in1=cs, op=ALU.mult), [zc6, ropek], ["rt0"])
                    op("dve", lambda e: e.tensor_tensor(out=rv[1], in0=x2, in1=sn, op=ALU.mult), [zc6, ropek], ["rt1"])
                    op("dve", lambda e: e.tensor_tensor(out=rv[2], in0=x2, in1=cs, op=ALU.mult), [zc6, ropek], ["rt2"])
                    op("dve", lambda e: e.tensor_tensor(out=rv[3], in0=x1, in1=sn, op=ALU.mult), [zc6, ropek], ["rt3"])
                    op("dve", lambda e: e.tensor_tensor(out=x1, in0=rv[0], in1=rv[1], op=ALU.subtract), ["rt0", "rt1", "rt3"], [zc6])
                    op("dve", lambda e: e.tensor_tensor(out=x2, in0=rv[2], in1=rv[3], op=ALU.add), ["rt2", "rt3"], [zc6])
                    if s.prompt:
                        dma("sp", O["pck"][l, r0:r0 + nt, :], z[:nt, 3328:3584], reads=[zc6], is_output=True)
                        dma("sp", O["pcv"][l, r0:r0 + nt, :], z[:nt, 3584:3840], reads=[zc7], is_output=True)
                        if t >= NT_P - 4:
                            rr0 = (t - (NT_P - 4)) * 128
                            dma("sp", O["pak"][l, rr0:rr0 + 128, :], z[:nt, 512:1024], reads=[zc1], is_output=True)
                            dma("sp", O["pav"][l, rr0:rr0 + 128, :], z[:nt, 1024:1536], reads=[zc2], is_output=True)
                    else:
                        dma("sp", O["sck"][l, s.si], z[:nt, 3328:3584], reads=[zc6], is_output=True)
                        dma("sp", O["scv2"][l, s.si], z[:nt, 3584:3840], reads=[zc7], is_output=True)
                        dma("sp", O["sak"][l, s.si], z[:nt, 512:1024], reads=[zc1], is_output=True)
                        dma("sp", O["sav"][l, s.si], z[:nt, 1024:1536], reads=[zc2], is_output=True)
                    op("dve", lambda e: e.tensor_scalar(out=stg[:nt, 0:512], in0=z[:nt, 0:512], scalar1=0.125, scalar2=None, op0=ALU.mult), [zc0], ["stg"])
                    op("dve", lambda e: e.tensor_copy(out=stg[:nt, 512:1024], in_=z[:nt, 512:1024]), [zc1], ["stg"])
                    op("dve", lambda e: e.tensor_copy(out=stg[:nt, 1024:1536], in_=z[:nt, 3072:3584]), [zc6], ["stg"])
                    op("dve", lambda e: e.tensor_tensor(out=u[:nt, :], in0=z[:nt, 2304:2560], in1=z[:nt, 2560:2816], op=ALU.mult),
                       [zc4, zc5], ["u"])
                    dma("sp", s.U[2 + r0:2 + r0 + nt, :], u[:nt, :], reads=["u"], writes=[("U", s.name, t)])
                    dma("sp", um1[:nt, :], s.U[1 + r0:1 + r0 + nt, :], reads=[("U", s.name, t), ("U", s.name, t - 1)], writes=["um1"])
                    dma("sp", um2[:nt, :], s.U[r0:r0 + nt, :], reads=[("U", s.name, t), ("U", s.name, t - 1)], writes=["um2"])
                    if t == s.ntiles - 1:
                        dst = O["pcv_"][l] if s.prompt else O["scvo"][l, s.si]
                        dma("sp", dst, s.U[s.T:s.T + 2, :], reads=[("U", s.name, t)], is_output=True)

                def epiT(t):
                    r0 = t * nt
                    ka0 = s.ca_tiles * 128 + r0
                    kc0 = s.cc_tiles * 128 + r0
                    zi = (zbase + t) % 2
                    z = zbuf[zi]
                    zc0, zc1, zc2, zc3, zc4, zc5, zc6, zc7 = ["zb%dc%d" % (zi, i) for i in range(8)]
                    xt, rope = xts[zi], ropes[zi]
                    xtk, ropek = "xt%d" % zi, "rope%d" % zi
                    for b in range(8):
                        op("pe", lambda e, b=b: e.transpose(T1[:, b, :nt], stg[:nt, b * 128:(b + 1) * 128], idb[:nt, :nt]),
                           ["stg", "idb"], ["T1"])
                    op("dve", lambda e: e.tensor_copy(out=tT[:, :, :nt], in_=T1[:, :, :nt]), ["T1"], ["tT"])
                    dma("sp", s.AQT.rearrange("(b p) t -> p b t", p=128)[:, :, r0:r0 + nt], tT[:, 0:4, :nt], reads=["tT"],
                        writes=[("AQT", s.name, t)])
                    dma("sp", s.AKT.rearrange("(b p) t -> p b t", p=128)[:, :, ka0:ka0 + nt], tT[:, 4:8, :nt], reads=["tT"],
                        writes=[("AKT", s.name, s.ca_tiles + t)])
                    for b in range(4):
                        op("pe", lambda e, b=b: e.transpose(T1[:, b, :nt], stg[:nt, 1024 + b * 128:1024 + (b + 1) * 128], idb[:nt, :nt]),
                           ["stg", "idb"], ["T1"])
                    op("dve", lambda e: e.tensor_copy(out=tT[:, 0:4, :nt], in_=T1[:, 0:4, :nt]), ["T1"], ["tT"])
                    dma("sp", s.CQT.rearrange("(b p) t -> p b t", p=128)[:, :, r0:r0 + nt], tT[:, 0:2, :nt], reads=["tT"],
                        writes=[("CQT", s.name, t)])
                    dma("sp", s.CKT.rearrange("(b p) t -> p b t", p=128)[:, :, kc0:kc0 + nt], tT[:, 2:4, :nt], reads=["tT"],
                        writes=[("CKT", s.name, s.cc_tiles + t)])

                def epiB(t):
                    r0 = t * nt
                    ka0 = s.ca_tiles * 128 + r0
                    kc0 = s.cc_tiles * 128 + r0
                    zi = (zbase + t) % 2
                    z = zbuf[zi]
                    zc0, zc1, zc2, zc3, zc4, zc5, zc6, zc7 = ["zb%dc%d" % (zi, i) for i in range(8)]
                    xt, rope = xts[zi], ropes[zi]
                    xtk, ropek = "xt%d" % zi, "rope%d" % zi
                    op("dve", lambda e: e.tensor_copy(out=av1[:nt, :, 0:64], in_=z[:nt, 1024:1536].rearrange("p (h e) -> p h e", e=64)),
                       [zc2], ["av1"])
                    dma("sp", s.AV1[:nt, (s.ca_tiles + t) * 520:(s.ca_tiles + t + 1) * 520], av1[:nt].rearrange("p h e -> p (h e)"), reads=["av1"],
                        writes=[("AV1", s.name, s.ca_tiles + t)])
                    op("dve", lambda e: e.tensor_copy(out=cv1[:nt, :, 0:64], in_=z[:nt, 3584:3840].rearrange("p (h e) -> p h e", e=64)),
                       [zc7], ["cv1"])
                    dma("sp", s.CV1[kc0:kc0 + nt, :], cv1[:nt].rearrange("p h e -> p (h e)"), reads=["cv1"],
                        writes=[("CV1", s.name, s.cc_tiles + t)])
                    op("dve", lambda e: e.tensor_scalar(out=sgt[:nt, 0:512], in0=sgt[:nt, 0:512], scalar1=1.0, scalar2=None, op0=ALU.add), ["sgtA"], ["sgtA"])
                    op("dve", lambda e: e.reciprocal(out=sgt[:nt, 0:512], in_=sgt[:nt, 0:512]), ["sgtA"], ["sgtA"])
                    op("dve", lambda e: e.tensor_tensor(out=ga[:nt, :], in0=z[:nt, 1536:2048], in1=sgt[:nt, 0:512], op=ALU.mult), [zc3, "sgtA"], ["ga"])
                    dma("sp", s.GA[:nt, t * 512:(t + 1) * 512], ga[:nt, :], reads=["ga"], writes=[("GA", s.name, t)])
                    op("dve", lambda e: e.tensor_scalar(out=sgt[:nt, 512:768], in0=sgt[:nt, 512:768], scalar1=1.0, scalar2=None, op0=ALU.add), ["sgtC"], ["sgtC"])
                    op("dve", lambda e: e.reciprocal(out=sgt[:nt, 512:768], in_=sgt[:nt, 512:768]), ["sgtC"], ["sgtC"])
                    op("dve", lambda e: e.tensor_tensor(out=gc[:nt, :], in0=z[:nt, 3840:4096], in1=sgt[:nt, 512:768], op=ALU.mult), [zc7, "sgtC"], ["gc"])
                    dma("sp", s.GC[:nt, t * 256:(t + 1) * 256], gc[:nt, :], reads=["gc"], writes=[("GC", s.name, t)])
                    op("dve", lambda e: e.tensor_tensor(out=cvt[:nt, :], in0=u[:nt, :], in1=cw[:nt, 2, :], op=ALU.mult), ["u", "cw"], ["cvt"])
                    op("dve", lambda e: e.tensor_tensor(out=um1[:nt, :], in0=um1[:nt, :], in1=cw[:nt, 1, :], op=ALU.mult), ["um1", "cw"], ["um1"])
                    op("dve", lambda e: e.tensor_tensor(out=um2[:nt, :], in0=um2[:nt, :], in1=cw[:nt, 0, :], op=ALU.mult), ["um2", "cw"], ["um2"])
                    op("dve", lambda e: e.tensor_tensor(out=cvt[:nt, :], in0=cvt[:nt, :], in1=um1[:nt, :], op=ALU.add), ["cvt", "um1"], ["cvt"])
                    op("dve", lambda e: e.tensor_tensor(out=cvt[:nt, :], in0=cvt[:nt, :], in1=um2[:nt, :], op=ALU.add), ["cvt", "um2"], ["cvt"])
                    op("dve", lambda e: e.tensor_scalar(out=sgt[:nt, 768:1024], in0=sgt[:nt, 768:1024], scalar1=1.0, scalar2=None, op0=ALU.add), ["sgtB"], ["sgtB"])
                    op("dve", lambda e: e.reciprocal(out=sgt[:nt, 768:1024], in_=sgt[:nt, 768:1024]), ["sgtB"], ["sgtB"])
                    op("dve", lambda e: e.tensor_tensor(out=sg[:nt, :], in0=z[:nt, 2816:3072], in1=sgt[:nt, 768:1024], op=ALU.mult), [zc5, "sgtB"], ["sg"])
                    op("dve", lambda e: e.tensor_tensor(out=cvt[:nt, :], in0=cvt[:nt, :], in1=z[:nt, 2048:2304], op=ALU.mult), ["cvt", zc4], ["cvt"])
                    op("dve", lambda e: e.tensor_tensor(out=ob[:nt, :], in0=cvt[:nt, :], in1=sg[:nt, :], op=ALU.mult), ["cvt", "sg"], ["ob"])
                    dma("sp", s.OB[:nt, t * 256:(t + 1) * 256], ob[:nt, :], reads=["ob"], writes=[("OB", s.name, t)])

                frontA(0)
                frontB(0)
                for t in range(s.ntiles):
                    if t + 1 < s.ntiles:
                        frontA(t + 1)
                    epiA(t)
                    if t + 1 < s.ntiles:
                        frontB(t + 1, mid=lambda t=t: epiT(t))
                    else:
                        epiT(t)
                    epiB(t)
                ztile[0] += s.ntiles

            KT = BIG[:, 0:16384]
            V1 = BIG[:, 16384:16384 + 128 * 130].rearrange("p (t h e) -> p t h e", h=2, e=65)
            for s in seqs:
                nkt_all = (s.KC + 127) // 128
                nq = 512 if s.prompt else T_S
                split = s.prompt and l == 1
                nqb = NSLOT if split else s.T // nq
                allq = [("CQT", s.name, i) for i in range(s.ntiles)]
                allg = [("GC", s.name, i) for i in range(s.ntiles)]
                for hp in range(2):
                    dma("sp", KT[:, 0:s.KC], s.CKT[hp * 128:(hp + 1) * 128, :],
                        reads=[("CKT", s.name, i) for i in range(nkt_all)], writes=["BIG"])
                    for kt in range(nkt_all):
                        kn = min(128, s.KC - kt * 128)
                        dma("sp", V1[:kn, kt, :, :],
                            s.CV1[kt * 128:kt * 128 + kn, hp * 130:(hp + 1) * 130].rearrange("t (h e) -> t h e", e=65),
                            reads=[("CV1", s.name, kt)], writes=["BIG"])
                    pend = []

                    def flush_pv(keep=1):
                        while len(pend) > keep:
                            pend.pop(0)()
                    ucount = 0
                    for qb in range(nqb):
                        q0 = qb * nq
                        qz = Qz[qb % 2]
                        qk = "Qz%d" % (qb % 2)
                        for hh in range(2):
                            for m in range(2):
                                rws = slice(64 * hh + 32 * m, 64 * hh + 32 * m + 32)
                                if split:
                                    dma("sp", qz[hh][rws, m, :nq],
                                        lambda c, s=s, r_=hp * 128 + rws.start, qb=qb: s.CQT[r_: r_ + 32, blk_of(c, qb) * 512:(blk_of(c, qb) + 1) * 512],
                                        reads=allq, writes=[qk], percore=True)
                                else:
                                    dma("sp", qz[hh][rws, m, :nq], s.CQT[hp * 128 + rws.start: hp * 128 + rws.stop, q0:q0 + nq],
                                        reads=[("CQT", s.name, i) for i in range(q0 // s.nt, (q0 + nq) // s.nt)], writes=[qk])
                        for hh in range(2):
                            h = 2 * hp + hh
                            pb = slice(64 * hh, 64 * hh + 64)
                            nkt = (4 * qb + 4) if s.prompt else nkt_all
                            if split:
                                nkt = min(32 * (qb + 1), nkt_all)
                            for kt in range(nkt):
                                kn = min(128, s.KC - kt * 128)
                                sI = kt - 4 * qb if (s.prompt and not split) else -1
                                um = (kt - 32 * qb) if split else -1
                                if um >= 0:
                                    rm = rms[ucount % 4]
                                    rmk = "rm%d" % (ucount % 4)
                                    dma("sp", rm[:], I["cmask"][qb, um], writes=[rmk])
                                c0 = 128 * sI if sI > 0 else 0
                                par = ucount % 2
                                ucount += 1
                                sc = FA if par == 0 else FB
                                sck = ["FA0", "FA1"] if par == 0 else ["FB"]
                                pm = Pm[(ucount - 1) % 3]
                                pmk = "Pm%d" % ((ucount - 1) % 3)
                                for m in range(2):
                                    op("pe", lambda e, m=m: e.matmul(
                                        sc[:kn, m * 512 + c0: m * 512 + nq], lhsT=KT[:, kt * 128: kt * 128 + kn],
                                        rhs=qz[hh][:, m, c0:nq], start=True, stop=(um < 0)), ["BIG", qk], sck)
                                    if um >= 0:
                                        op("pe", lambda e, m=m: e.matmul(sc[:kn, m * 512: m * 512 + nq], lhsT=lmc[0:2, :kn], rhs=rm[0:2, :nq],
                                                                         start=False, stop=True), ["lmc", rmk], sck)
                                op("act", lambda e: e.activation(
                                    out=pm[:kn, :, c0:nq], in_=sc[:kn, :].rearrange("p (m q) -> p m q", m=2)[:, :, c0:nq],
                                    func=AF.Exp, scale=float(C_SCALE)), sck, [pmk])
                                if sI >= 0:
                                    op("pool", lambda e: e.memset(pm[64:128, :, c0:c0 + 64], 0.0), [pmk], [pmk])
                                flush_pv()

                                def pv(kt=kt, kn=kn, c0=c0, pm=pm, pmk=pmk, hh=hh, nkt=nkt, h=h, q0=q0, nq=nq, split=split):
                                    for m in range(2):
                                        op("pe", lambda e, m=m: e.matmul(
                                            FC[:, m * 512 + c0: m * 512 + nq],
                                            lhsT=BIG[:kn, 16384 + (kt * 2 + hh) * 65: 16384 + (kt * 2 + hh) * 65 + 128], rhs=pm[:kn, m, c0:nq],
                                            start=(kt == 0), stop=(kt == nkt - 1)), ["BIG", pmk], ["FC"])
                                    if kt == nkt - 1:
                                        c_epilogue(s, h, q0, nq, split, q0 // 512)
                                pend.append(pv)
                    flush_pv(0)

            WO = BIG[:, 0:8192].rearrange("p (k c) -> p k c", c=1024)
            for k in range(8):
                dma("sp", zbuf[0][:, 0:1024], I["wout"][l, k * 128:(k + 1) * 128, :], writes=["zb0c0", "zb0c1"])
                op("dve", lambda e, k=k: e.tensor_copy(out=WO[:, k, :], in_=zbuf[0][:, 0:1024]), reads=["zb0c0", "zb0c1"], writes=["BIG"])
            for s in seqs:
                nt = s.nt
                bias = biasp if s.prompt else biass
                split = s.prompt and l == 1
                tiles = [(sl, qt) for sl in range(NSLOT) for qt in range(4)] if split else [(None, t) for t in range(s.ntiles)]
                allt = lambda nm, n_=None: [(nm, s.name, i) for i in range(n_ if n_ is not None else s.ntiles)]
                for ti, (sl, t) in enumerate(tiles):
                    r0 = t * nt
                    lt = ti
                    pad = 4 if s.prompt else 0
                    gk = s.ca_tiles + t
                    k_lo = max(pad, gk - 4)
                    nk = 5 if split else gk - k_lo + 1
                    j0 = 0 if split else 4 - (gk - k_lo)
                    kcols = (nk - 1) * 128 + nt
                    av1b = av1bs[ti % 2]
                    avk = "av1b%d" % (ti % 2)
                    gat = gats[ti % 2]
                    gak = "gat%d" % (ti % 2)
                    if split:
                        dma("sp", av1b[:, 0:5, :].rearrange("p t c -> p (t c)"), lambda c, s=s, lt=lt: s.AV1[:, g_of(c, lt) * 520:(g_of(c, lt) + 5) * 520],
                            reads=allt("AV1", s.ntiles + 4), writes=[avk], percore=True)
                        dma("sp", gat[:nt, :], lambda c, s=s, lt=lt: s.GA[:, g_of(c, lt) * 512:(g_of(c, lt) + 1) * 512], reads=allt("GA"), writes=[gak], percore=True)
                    else:
                        if nk > 1:
                            dma("sp", av1b[:, 0:nk - 1, :].rearrange("p t c -> p (t c)"), s.AV1[:, k_lo * 520:(k_lo + nk - 1) * 520],
                                reads=[("AV1", s.name, i) for i in range(k_lo, gk)], writes=[avk])
                        dma("sp", av1b[:nt, nk - 1, :], s.AV1[:nt, gk * 520:(gk + 1) * 520], reads=[("AV1", s.name, gk)], writes=[avk])
                        dma("sp", gat[:nt, :], s.GA[:nt, t * 512:(t + 1) * 512], reads=[("GA", s.name, t)], writes=[gak])
                    pend = [None]
                    for p in range(4):
                        KTb, QTa = KTbs[p % 2], QTas[p % 2]
                        kq = "KQ%d" % (p % 2)
                        if split:
                            dma("sp", KTb[:, 0:640], lambda c, s=s, p=p, lt=lt: s.AKT[p * 128:(p + 1) * 128, g_of(c, lt) * 128:(g_of(c, lt) + 5) * 128],
                                reads=allt("AKT", s.ntiles + 4), writes=[kq], percore=True)
                            dma("sp", QTa[:, :nt], lambda c, s=s, p=p, lt=lt: s.AQT[p * 128:(p + 1) * 128, g_of(c, lt) * 128:(g_of(c, lt) + 1) * 128],
                                reads=allt("AQT"), writes=[kq], percore=True)
                        else:
                            dma("sp", KTb[:, 0:kcols], s.AKT[p * 128:(p + 1) * 128, k_lo * 128:k_lo * 128 + kcols],
                                reads=[("AKT", s.name, i) for i in range(k_lo, gk + 1)], writes=[kq])
                            dma("sp", QTa[:, :nt], s.AQT[p * 128:(p + 1) * 128, r0:r0 + nt], reads=[("AQT", s.name, t)], writes=[kq])
                        for hh in range(2):
                            h = 2 * p + hh
                            pb = slice(64 * hh, 64 * hh + 64)
                            sc = FA if h % 2 == 0 else FB
                            sck = ["FA0", "FA1"] if h % 2 == 0 else ["FB"]
                            Pa = Pas[h % 2]
                            pak = "Pa%d" % (h % 2)
                            for j in range(nk):
                                kn = 128 if j < nk - 1 else nt
                                op("pe", lambda e: e.matmul(sc[:kn, j * 128: j * 128 + nt], lhsT=KTb[pb, j * 128: j * 128 + kn],
                                                            rhs=QTa[pb, :nt], start=True, stop=False), [kq], sck)
                                op("pe", lambda e: e.matmul(sc[:kn, j * 128: j * 128 + nt], lhsT=idb[:kn, :kn],
                                                            rhs=bias[:kn, h, j0 + j, :nt], start=False, stop=True),
                                   ["idb", "biasp", "biass"], sck)
                            if split:
                                for j in range(nk):
                                    op("act", lambda e: e.activation(out=Pa[:, j, :], in_=sc[:, j * 128:(j + 1) * 128], func=AF.Exp,
                                                                     bias=padb[:, lt * 5 + j: lt * 5 + j + 1]), sck + ["padb"], [pak])
                            elif nt == 128:
                                op("act", lambda e: e.activation(out=Pa[:, 0:nk, :].rearrange("p j q -> p (j q)"), in_=sc[:, 0:nk * 128], func=AF.Exp),
                                   sck, [pak])
                            else:
                                if nk > 1:
                                    op("act", lambda e: e.activation(out=Pa[:, 0:nk - 1, :nt],
                                                                     in_=sc[:, 0:(nk - 1) * 128].rearrange("p (j q) -> p j q", q=128)[:, :, :nt],
                                                                     func=AF.Exp), sck, [pak])
                                op("act", lambda e: e.activation(out=Pa[:nt, nk - 1, :nt], in_=sc[:nt, (nk - 1) * 128:(nk - 1) * 128 + nt], func=AF.Exp),
                                   sck, [pak])
                            if pend[0] is not None:
                                pend[0]()

                            def pv(h=h, Pa=Pa, pak=pak, nk=nk, nt=nt, av1b=av1b, avk=avk):
                                for j in range(nk):
                                    kn = 128 if j < nk - 1 else nt
                                    op("pe", lambda e: e.matmul(FC[:nt, h * 128: h * 128 + 65], lhsT=Pa[:kn, j, :nt],
                                                                rhs=av1b[:kn, j, h * 65:(h + 1) * 65], start=(j == 0), stop=(j == nk - 1)),
                                       [pak, avk], ["FC"])
                            pend[0] = pv
                    pend[0]()
                    op("dve", lambda e: e.reciprocal(out=ra[:nt, :], in_=FC[:nt, :].rearrange("p (h c) -> p h c", c=128)[:, :, 64]), ["FC"], ["ra"])
                    for h in range(8):
                        op("dve", lambda e, h=h: e.scalar_tensor_tensor(out=y[:nt, h * 64:(h + 1) * 64], in0=FC[:nt, h * 128: h * 128 + 64],
                                                                       scalar=ra[:nt, h:h + 1], in1=gat[:nt, h * 64:(h + 1) * 64],
                                                                       op0=ALU.mult, op1=ALU.mult), ["FC", "ra", gak], ["y"])
                    xsrc = s.x_in if l == 0 else s.X1
                    if split:
                        dma("sp", y[:nt, 512:768], lambda c, s=s, lt=lt: s.OB[:, g_of(c, lt) * 256:(g_of(c, lt) + 1) * 256], reads=allt("OB"), writes=["y"], percore=True)
                        dma("sp", y[:nt, 768:1024], s.YC2[lt * 128:(lt + 1) * 128, :], reads=[("YC2", s.name, lt, h) for h in range(4)], writes=["y"])
                        dma("sp", xr[:nt, :], lambda c, s=s, lt=lt: s.X1[:, g_of(c, lt) * 1024:(g_of(c, lt) + 1) * 1024], reads=allt("X1"), writes=["xr"], percore=True)
                    else:
                        dma("sp", y[:nt, 512:768], s.OB[:nt, t * 256:(t + 1) * 256], reads=[("OB", s.name, t)], writes=["y"])
                        dma("sp", y[:nt, 768:1024], s.YC[r0:r0 + nt, :], reads=[("YC", s.name, t, h) for h in range(4)], writes=["y"])
                        dma("sp", xr[:nt, :], s.x_in[r0:r0 + nt, :] if l == 0 else s.X1[:nt, t * 1024:(t + 1) * 1024],
                            reads=[("X1", s.name, t)] if l else [], writes=["xr"])
                    for k in range(8):
                        op("pe", lambda e, k=k: e.transpose(T0[:, k, :nt], y[:nt, k * 128:(k + 1) * 128], idb[:nt, :nt]), ["y", "idb"], ["T0"])
                    evac(yT[:, :, :nt], T0[:, :, :nt], ["T0"], ["yT"])
                    for cb in range(2):
                        for k in range(8):
                            op("pe", lambda e, k=k, cb=cb: e.matmul(FA[:nt, cb * 512:(cb + 1) * 512], lhsT=yT[:, k, :nt],
                                                                    rhs=WO[:, k, cb * 512:(cb + 1) * 512], start=(k == 0), stop=(k == 7)),
                               ["yT", "BIG"], ["FA%d" % cb])
                        op("dve", lambda e, cb=cb: e.tensor_tensor(out=xo[:nt, cb * 512:(cb + 1) * 512], in0=FA[:nt, cb * 512:(cb + 1) * 512],
                                                                   in1=xr[:nt, cb * 512:(cb + 1) * 512], op=ALU.add), ["FA%d" % cb, "xr"], ["xr"])
                    if l == 0:
                        dma("sp", s.X1[:nt, t * 1024:(t + 1) * 1024], xo[:nt, :], reads=["xr"], writes=[("X1", s.name, t)])
                    else:
                        op("pool", lambda e: e.tensor_tensor(out=sq[:nt, :], in0=xo[:nt, :], in1=xo[:nt, :], op=ALU.mult), ["xr"], ["sq"])
                        op("dve", lambda e: e.reduce_sum(out=st[:nt, 1:2], in_=sq[:nt, :], axis=AX.X), ["sq"], ["st1"])
                        op("dve", lambda e: e.tensor_scalar(out=st[:nt, 1:2], in0=st[:nt, 1:2], scalar1=1.0 / 1024, scalar2=1e-6,
                                                            op0=ALU.mult, op1=ALU.add), ["st1"], ["st1"])
                        op("act", lambda e: e.activation(out=st[:nt, 1:2], in_=st[:nt, 1:2], func=AF.Ln), ["st1"], ["st1"])
                        op("act", lambda e: e.activation(out=st[:nt, 1:2], in_=st[:nt, 1:2], func=AF.Exp, scale=-0.5), ["st1"], ["st1"])
                        op("dve", lambda e: e.scalar_tensor_tensor(out=xo[:nt, :], in0=xo[:nt, :], scalar=st[:nt, 1:2], in1=fing[:nt, :],
                                                                   op0=ALU.mult, op1=ALU.mult), ["xr", "st1", "fing"], ["xr"])
                        dst = O["yp"][lt * 128:(lt + 1) * 128, :] if s.prompt else O["ys"][s.si]
                        dma("sp", dst, xo[:nt, :], reads=["xr"], is_output=True)

        tr.finish()
        with nc.Block() as block:
            @block.tensor
            def _(e):
                for f in tr.ops["pe"]:
                    f(e)

            @block.scalar
            def _(e):
                for f in tr.ops["act"]:
                    f(e)

            @block.vector
            def _(e):
                for f in tr.ops["dve"]:
                    f(e)

            @block.gpsimd
            def _(e):
                for f in tr.ops["pool"]:
                    f(e)

            @block.sync
            def _(e):
                ops_sp = tr.ops["sp"]
                i_ = 0
                while i_ < len(ops_sp):
                    if getattr(ops_sp[i_], "percore", False):
                        j_ = i_
                        while j_ < len(ops_sp) and getattr(ops_sp[j_], "percore", False):
                            j_ += 1
                        for arm in e.switch_core_id(n=128):
                            for f in ops_sp[i_:j_]:
                                f(e, arm.logical % 8)
                        i_ = j_
                    else:
                        ops_sp[i_](e)
                        i_ += 1
    return nc


def _host_consts(rel_bias):
    import ml_dtypes
    kk = np.arange(128)[:, None]
    biasp = np.zeros((2, 128, 8, 5, 128), np.float32)
    for j in range(5):
        toff = j - 4
        q = np.arange(128)[None, :]
        dist = q - kk - 128 * toff
        idx = np.clip(dist, -128, 128) + 128
        vals = rel_bias[:, :, idx]
        qc, kc = (q // 64), ((kk + 128 * toff) // 64)
        ok = (kc <= qc) & (kc >= qc - 8)
        vals = np.where(ok[None, None], vals, np.float32(NEG))
        biasp[:, :, :, j, :] = vals.transpose(0, 2, 1, 3)
    biass = np.zeros((2, 128, 8, 5, T_S), np.float32)
    for j in range(5):
        r = j * 128 + kk
        q = np.arange(T_S)[None, :]
        dist = q + AWIN - r
        idx = np.clip(dist, -128, 128) + 128
        vals = rel_bias[:, :, idx]
        biass[:, :, :, j, :] = vals.transpose(0, 2, 1, 3)
    half = 4
    inv = (500000.0 ** (-np.arange(half, dtype=np.float32) * np.float32(2.0 / 8))).astype(np.float32)

    def rope_tab(pos):
        ang = pos.astype(np.float32)[:, None] * inv[None, :]
        c, s_ = np.cos(ang).astype(np.float32), np.sin(ang).astype(np.float32)
        return np.concatenate([np.tile(c, (1, 16)), np.tile(s_, (1, 16))], axis=1).astype(np.float32)
    lmc = np.zeros((2, 128), np.float32); lmc[0, :] = 1.0; lmc[1, 64:] = 1.0
    return dict(lmc=lmc.astype(ml_dtypes.bfloat16), biasp=biasp, biass=biass, ropep=rope_tab(np.arange(S_P)), ropes=rope_tab(PAST + np.arange(T_S)),
                idb=np.eye(128, dtype=np.float32).astype(ml_dtypes.bfloat16), idf=np.eye(128, dtype=np.float32))


def kernel(x_prompt, x_sample, cache_a_k, cache_a_v, state_conv, cache_c_k, cache_c_v,
           norm_g, w_in, w_out, rel_bias, conv_w, lam_q1, lam_k1, lam_q2, lam_k2, subln_g, final_g):
    f = lambda a: np.ascontiguousarray(np.asarray(a, dtype=np.float32))
    x_prompt, x_sample = f(x_prompt), f(x_sample)
    consts = _host_consts(f(rel_bias))
    shared = dict(
        xp=f(x_prompt[0]),
        ng=f(f(norm_g).reshape(2, 8, 128).transpose(0, 2, 1)),
        win=f(w_in), wout=f(w_out),
        convw=f(np.broadcast_to(f(conv_w)[:, :, None, :], (2, 3, 128, 256))),
        lam=f(np.broadcast_to(np.stack([f(lam_q1), f(lam_k1), f(lam_q2), f(lam_k2)], axis=1)[:, :, None, :], (2, 4, 128, 32))),
        subg=f(np.broadcast_to(f(subln_g)[:, None, :], (2, 128, 64))),
        fing=f(np.broadcast_to(f(final_g)[None, :], (128, 1024))),
        **consts,
    )
    import ml_dtypes
    in_maps = []
    blocks_of = []
    for c in range(NCORES):
        blk = [8 * j + (c if j % 2 == 0 else 7 - c) for j in range(NSLOT)]
        blocks_of.append(blk)
        cmask = np.zeros((NSLOT, 32, 2, 512), np.float32)
        padb = np.zeros((128, NSLOT * 20), np.float32)
        for j, b in enumerate(blk):
            for u in range(32):
                kt = 32 * j + u
                for sq_ in range(4):
                    gq = 4 * b + sq_
                    cols = slice(sq_ * 128, (sq_ + 1) * 128)
                    if kt > gq:
                        cmask[j, u, 0, cols] = NEG
                    elif kt == gq:
                        cmask[j, u, 1, sq_ * 128: sq_ * 128 + 64] = NEG
            for qt in range(4):
                g = 4 * b + qt
                for jj in range(5):
                    if g + jj - 4 < 0:
                        padb[:, (j * 4 + qt) * 5 + jj] = NEG
        shared_c = dict(cmask=cmask.astype(ml_dtypes.bfloat16), padb=padb)
        sl = slice(NSEQ * c, NSEQ * (c + 1))
        m = dict(shared)
        m.update(shared_c)
        m.update(
            xs=f(x_sample[sl]),
            cak=f(f(cache_a_k)[:, sl].reshape(2, NSEQ, AWIN, 512)), cav=f(f(cache_a_v)[:, sl].reshape(2, NSEQ, AWIN, 512)),
            scv=f(f(state_conv)[:, sl]),
            cck=f(f(cache_c_k)[:, sl].reshape(2, NSEQ, PAST, 256)), ccv=f(f(cache_c_v)[:, sl].reshape(2, NSEQ, PAST, 256)),
        )
        in_maps.append(m)
    nc = build_nc()
    res = run_bass_kernel_spmd(nc, in_maps, core_ids=list(range(NCORES)))
    R = res.results
    cat = lambda k, ax: np.concatenate([R[c][k] for c in range(NCORES)], axis=ax)
    r0 = R[0]
    yp_full = np.zeros((1, S_P, 1024), np.float32)
    for c in range(NCORES):
        for j, b in enumerate(blocks_of[c]):
            yp_full[0, b * 512:(b + 1) * 512] = R[c]["yp"][j * 512:(j + 1) * 512]
    return (
        yp_full,
        cat("ys", 0).reshape(16, T_S, 1024).astype(np.float32),
        r0["pak"].reshape(2, 1, 512, 8, 64), r0["pav"].reshape(2, 1, 512, 8, 64),
        r0["pconv"].reshape(2, 1, 2, 256),
        r0["pck"].reshape(2, 1, S_P, 4, 2, 32), r0["pcv"].reshape(2, 1, S_P, 4, 64),
        cat("sak", 1).reshape(2, 16, T_S, 8, 64), cat("sav", 1).reshape(2, 16, T_S, 8, 64),
        cat("sconv", 1).reshape(2, 16, 2, 256),
        cat("sck", 1).reshape(2, 16, T_S, 4, 2, 32), cat("scv2", 1).reshape(2, 16, T_S, 4, 64),
    )
```

```python
import math
from contextlib import ExitStack
import numpy as np
import concourse.bass as bass
import concourse.mybir as mybir
from concourse.bass_utils import run_bass_kernel_spmd

F32 = mybir.dt.float32
BF16 = mybir.dt.bfloat16
AF = mybir.ActivationFunctionType
ALU = mybir.AluOpType
AX = mybir.AxisListType

S_P = 16384
NT_P = 128
T_S = 16
PAST = 4096
AWIN = 512
NEG = -30000.0
C_SCALE = 32 ** -0.5
NCORES = 8
NSEQ = 2
NSLOT = 4


class _Rec:
    def __init__(self):
        self.calls = []

    def __getattr__(self, name):
        def call(*a, **k):
            self.calls.append((name, a, k))
            return self
        return call


class Tracker:
    def __init__(self, nc, es):
        self.nc = nc
        self.ops = {e: [] for e in ("pe", "act", "dve", "pool", "sp")}
        self.csem = {e: es.enter_context(nc.semaphore("c_" + e)) for e in ("pe", "act", "dve", "pool")}
        self.ccnt = {e: 0 for e in self.csem}
        self.dsem = {q: [es.enter_context(nc.semaphore("d_%s%d" % (q, i))) for i in range(8)] for q in ("sp", "pool")}
        self.dcnt = {q: 0 for q in self.dsem}
        self.seen = {e: {} for e in self.ops}
        self.lastw = {}
        self.readers = {}
        self.out_tokens = []

    def _deps(self, reads, writes):
        deps = []
        for k in list(reads) + list(writes):
            if k in self.lastw:
                deps.append(self.lastw[k])
        for k in writes:
            deps.extend(self.readers.get(k, []))
        return deps

    def _update(self, tok, reads, writes):
        for k in writes:
            self.lastw[k] = tok
            self.readers[k] = []
        for k in reads:
            self.readers.setdefault(k, []).append(tok)

    def _waits(self, eng, deps, skip_self_sem=None):
        need = {}
        for (sem, val, seng) in deps:
            if skip_self_sem is not None and seng == skip_self_sem:
                continue
            key = id(sem)
            if self.seen[eng].get(key, 0) >= val:
                continue
            if key not in need or need[key][1] < val:
                need[key] = (sem, val)
        for key, (sem, val) in need.items():
            self.seen[eng][key] = val
        return list(need.values())

    def op(self, eng, fn, reads=(), writes=()):
        deps = self._deps(reads, writes)
        waits = self._waits(eng, deps, skip_self_sem=("pe" if eng == "pe" else None))
        self.ccnt[eng] += 1
        sem, val = self.csem[eng], self.ccnt[eng]

        rec = _Rec()
        fn(rec)
        name, a, k = rec.calls[0]

        def emit(e, waits=waits, name=name, a=a, k=k, sem=sem):
            for (s, v) in waits:
                e.wait_ge(s, v)
            getattr(e, name)(*a, **k).then_inc(sem, 1)
        self.ops[eng].append(emit)
        self._update((sem, val, eng), reads, writes)

    def dma(self, q, out, in_, reads=(), writes=(), is_output=False, percore=False):
        deps = self._deps(reads, writes)
        i = self.dcnt[q]
        self.dcnt[q] += 1
        sem = self.dsem[q][i % 8]
        val = (i // 8 + 1) * 16
        if i >= 8:
            deps.append((sem, val - 16, "dq"))
        waits = self._waits(q, deps)

        def emit(e, c=None, waits=waits, sem=sem, out=out, in_=in_):
            for (s, v) in waits:
                e.wait_ge(s, v)
            i_ = in_(c) if callable(in_) else in_
            e.dma_start(out=out, in_=i_).then_inc(sem, 16)
        emit.percore = percore
        self.ops[q].append(emit)
        tok = (sem, val, "dq")
        self._update(tok, reads, writes)
        if is_output:
            self.out_tokens.append(tok)

    def finish(self):
        for q in ("sp", "pool"):
            toks = []
            n = self.dcnt[q]
            for s in range(min(8, n)):
                last_i = ((n - 1 - s) // 8) * 8 + s
                toks.append((self.dsem[q][s], (last_i // 8 + 1) * 16, "dq"))
            if q == "sp":
                toks = toks + self.out_tokens
            waits = self._waits(q, toks)

            def emit(e, waits=waits):
                for (s, v) in waits:
                    e.wait_ge(s, v)
            self.ops[q].append(emit)


def build_nc():
    nc = bass.Bass("TRN2", target_bir_lowering=False)

    def din(name, shape, dt=F32):
        return nc.dram_tensor(name, list(shape), dt, kind="ExternalInput").ap()

    def dout(name, shape):
        return nc.dram_tensor(name, list(shape), F32, kind="ExternalOutput").ap()

    def dscr(name, shape, dt):
        return nc.dram_tensor(name, list(shape), dt).ap()

    I = dict(
        xp=din("xp", [S_P, 1024]), xs=din("xs", [NSEQ, T_S, 1024]),
        cak=din("cak", [2, NSEQ, AWIN, 512]), cav=din("cav", [2, NSEQ, AWIN, 512]),
        scv=din("scv", [2, NSEQ, 2, 256]),
        cck=din("cck", [2, NSEQ, PAST, 256]), ccv=din("ccv", [2, NSEQ, PAST, 256]),
        ng=din("ng", [2, 128, 8]), win=din("win", [2, 1024, 4096]), wout=din("wout", [2, 1024, 1024]),
        biasp=din("biasp", [2, 128, 8, 5, 128]), biass=din("biass", [2, 128, 8, 5, T_S]),
        convw=din("convw", [2, 3, 128, 256]),
        lam=din("lam", [2, 4, 128, 32]), subg=din("subg", [2, 128, 64]), fing=din("fing", [128, 1024]),
        ropep=din("ropep", [S_P, 128]), ropes=din("ropes", [T_S, 128]),
        idb=din("idb", [128, 128], BF16), idf=din("idf", [128, 128]),
        cmask=din("cmask", [NSLOT, 32, 2, 512], BF16),
        padb=din("padb", [128, NSLOT * 4 * 5]), lmc=din("lmc", [2, 128], BF16),
    )
    O = dict(
        yp=dout("yp", [NSLOT * 512, 1024]), ys=dout("ys", [NSEQ, T_S, 1024]),
        pak=dout("pak", [2, 512, 512]), pav=dout("pav", [2, 512, 512]), pcv_=dout("pconv", [2, 2, 256]),
        pck=dout("pck", [2, S_P, 256]), pcv=dout("pcv", [2, S_P, 256]),
        sak=dout("sak", [2, NSEQ, T_S, 512]), sav=dout("sav", [2, NSEQ, T_S, 512]),
        scvo=dout("sconv", [2, NSEQ, 2, 256]),
        sck=dout("sck", [2, NSEQ, T_S, 256]), scv2=dout("scv2", [2, NSEQ, T_S, 256]),
    )

    class Seq:
        pass

    seqs = []
    for si in range(1 + NSEQ):
        s = Seq()
        s.name = "p" if si == 0 else "s%d" % (si - 1)
        s.prompt = si == 0
        s.T = S_P if s.prompt else T_S
        s.nt = 128 if s.prompt else T_S
        s.ntiles = NT_P if s.prompt else 1
        s.ca_tiles = 4
        s.cc_tiles = 0 if s.prompt else PAST // 128
        s.KA = s.ca_tiles * 128 + s.T
        s.KC = s.cc_tiles * 128 + s.T
        s.x_in = I["xp"] if s.prompt else I["xs"][si - 1]
        s.rope = I["ropep"] if s.prompt else I["ropes"]
        s.si = si - 1
        n = s.name
        s.X1 = dscr("X1" + n, [s.nt, s.ntiles * 1024], F32)
        s.U = dscr("U" + n, [s.T + 2, 256], F32)
        s.AQT = dscr("AQT" + n, [512, s.T], BF16)
        s.AKT = dscr("AKT" + n, [512, s.KA], BF16)
        s.AV1 = dscr("AV1" + n, [128, (s.ca_tiles + s.ntiles) * 520], BF16)
        s.CQT = dscr("CQT" + n, [256, s.T], BF16)
        s.CKT = dscr("CKT" + n, [256, s.KC], BF16)
        s.CV1 = dscr("CV1" + n, [s.KC, 260], BF16)
        s.GA = dscr("GA" + n, [s.nt, s.ntiles * 512], BF16)
        s.GC = dscr("GC" + n, [s.nt, s.ntiles * 256], BF16)
        s.OB = dscr("OB" + n, [s.nt, s.ntiles * 256], BF16)
        s.YC = dscr("YC" + n, [s.T, 256], BF16)
        s.YC2 = dscr("YC2" + n, [NSLOT * 512, 256], BF16)
        seqs.append(s)

    es = ExitStack()
    with es:
        def sb(name, shape, dt=F32):
            return es.enter_context(nc.sbuf_tensor("sb_" + name, list(shape), dt))

        def ps(name, shape, dt=F32):
            return es.enter_context(nc.psum_tensor("ps_" + name, list(shape), dt))

        BIG = sb("BIG", [128, 33152], BF16)
        zbuf = [sb("z0", [128, 4096]), sb("z1", [128, 4096])]
        z = zbuf[0]
        ztile = [0]
        xts = [sb("xt%d" % i, [128, 1024]) for i in range(2)]; xn = sb("xn", [128, 1024], BF16)
        hT = sb("hT", [128, 8, 128], BF16)
        st = sb("st", [128, 8]); rs = sb("rs", [128, 4]); gcol = sb("gcol", [128, 16])
        ropes = [sb("rope%d" % i, [128, 128]) for i in range(2)]; rt = sb("rt", [128, 4, 64])
        stg = sb("stg", [128, 1536], BF16); tT = sb("tT", [128, 8, 128], BF16)
        av1 = sb("av1", [128, 8, 65], BF16); cv1 = sb("cv1", [128, 4, 65], BF16)
        ga = sb("ga", [128, 512], BF16); gc = sb("gc", [128, 256], BF16)
        u = sb("u", [128, 256]); um1 = sb("um1", [128, 256]); um2 = sb("um2", [128, 256]); cvt = sb("cvt", [128, 256])
        sg = sb("sg", [128, 256]); sgt = sb("sgt", [128, 1024]); ob = sb("ob", [128, 256], BF16)
        cw = sb("cw", [128, 3, 256]); zero2 = sb("zero2", [2, 256])
        lamt = sb("lamt", [128, 4, 32]); lamj = sb("lamj", [128, 32]); lamv = sb("lamv", [128, 8])
        subg = sb("subg", [128, 64]); fing = sb("fing", [128, 1024])
        idb = sb("idb", [128, 128], BF16); idf = sb("idf", [128, 128])
        Qz = [[sb("Qz%d_%d" % (i, hh), [128, 2, 512], BF16) for hh in range(2)] for i in range(2)]
        Pm = [sb("Pm%d" % i, [128, 2, 512], BF16) for i in range(3)]
        oT = sb("oT", [128, 2, 512])
        rr = sb("rr", [128, 16]); o1 = sb("o1", [128, 64]); o2 = sb("o2", [128, 64]); ssq = sb("ssq", [128, 4])
        gct = sb("gct", [128, 256], BF16); yct = sb("yct", [128, 256], BF16)
        KTbs = [sb("KTb%d" % i, [128, 640], BF16) for i in range(2)]; QTas = [sb("QTa%d" % i, [128, 128], BF16) for i in range(2)]
        biasp = sb("biasp", [128, 8, 5, 128], BF16); biass = sb("biass", [128, 8, 5, T_S], BF16)
        Pas = [sb("Pa%d" % i, [128, 5, 128], BF16) for i in range(2)]; av1bs = [sb("av1b%d" % i, [128, 5, 520], BF16) for i in range(2)]
        ra = sb("ra", [128, 8]); gats = [sb("gat%d" % i, [128, 512], BF16) for i in range(2)]
        y = sb("y", [128, 1024], BF16); yT = sb("yT", [128, 8, 128], BF16)
        xr = sb("xr", [128, 1024]); sq = sb("sq", [128, 1024]); xo = xr
        FA = ps("FA", [128, 1024]); FB = ps("FB", [128, 1024]); FC = ps("FC", [128, 1024])
        T0 = ps("T0", [128, 8, 128], BF16); T1 = ps("T1", [128, 8, 128], BF16)

        zt = sb("zt", [128, 640], BF16); padb = sb("padb", [128, NSLOT * 20]); lmc = sb("lmc", [2, 128], BF16)
        rms = [sb("rm%d" % i, [2, 512], BF16) for i in range(4)]
        tr = Tracker(nc, es)
        blk_of = lambda c, j: 8 * j + (c if j % 2 == 0 else 7 - c)
        g_of = lambda c, lt: 4 * blk_of(c, lt // 4) + lt % 4
        op, dma = tr.op, tr.dma
        alt = [0]

        def evac(out, in_, reads, writes):
            alt[0] ^= 1
            if alt[0]:
                op("act", lambda e: e.activation(out=out, in_=in_, func=AF.Copy), reads, writes)
            else:
                op("dve", lambda e: e.tensor_copy(out=out, in_=in_), reads, writes)

        def c_epilogue(s, h, q0, nq, split=False, slot=0):
            evac(oT[0:65, :, :nq], FC[0:65, :].rearrange("p (m q) -> p m q", m=2)[:, :, :nq], ["FC"], ["oT"])
            for qt in range(max(1, nq // 128)):
                qn = min(128, nq)
                tok0 = q0 + qt * 128
                for m in range(2):
                    op("pe", lambda e, m=m: e.transpose(
                        FA[:qn, m * 128: m * 128 + 65], oT[0:65, m, qt * 128: qt * 128 + qn], idf[0:65, 0:65]),
                        ["oT", "idf"], ["FA0"])
                if split:
                    dma("sp", gct[:qn, :], lambda c, s=s, lt_=slot * 4 + qt: s.GC[:, g_of(c, lt_) * 256:(g_of(c, lt_) + 1) * 256],
                        reads=[("GC", s.name, i) for i in range(s.ntiles)], writes=["gct"], percore=True)
                else:
                    dma("sp", gct[:qn, :], s.GC[:qn, (tok0 // s.nt) * 256:(tok0 // s.nt + 1) * 256], reads=[("GC", s.name, tok0 // s.nt)], writes=["gct"])
                op("dve", lambda e: e.reciprocal(out=rr[:qn, 0:1], in_=FA[:qn, 64:65]), ["FA0"], ["rr"])
                op("dve", lambda e: e.reciprocal(out=rr[:qn, 1:2], in_=FA[:qn, 128 + 64:128 + 65]), ["FA0"], ["rr"])
                op("dve", lambda e: e.tensor_tensor(out=rr[:qn, 1:2], in0=rr[:qn, 1:2], in1=lamv[:qn, 2:3], op=ALU.mult),
                   ["rr", "lamv"], ["rr"])
                op("dve", lambda e: e.tensor_scalar(out=o1[:qn, :], in0=FA[:qn, 0:64], scalar1=rr[:qn, 0:1], scalar2=None,
                                                    op0=ALU.mult), ["FA0", "rr"], ["o1"])
                op("dve", lambda e: e.scalar_tensor_tensor(out=o1[:qn, :], in0=FA[:qn, 128:192], scalar=rr[:qn, 1:2],
                                                           in1=o1[:qn, :], op0=ALU.mult, op1=ALU.add),
                   ["FA0", "rr", "o1"], ["o1"])
                op("dve", lambda e: e.tensor_tensor(out=o2[:qn, :], in0=o1[:qn, :], in1=o1[:qn, :], op=ALU.mult), ["o1"], ["o2"])
                op("dve", lambda e: e.reduce_sum(out=ssq[:qn, 0:1], in_=o2[:qn, :], axis=AX.X), ["o2"], ["ssq"])
                op("dve", lambda e: e.tensor_scalar(out=ssq[:qn, 0:1], in0=ssq[:qn, 0:1], scalar1=1.0 / 64, scalar2=1e-5,
                                                    op0=ALU.mult, op1=ALU.add), ["ssq"], ["ssq"])
                op("act", lambda e: e.activation(out=ssq[:qn, 0:1], in_=ssq[:qn, 0:1], func=AF.Ln), ["ssq"], ["ssq"])
                op("act", lambda e: e.activation(out=ssq[:qn, 0:1], in_=ssq[:qn, 0:1], func=AF.Exp, scale=-0.5), ["ssq"], ["ssq"])
                op("dve", lambda e: e.scalar_tensor_tensor(out=o1[:qn, :], in0=o1[:qn, :], scalar=ssq[:qn, 0:1],
                                                           in1=subg[:qn, :], op0=ALU.mult, op1=ALU.mult),
                   ["o1", "ssq", "subg"], ["o1"])
                op("dve", lambda e: e.tensor_tensor(out=yct[:qn, 0:64], in0=o1[:qn, :], in1=gct[:qn, h * 64:(h + 1) * 64],
                                                    op=ALU.mult), ["o1", "gct"], ["yct"])
                dma("sp", (s.YC2 if split else s.YC)[tok0:tok0 + qn, h * 64:(h + 1) * 64], yct[:qn, 0:64], reads=["yct"],
                    writes=[("YC2" if split else "YC", s.name, tok0 // s.nt, h)])

        dma("sp", idb[:], I["idb"][:, :], writes=["idb"])
        dma("sp", idf[:], I["idf"][:, :], writes=["idf"])
        dma("sp", fing[:], I["fing"][:, :], writes=["fing"])
        op("pool", lambda e: e.memset(zero2[:], 0.0), writes=["zero2"])
        op("pool", lambda e: e.memset(zt[:], 0.0), writes=["zt"])
        dma("sp", padb[:], I["padb"][:, :], writes=["padb"])
        dma("sp", lmc[:], I["lmc"][:, :], writes=["lmc"])
        op("pool", lambda e: e.memset(av1[:], 1.0), writes=["av1"])
        op("pool", lambda e: e.memset(cv1[:], 1.0), writes=["cv1"])
        for i in range(2):
            for hh in range(2):
                op("pool", lambda e: e.memset(Qz[i][hh][:], 0.0), writes=["Qz%d" % i])
        op("pool", lambda e: e.memset(BIG[:, 33024:33152], 0.0), writes=["BIG"])

        for l in range(2):
            lam_init = 0.8 - 0.6 * math.exp(-0.3 * l)
            dma("sp", gcol[:, 0:8], I["ng"][l], writes=["gcol"])
            for k in range(8):
                for hf in range(2):
                    zk = "z%d" % hf
                    dma("sp", zbuf[0][:, hf * 2048:(hf + 1) * 2048], I["win"][l, k * 128:(k + 1) * 128, hf * 2048:(hf + 1) * 2048],
                        writes=["zb0c%d" % i for i in range(hf * 4, hf * 4 + 4)])
                    dst = BIG[:, k * 4096 + hf * 2048: k * 4096 + (hf + 1) * 2048]
                    if hf:
                        op("dve", lambda e, dst=dst, hf=hf, k=k: e.tensor_scalar(out=dst, in0=zbuf[0][:, hf * 2048:(hf + 1) * 2048],
                                                                                 scalar1=gcol[:, k:k + 1], scalar2=None, op0=ALU.mult),
                           reads=["zb0c%d" % i for i in range(hf * 4, hf * 4 + 4)] + ["gcol"], writes=["BIG"])
                    else:
                        op("act", lambda e, dst=dst, hf=hf, k=k: e.activation(out=dst, in_=zbuf[0][:, hf * 2048:(hf + 1) * 2048], func=AF.Copy,
                                                                              scale=gcol[:, k:k + 1]),
                           reads=["zb0c%d" % i for i in range(hf * 4, hf * 4 + 4)] + ["gcol"], writes=["BIG"])
            dma("pool", biasp[:].rearrange("p a b c -> p (a b c)"), I["biasp"][l].rearrange("p a b c -> p (a b c)"), writes=["biasp"])
            dma("pool", biass[:].rearrange("p a b c -> p (a b c)"), I["biass"][l].rearrange("p a b c -> p (a b c)"), writes=["biass"])
            dma("sp", cw[:], I["convw"][l].rearrange("j p c -> p j c"), writes=["cw"])
            dma("sp", lamt[:], I["lam"][l].rearrange("j p c -> p j c"), writes=["lamt"])
            dma("sp", subg[:], I["subg"][l], writes=["subg"])
            op("dve", lambda e: e.tensor_scalar(out=subg[:], in0=subg[:], scalar1=float(1.0 - lam_init), scalar2=None, op0=ALU.mult),
               reads=["subg"], writes=["subg"])
            for j in range(2):
                op("dve", lambda e, j=j: e.tensor_tensor(out=lamj[:], in0=lamt[:, 2 * j, :], in1=lamt[:, 2 * j + 1, :], op=ALU.mult),
                   reads=["lamt"], writes=["lamj"])
                op("dve", lambda e, j=j: e.reduce_sum(out=lamv[:, j:j + 1], in_=lamj[:], axis=AX.X), reads=["lamj"], writes=["lamv"])
            op("act", lambda e: e.activation(out=lamv[:, 0:2], in_=lamv[:, 0:2], func=AF.Exp), reads=["lamv"], writes=["lamv"])
            op("dve", lambda e: e.tensor_tensor(out=lamv[:, 2:3], in0=lamv[:, 1:2], in1=lamv[:, 0:1], op=ALU.subtract),
               reads=["lamv"], writes=["lamv"])
            op("dve", lambda e: e.tensor_scalar(out=lamv[:, 2:3], in0=lamv[:, 2:3], scalar1=float(-lam_init), scalar2=None, op0=ALU.add),
               reads=["lamv"], writes=["lamv"])

            for s in seqs:
                nt = s.nt
                if s.prompt:
                    dma("sp", s.U[0:2, :], zero2[:], reads=["zero2"], writes=[("U", s.name, -1)])
                    if l == 0:
                        for p4 in range(4):
                            dma("sp", s.AKT[p4 * 128:(p4 + 1) * 128, 0:512], zt[:, 0:512], reads=["zt"], writes=[("AKT", s.name, i) for i in range(4)])
                            dma("sp", s.AV1[:, p4 * 520:(p4 + 1) * 520], zt[:, 0:520], reads=["zt"], writes=[("AV1", s.name, p4)])
                else:
                    dma("sp", um1[0:2, :], I["scv"][l, s.si], writes=["um1"])
                    dma("sp", s.U[0:2, :], um1[0:2, :], reads=["um1"], writes=[("U", s.name, -1)])
                    for kt in range(s.cc_tiles):
                        dma("pool", stg[:, 0:256], I["cck"][l, s.si, kt * 128:(kt + 1) * 128, :], writes=["stg"])
                        for b in range(2):
                            op("pe", lambda e, b=b: e.transpose(T1[:, b, :], stg[:, b * 128:(b + 1) * 128], idb[:]),
                               reads=["stg", "idb"], writes=["T1"])
                        evac(tT[:, 0:2, :], T1[:, 0:2, :], ["T1"], ["tT"])
                        dma("sp", s.CKT.rearrange("(b p) t -> p b t", p=128)[:, :, kt * 128:(kt + 1) * 128], tT[:, 0:2, :],
                            reads=["tT"], writes=[("CKT", s.name, kt)])
                        dma("pool", cv1[:, :, 0:64], I["ccv"][l, s.si, kt * 128:(kt + 1) * 128, :].rearrange("t (h e) -> t h e", e=64),
                            writes=["cv1"])
                        dma("sp", s.CV1[kt * 128:(kt + 1) * 128, :], cv1[:].rearrange("p h e -> p (h e)"), reads=["cv1"],
                            writes=[("CV1", s.name, kt)])
                    for kt in range(0 if s.prompt else s.ca_tiles):
                        dma("pool", stg[:, 0:512], I["cak"][l, s.si, kt * 128:(kt + 1) * 128, :], writes=["stg"])
                        for b in range(4):
                            op("pe", lambda e, b=b: e.transpose(T1[:, b, :], stg[:, b * 128:(b + 1) * 128], idb[:]),
                               reads=["stg", "idb"], writes=["T1"])
                        evac(tT[:, 0:4, :], T1[:, 0:4, :], ["T1"], ["tT"])
                        dma("sp", s.AKT.rearrange("(b p) t -> p b t", p=128)[:, :, kt * 128:(kt + 1) * 128], tT[:, 0:4, :],
                            reads=["tT"], writes=[("AKT", s.name, kt)])
                        dma("pool", av1[:, :, 0:64], I["cav"][l, s.si, kt * 128:(kt + 1) * 128, :].rearrange("t (h e) -> t h e", e=64),
                            writes=["av1"])
                        dma("sp", s.AV1[:, kt * 520:(kt + 1) * 520], av1[:].rearrange("p h e -> p (h e)"), reads=["av1"],
                            writes=[("AV1", s.name, kt)])

                zbase = ztile[0]

                def frontA(t):
                    r0 = t * nt
                    ka0 = s.ca_tiles * 128 + r0
                    kc0 = s.cc_tiles * 128 + r0
                    zi = (zbase + t) % 2
                    z = zbuf[zi]
                    zc0, zc1, zc2, zc3, zc4, zc5, zc6, zc7 = ["zb%dc%d" % (zi, i) for i in range(8)]
                    xt, rope = xts[zi], ropes[zi]
                    xtk, ropek = "xt%d" % zi, "rope%d" % zi
                    dma("pool", xt[:nt, :], s.x_in[r0:r0 + nt, :] if l == 0 else s.X1[:nt, t * 1024:(t + 1) * 1024],
                        reads=[("X1", s.name, t)] if l else [], writes=[xtk])
                    dma("pool", rope[:nt, :], s.rope[r0:r0 + nt, :], writes=[ropek])
                    op("act", lambda e: e.activation(out=xn[:nt, :], in_=xt[:nt, :], func=AF.Copy), [xtk], ["xn"])
                    for k in range(8):
                        op("pe", lambda e, k=k: e.transpose(T0[:, k, :nt], xn[:nt, k * 128:(k + 1) * 128], idb[:nt, :nt]),
                           ["xn", "idb"], ["T0"])
                    op("act", lambda e: e.activation(out=hT[:, :, :nt], in_=T0[:, :, :nt], func=AF.Copy), ["T0"], ["hT"])

                def frontB(t, mid=None):
                    r0 = t * nt
                    ka0 = s.ca_tiles * 128 + r0
                    kc0 = s.cc_tiles * 128 + r0
                    zi = (zbase + t) % 2
                    z = zbuf[zi]
                    zc0, zc1, zc2, zc3, zc4, zc5, zc6, zc7 = ["zb%dc%d" % (zi, i) for i in range(8)]
                    xt, rope = xts[zi], ropes[zi]
                    xtk, ropek = "xt%d" % zi, "rope%d" % zi
                    for cb in range(8):
                        pk = "FA%d" % (cb % 2)
                        pt = FA[:nt, (cb % 2) * 512:(cb % 2 + 1) * 512]
                        for k in range(8):
                            op("pe", lambda e, k=k, pt=pt, cb=cb: e.matmul(pt, lhsT=hT[:, k, :nt],
                                                                           rhs=BIG[:, k * 4096 + cb * 512: k * 4096 + (cb + 1) * 512],
                                                                           start=(k == 0), stop=(k == 7)),
                               ["hT", "BIG"], [pk])
                        op("act", lambda e: e.activation(out=z[:nt, cb * 512:(cb + 1) * 512], in_=pt, func=AF.Copy),
                           [pk], ["zb%dc%d" % (zi, cb)])
                        if cb == 3 and mid is not None:
                            mid()

                def epiA(t):
                    r0 = t * nt
                    ka0 = s.ca_tiles * 128 + r0
                    kc0 = s.cc_tiles * 128 + r0
                    zi = (zbase + t) % 2
                    z = zbuf[zi]
                    zc0, zc1, zc2, zc3, zc4, zc5, zc6, zc7 = ["zb%dc%d" % (zi, i) for i in range(8)]
                    xt, rope = xts[zi], ropes[zi]
                    xtk, ropek = "xt%d" % zi, "rope%d" % zi
                    rsk = "rs%d" % zi
                    op("dve", lambda e: e.tensor_tensor(out=sq[:nt, :], in0=xt[:nt, :], in1=xt[:nt, :], op=ALU.mult), [xtk], ["sq"])
                    op("dve", lambda e: e.reduce_sum(out=rs[:nt, zi:zi + 1], in_=sq[:nt, :], axis=AX.X), ["sq"], [rsk])
                    op("dve", lambda e: e.tensor_scalar(out=rs[:nt, zi:zi + 1], in0=rs[:nt, zi:zi + 1], scalar1=1.0 / 1024, scalar2=1e-6,
                                                         op0=ALU.mult, op1=ALU.add), [rsk], [rsk])
                    op("act", lambda e: e.activation(out=rs[:nt, zi:zi + 1], in_=rs[:nt, zi:zi + 1], func=AF.Ln), [rsk], [rsk])
                    op("act", lambda e: e.activation(out=rs[:nt, zi:zi + 1], in_=rs[:nt, zi:zi + 1], func=AF.Exp, scale=-0.5), [rsk], [rsk])
                    op("dve", lambda e: e.tensor_scalar(out=rs[:nt, 2 + zi:3 + zi], in0=rs[:nt, zi:zi + 1], scalar1=-1.0, scalar2=None, op0=ALU.mult),
                       [rsk], [rsk + "n"])
                    op("act", lambda e: e.activation(out=sgt[:nt, 0:512], in_=z[:nt, 1536:2048], func=AF.Exp, scale=rs[:nt, 2 + zi:3 + zi]), [zc3, rsk + "n"], ["sgtA"])
                    op("act", lambda e: e.activation(out=sgt[:nt, 512:768], in_=z[:nt, 3840:4096], func=AF.Exp, scale=rs[:nt, 2 + zi:3 + zi]), [zc7, rsk + "n"], ["sgtC"])
                    op("act", lambda e: e.activation(out=sgt[:nt, 768:1024], in_=z[:nt, 2816:3072], func=AF.Exp, scale=rs[:nt, 2 + zi:3 + zi]), [zc5, rsk + "n"], ["sgtB"])
                    for cb in range(8):
                        op("dve", lambda e, cb=cb: e.tensor_scalar(out=z[:nt, cb * 512:(cb + 1) * 512], in0=z[:nt, cb * 512:(cb + 1) * 512],
                                                                   scalar1=rs[:nt, zi:zi + 1], scalar2=None, op0=ALU.mult),
                           ["zb%dc%d" % (zi, cb), rsk], ["zb%dc%d" % (zi, cb)])
                    zz = z[:nt, 3072:3584].rearrange("p (g d) -> p g d", d=32)
                    x1, x2 = zz[:, :, 0:4], zz[:, :, 4:8]
                    cs = rope[:nt, 0:64].rearrange("p (g d) -> p g d", d=4)
                    sn = rope[:nt, 64:128].rearrange("p (g d) -> p g d", d=4)
                    rv = [rt[:nt, i, :].rearrange("p (g d) -> p g d", d=4) for i in range(4)]
                    op("dve", lambda e: e.tensor_tensor(out=rv[0], in0=x1, in1=cs, op=ALU.mult), [zc6, ropek], ["rt0"])
                    op("dve", lambda e: e.tensor_tensor(out=rv[1], in0=x2, in1=sn, op=ALU.mult), [zc6, ropek], ["rt1"])
                    op("dve", lambda e: e.tensor_tensor(out=rv[2], in0=x2, in1=cs, op=ALU.mult), [zc6, ropek], ["rt2"])
                    op("dve", lambda e: e.tensor_tensor(out=rv[3], in0=x1, in1=sn, op=ALU.mult), [zc6, ropek], ["rt3"])
                    op("dve", lambda e: e.tensor_tensor(out=x1, in0=rv[0], in1=rv[1], op=ALU.subtract), ["rt0", "rt1", "rt3"], [zc6])
                    op("dve", lambda e: e.tensor_tensor(out=x2, in0=rv[2], in1=rv[3], op=ALU.add), ["rt2", "rt3"], [zc6])
                    if s.prompt:
                        dma("sp", O["pck"][l, r0:r0 + nt, :], z[:nt, 3328:3584], reads=[zc6], is_output=True)
                        dma("sp", O["pcv"][l, r0:r0 + nt, :], z[:nt, 3584:3840], reads=[zc7], is_output=True)
                        if t >= NT_P - 4:
                            rr0 = (t - (NT_P - 4)) * 128
                            dma("sp", O["pak"][l, rr0:rr0 + 128, :], z[:nt, 512:1024], reads=[zc1], is_output=True)
                            dma("sp", O["pav"][l, rr0:rr0 + 128, :], z[:nt, 1024:1536], reads=[zc2], is_output=True)
                    else:
                        dma("sp", O["sck"][l, s.si], z[:nt, 3328:3584], reads=[zc6], is_output=True)
                        dma("sp", O["scv2"][l, s.si], z[:nt, 3584:3840], reads=[zc7], is_output=True)
                        dma("sp", O["sak"][l, s.si], z[:nt, 512:1024], reads=[zc1], is_output=True)
                        dma("sp", O["sav"][l, s.si], z[:nt, 1024:1536], reads=[zc2], is_output=True)
                    op("dve", lambda e: e.tensor_scalar(out=stg[:nt, 0:512], in0=z[:nt, 0:512], scalar1=0.125, scalar2=None, op0=ALU.mult), [zc0], ["stg"])
                    op("dve", lambda e: e.tensor_copy(out=stg[:nt, 512:1024], in_=z[:nt, 512:1024]), [zc1], ["stg"])
                    op("dve", lambda e: e.tensor_copy(out=stg[:nt, 1024:1536], in_=z[:nt, 3072:3584]), [zc6], ["stg"])
                    op("dve", lambda e: e.tensor_tensor(out=u[:nt, :], in0=z[:nt, 2304:2560], in1=z[:nt, 2560:2816], op=ALU.mult),
                       [zc4, zc5], ["u"])
                    dma("sp", s.U[2 + r0:2 + r0 + nt, :], u[:nt, :], reads=["u"], writes=[("U", s.name, t)])
                    dma("sp", um1[:nt, :], s.U[1 + r0:1 + r0 + nt, :], reads=[("U", s.name, t), ("U", s.name, t - 1)], writes=["um1"])
                    dma("sp", um2[:nt, :], s.U[r0:r0 + nt, :], reads=[("U", s.name, t), ("U", s.name, t - 1)], writes=["um2"])
                    if t == s.ntiles - 1:
                        dst = O["pcv_"][l] if s.prompt else O["scvo"][l, s.si]
                        dma("sp", dst, s.U[s.T:s.T + 2, :], reads=[("U", s.name, t)], is_output=True)

                def epiT(t):
                    r0 = t * nt
                    ka0 = s.ca_tiles * 128 + r0
                    kc0 = s.cc_tiles * 128 + r0
                    zi = (zbase + t) % 2
                    z = zbuf[zi]
                    zc0, zc1, zc2, zc3, zc4, zc5, zc6, zc7 = ["zb%dc%d" % (zi, i) for i in range(8)]
                    xt, rope = xts[zi], ropes[zi]
                    xtk, ropek = "xt%d" % zi, "rope%d" % zi
                    for b in range(8):
                        op("pe", lambda e, b=b: e.transpose(T1[:, b, :nt], stg[:nt, b * 128:(b + 1) * 128], idb[:nt, :nt]),
                           ["stg", "idb"], ["T1"])
                    op("dve", lambda e: e.tensor_copy(out=tT[:, :, :nt], in_=T1[:, :, :nt]), ["T1"], ["tT"])
                    dma("sp", s.AQT.rearrange("(b p) t -> p b t", p=128)[:, :, r0:r0 + nt], tT[:, 0:4, :nt], reads=["tT"],
                        writes=[("AQT", s.name, t)])
                    dma("sp", s.AKT.rearrange("(b p) t -> p b t", p=128)[:, :, ka0:ka0 + nt], tT[:, 4:8, :nt], reads=["tT"],
                        writes=[("AKT", s.name, s.ca_tiles + t)])
                    for b in range(4):
                        op("pe", lambda e, b=b: e.transpose(T1[:, b, :nt], stg[:nt, 1024 + b * 128:1024 + (b + 1) * 128], idb[:nt, :nt]),
                           ["stg", "idb"], ["T1"])
                    op("dve", lambda e: e.tensor_copy(out=tT[:, 0:4, :nt], in_=T1[:, 0:4, :nt]), ["T1"], ["tT"])
                    dma("sp", s.CQT.rearrange("(b p) t -> p b t", p=128)[:, :, r0:r0 + nt], tT[:, 0:2, :nt], reads=["tT"],
                        writes=[("CQT", s.name, t)])
                    dma("sp", s.CKT.rearrange("(b p) t -> p b t", p=128)[:, :, kc0:kc0 + nt], tT[:, 2:4, :nt], reads=["tT"],
                        writes=[("CKT", s.name, s.cc_tiles + t)])

                def epiB(t):
                    r0 = t * nt
                    ka0 = s.ca_tiles * 128 + r0
                    kc0 = s.cc_tiles * 128 + r0
                    zi = (zbase + t) % 2
                    z = zbuf[zi]
                    zc0, zc1, zc2, zc3, zc4, zc5, zc6, zc7 = ["zb%dc%d" % (zi, i) for i in range(8)]
                    xt, rope = xts[zi], ropes[zi]
                    xtk, ropek = "xt%d" % zi, "rope%d" % zi
                    op("dve", lambda e: e.tensor_copy(out=av1[:nt, :, 0:64], in_=z[:nt, 1024:1536].rearrange("p (h e) -> p h e", e=64)),
                       [zc2], ["av1"])
                    dma("sp", s.AV1[:nt, (s.ca_tiles + t) * 520:(s.ca_tiles + t + 1) * 520], av1[:nt].rearrange("p h e -> p (h e)"), reads=["av1"],
                        writes=[("AV1", s.name, s.ca_tiles + t)])
                    op("dve", lambda e: e.tensor_copy(out=cv1[:nt, :, 0:64], in_=z[:nt, 3584:3840].rearrange("p (h e) -> p h e", e=64)),
                       [zc7], ["cv1"])
                    dma("sp", s.CV1[kc0:kc0 + nt, :], cv1[:nt].rearrange("p h e -> p (h e)"), reads=["cv1"],
                        writes=[("CV1", s.name, s.cc_tiles + t)])
                    op("dve", lambda e: e.tensor_scalar(out=sgt[:nt, 0:512], in0=sgt[:nt, 0:512], scalar1=1.0, scalar2=None, op0=ALU.add), ["sgtA"], ["sgtA"])
                    op("dve", lambda e: e.reciprocal(out=sgt[:nt, 0:512], in_=sgt[:nt, 0:512]), ["sgtA"], ["sgtA"])
                    op("dve", lambda e: e.tensor_tensor(out=ga[:nt, :], in0=z[:nt, 1536:2048], in1=sgt[:nt, 0:512], op=ALU.mult), [zc3, "sgtA"], ["ga"])
                    dma("sp", s.GA[:nt, t * 512:(t + 1) * 512], ga[:nt, :], reads=["ga"], writes=[("GA", s.name, t)])
                    op("dve", lambda e: e.tensor_scalar(out=sgt[:nt, 512:768], in0=sgt[:nt, 512:768], scalar1=1.0, scalar2=None, op0=ALU.add), ["sgtC"], ["sgtC"])
                    op("dve", lambda e: e.reciprocal(out=sgt[:nt, 512:768], in_=sgt[:nt, 512:768]), ["sgtC"], ["sgtC"])
                    op("dve", lambda e: e.tensor_tensor(out=gc[:nt, :], in0=z[:nt, 3840:4096], in1=sgt[:nt, 512:768], op=ALU.mult), [zc7, "sgtC"], ["gc"])
                    dma("sp", s.GC[:nt, t * 256:(t + 1) * 256], gc[:nt, :], reads=["gc"], writes=[("GC", s.name, t)])
                    op("dve", lambda e: e.tensor_tensor(out=cvt[:nt, :], in0=u[:nt, :], in1=cw[:nt, 2, :], op=ALU.mult), ["u", "cw"], ["cvt"])
                    op("dve", lambda e: e.tensor_tensor(out=um1[:nt, :], in0=um1[:nt, :], in1=cw[:nt, 1, :], op=ALU.mult), ["um1", "cw"], ["um1"])
                    op("dve", lambda e: e.tensor_tensor(out=um2[:nt, :], in0=um2[:nt, :], in1=cw[:nt, 0, :], op=ALU.mult), ["um2", "cw"], ["um2"])
                    op("dve", lambda e: e.tensor_tensor(out=cvt[:nt, :], in0=cvt[:nt, :], in1=um1[:nt, :], op=ALU.add), ["cvt", "um1"], ["cvt"])
                    op("dve", lambda e: e.tensor_tensor(out=cvt[:nt, :], in0=cvt[:nt, :], in1=um2[:nt, :], op=ALU.add), ["cvt", "um2"], ["cvt"])
                    op("dve", lambda e: e.tensor_scalar(out=sgt[:nt, 768:1024], in0=sgt[:nt, 768:1024], scalar1=1.0, scalar2=None, op0=ALU.add), ["sgtB"], ["sgtB"])
                    op("dve", lambda e: e.reciprocal(out=sgt[:nt, 768:1024], in_=sgt[:nt, 768:1024]), ["sgtB"], ["sgtB"])
                    op("dve", lambda e: e.tensor_tensor(out=sg[:nt, :], in0=z[:nt, 2816:3072], in1=sgt[:nt, 768:1024], op=ALU.mult), [zc5, "sgtB"], ["sg"])
                    op("dve", lambda e: e.tensor_tensor(out=cvt[:nt, :], in0=cvt[:nt, :], in1=z[:nt, 2048:2304], op=ALU.mult), ["cvt", zc4], ["cvt"])
                    op("dve", lambda e: e.tensor_tensor(out=ob[:nt, :], in0=cvt[:nt, :], in1=sg[:nt, :], op=ALU.mult), ["cvt", "sg"], ["ob"])
                    dma("sp", s.OB[:nt, t * 256:(t + 1) * 256], ob[:nt, :], reads=["ob"], writes=[("OB", s.name, t)])

                frontA(0)
                frontB(0)
                for t in range(s.ntiles):
                    if t + 1 < s.ntiles:
                        frontA(t + 1)
                    epiA(t)
                    if t + 1 < s.ntiles:
                        frontB(t + 1, mid=lambda t=t: epiT(t))
                    else:
                        epiT(t)
                    epiB(t)
                ztile[0] += s.ntiles

            KT = BIG[:, 0:16384]
            V1 = BIG[:, 16384:16384 + 128 * 130].rearrange("p (t h e) -> p t h e", h=2, e=65)
            for s in seqs:
                nkt_all = (s.KC + 127) // 128
                nq = 512 if s.prompt else T_S
                split = s.prompt and l == 1
                nqb = NSLOT if split else s.T // nq
                allq = [("CQT", s.name, i) for i in range(s.ntiles)]
                allg = [("GC", s.name, i) for i in range(s.ntiles)]
                for hp in range(2):
                    dma("sp", KT[:, 0:s.KC], s.CKT[hp * 128:(hp + 1) * 128, :],
                        reads=[("CKT", s.name, i) for i in range(nkt_all)], writes=["BIG"])
                    for kt in range(nkt_all):
                        kn = min(128, s.KC - kt * 128)
                        dma("sp", V1[:kn, kt, :, :],
                            s.CV1[kt * 128:kt * 128 + kn, hp * 130:(hp + 1) * 130].rearrange("t (h e) -> t h e", e=65),
                            reads=[("CV1", s.name, kt)], writes=["BIG"])
                    pend = []

                    def flush_pv(keep=1):
                        while len(pend) > keep:
                            pend.pop(0)()
                    ucount = 0
                    for qb in range(nqb):
                        q0 = qb * nq
                        qz = Qz[qb % 2]
                        qk = "Qz%d" % (qb % 2)
                        for hh in range(2):
                            for m in range(2):
                                rws = slice(64 * hh + 32 * m, 64 * hh + 32 * m + 32)
                                if split:
                                    dma("sp", qz[hh][rws, m, :nq],
                                        lambda c, s=s, r_=hp * 128 + rws.start, qb=qb: s.CQT[r_: r_ + 32, blk_of(c, qb) * 512:(blk_of(c, qb) + 1) * 512],
                                        reads=allq, writes=[qk], percore=True)
                                else:
                                    dma("sp", qz[hh][rws, m, :nq], s.CQT[hp * 128 + rws.start: hp * 128 + rws.stop, q0:q0 + nq],
                                        reads=[("CQT", s.name, i) for i in range(q0 // s.nt, (q0 + nq) // s.nt)], writes=[qk])
                        for hh in range(2):
                            h = 2 * hp + hh
                            pb = slice(64 * hh, 64 * hh + 64)
                            nkt = (4 * qb + 4) if s.prompt else nkt_all
                            if split:
                                nkt = min(32 * (qb + 1), nkt_all)
                            for kt in range(nkt):
                                kn = min(128, s.KC - kt * 128)
                                sI = kt - 4 * qb if (s.prompt and not split) else -1
                                um = (kt - 32 * qb) if split else -1
                                if um >= 0:
                                    rm = rms[ucount % 4]
                                    rmk = "rm%d" % (ucount % 4)
                                    dma("sp", rm[:], I["cmask"][qb, um], writes=[rmk])
                                c0 = 128 * sI if sI > 0 else 0
                                par = ucount % 2
                                ucount += 1
                                sc = FA if par == 0 else FB
                                sck = ["FA0", "FA1"] if par == 0 else ["FB"]
                                pm = Pm[(ucount - 1) % 3]
                                pmk = "Pm%d" % ((ucount - 1) % 3)
                                for m in range(2):
                                    op("pe", lambda e, m=m: e.matmul(
                                        sc[:kn, m * 512 + c0: m * 512 + nq], lhsT=KT[:, kt * 128: kt * 128 + kn],
                                        rhs=qz[hh][:, m, c0:nq], start=True, stop=(um < 0)), ["BIG", qk], sck)
                                    if um >= 0:
                                        op("pe", lambda e, m=m: e.matmul(sc[:kn, m * 512: m * 512 + nq], lhsT=lmc[0:2, :kn], rhs=rm[0:2, :nq],
                                                                         start=False, stop=True), ["lmc", rmk], sck)
                                op("act", lambda e: e.activation(
                                    out=pm[:kn, :, c0:nq], in_=sc[:kn, :].rearrange("p (m q) -> p m q", m=2)[:, :, c0:nq],
                                    func=AF.Exp, scale=float(C_SCALE)), sck, [pmk])
                                if sI >= 0:
                                    op("pool", lambda e: e.memset(pm[64:128, :, c0:c0 + 64], 0.0), [pmk], [pmk])
                                flush_pv()

                                def pv(kt=kt, kn=kn, c0=c0, pm=pm, pmk=pmk, hh=hh, nkt=nkt, h=h, q0=q0, nq=nq, split=split):
                                    for m in range(2):
                                        op("pe", lambda e, m=m: e.matmul(
                                            FC[:, m * 512 + c0: m * 512 + nq],
                                            lhsT=BIG[:kn, 16384 + (kt * 2 + hh) * 65: 16384 + (kt * 2 + hh) * 65 + 128], rhs=pm[:kn, m, c0:nq],
                                            start=(kt == 0), stop=(kt == nkt - 1)), ["BIG", pmk], ["FC"])
                                    if kt == nkt - 1:
                                        c_epilogue(s, h, q0, nq, split, q0 // 512)
                                pend.append(pv)
                    flush_pv(0)

            WO = BIG[:, 0:8192].rearrange("p (k c) -> p k c", c=1024)
            for k in range(8):
                dma("sp", zbuf[0][:, 0:1024], I["wout"][l, k * 128:(k + 1) * 128, :], writes=["zb0c0", "zb0c1"])
                op("dve", lambda e, k=k: e.tensor_copy(out=WO[:, k, :], in_=zbuf[0][:, 0:1024]), reads=["zb0c0", "zb0c1"], writes=["BIG"])
            for s in seqs:
                nt = s.nt
                bias = biasp if s.prompt else biass
                split = s.prompt and l == 1
                tiles = [(sl, qt) for sl in range(NSLOT) for qt in range(4)] if split else [(None, t) for t in range(s.ntiles)]
                allt = lambda nm, n_=None: [(nm, s.name, i) for i in range(n_ if n_ is not None else s.ntiles)]
                def tparams(ti):
                    sl, t = tiles[ti]
                    pad = 4 if s.prompt else 0
                    gk = s.ca_tiles + t
                    k_lo = max(pad, gk - 4)
                    nk = 5 if split else gk - k_lo + 1
                    j0 = 0 if split else 4 - (gk - k_lo)
                    return sl, t, t * nt, ti, gk, k_lo, nk, j0, (nk - 1) * 128 + nt

                def kq_load(ti, p):
                    sl, t, r0, lt, gk, k_lo, nk, j0, kcols = tparams(ti)
                    KTb, QTa = KTbs[p % 2], QTas[p % 2]
                    kq = "KQ%d" % (p % 2)
                    if split:
                        dma("sp", KTb[:, 0:640], lambda c, s=s, p=p, lt=lt: s.AKT[p * 128:(p + 1) * 128, g_of(c, lt) * 128:(g_of(c, lt) + 5) * 128],
                            reads=allt("AKT", s.ntiles + 4), writes=[kq], percore=True)
                        dma("sp", QTa[:, :nt], lambda c, s=s, p=p, lt=lt: s.AQT[p * 128:(p + 1) * 128, g_of(c, lt) * 128:(g_of(c, lt) + 1) * 128],
                            reads=allt("AQT"), writes=[kq], percore=True)
                    else:
                        dma("sp", KTb[:, 0:kcols], s.AKT[p * 128:(p + 1) * 128, k_lo * 128:k_lo * 128 + kcols],
                            reads=[("AKT", s.name, i) for i in range(k_lo, gk + 1)], writes=[kq])
                        dma("sp", QTa[:, :nt], s.AQT[p * 128:(p + 1) * 128, r0:r0 + nt], reads=[("AQT", s.name, t)], writes=[kq])

                def tile_loads(ti):
                    sl, t, r0, lt, gk, k_lo, nk, j0, kcols = tparams(ti)
                    av1b, avk = av1bs[ti % 2], "av1b%d" % (ti % 2)
                    gat, gak = gats[ti % 2], "gat%d" % (ti % 2)
                    if split:
                        dma("sp", av1b[:, 0:5, :].rearrange("p t c -> p (t c)"), lambda c, s=s, lt=lt: s.AV1[:, g_of(c, lt) * 520:(g_of(c, lt) + 5) * 520],
                            reads=allt("AV1", s.ntiles + 4), writes=[avk], percore=True)
                        dma("sp", gat[:nt, :], lambda c, s=s, lt=lt: s.GA[:, g_of(c, lt) * 512:(g_of(c, lt) + 1) * 512], reads=allt("GA"), writes=[gak], percore=True)
                    else:
                        if nk > 1:
                            dma("sp", av1b[:, 0:nk - 1, :].rearrange("p t c -> p (t c)"), s.AV1[:, k_lo * 520:(k_lo + nk - 1) * 520],
                                reads=[("AV1", s.name, i) for i in range(k_lo, gk)], writes=[avk])
                        dma("sp", av1b[:nt, nk - 1, :], s.AV1[:nt, gk * 520:(gk + 1) * 520], reads=[("AV1", s.name, gk)], writes=[avk])
                        dma("sp", gat[:nt, :], s.GA[:nt, t * 512:(t + 1) * 512], reads=[("GA", s.name, t)], writes=[gak])
                    kq_load(ti, 0)

                tile_loads(0)
                for ti in range(len(tiles)):
                    sl, t, r0, lt, gk, k_lo, nk, j0, kcols = tparams(ti)
                    av1b, avk = av1bs[ti % 2], "av1b%d" % (ti % 2)
                    gat, gak = gats[ti % 2], "gat%d" % (ti % 2)
                    pend = [None]
                    for p in range(4):
                        KTb, QTa = KTbs[p % 2], QTas[p % 2]
                        kq = "KQ%d" % (p % 2)
                        if p + 1 < 4:
                            kq_load(ti, p + 1)
                        elif ti + 1 < len(tiles):
                            tile_loads(ti + 1)
                        for hh in range(2):
                            h = 2 * p + hh
                            pb = slice(64 * hh, 64 * hh + 64)
                            sc = FA if h % 2 == 0 else FB
                            sck = ["FA0", "FA1"] if h % 2 == 0 else ["FB"]
                            Pa = Pas[h % 2]
                            pak = "Pa%d" % (h % 2)
                            for j in range(nk):
                                kn = 128 if j < nk - 1 else nt
                                op("pe", lambda e: e.matmul(sc[:kn, j * 128: j * 128 + nt], lhsT=KTb[pb, j * 128: j * 128 + kn],
                                                            rhs=QTa[pb, :nt], start=True, stop=False), [kq], sck)
                                op("pe", lambda e: e.matmul(sc[:kn, j * 128: j * 128 + nt], lhsT=idb[:kn, :kn],
                                                            rhs=bias[:kn, h, j0 + j, :nt], start=False, stop=True),
                                   ["idb", "biasp", "biass"], sck)
                            if split:
                                for j in range(nk):
                                    op("act", lambda e: e.activation(out=Pa[:, j, :], in_=sc[:, j * 128:(j + 1) * 128], func=AF.Exp,
                                                                     bias=padb[:, lt * 5 + j: lt * 5 + j + 1]), sck + ["padb"], [pak])
                            elif nt == 128:
                                op("act", lambda e: e.activation(out=Pa[:, 0:nk, :].rearrange("p j q -> p (j q)"), in_=sc[:, 0:nk * 128], func=AF.Exp),
                                   sck, [pak])
                            else:
                                if nk > 1:
                                    op("act", lambda e: e.activation(out=Pa[:, 0:nk - 1, :nt],
                                                                     in_=sc[:, 0:(nk - 1) * 128].rearrange("p (j q) -> p j q", q=128)[:, :, :nt],
                                                                     func=AF.Exp), sck, [pak])
                                op("act", lambda e: e.activation(out=Pa[:nt, nk - 1, :nt], in_=sc[:nt, (nk - 1) * 128:(nk - 1) * 128 + nt], func=AF.Exp),
                                   sck, [pak])
                            if pend[0] is not None:
                                pend[0]()

                            def pv(h=h, Pa=Pa, pak=pak, nk=nk, nt=nt, av1b=av1b, avk=avk):
                                for j in range(nk):
                                    kn = 128 if j < nk - 1 else nt
                                    op("pe", lambda e: e.matmul(FC[:nt, h * 128: h * 128 + 65], lhsT=Pa[:kn, j, :nt],
                                                                rhs=av1b[:kn, j, h * 65:(h + 1) * 65], start=(j == 0), stop=(j == nk - 1)),
                                       [pak, avk], ["FC"])
                            pend[0] = pv
                    pend[0]()
                    op("dve", lambda e: e.reciprocal(out=ra[:nt, :], in_=FC[:nt, :].rearrange("p (h c) -> p h c", c=128)[:, :, 64]), ["FC"], ["ra"])
                    for h in range(8):
                        op("dve", lambda e, h=h: e.scalar_tensor_tensor(out=y[:nt, h * 64:(h + 1) * 64], in0=FC[:nt, h * 128: h * 128 + 64],
                                                                       scalar=ra[:nt, h:h + 1], in1=gat[:nt, h * 64:(h + 1) * 64],
                                                                       op0=ALU.mult, op1=ALU.mult), ["FC", "ra", gak], ["y"])
                    xsrc = s.x_in if l == 0 else s.X1
                    if split:
                        dma("sp", y[:nt, 512:768], lambda c, s=s, lt=lt: s.OB[:, g_of(c, lt) * 256:(g_of(c, lt) + 1) * 256], reads=allt("OB"), writes=["y"], percore=True)
                        dma("sp", y[:nt, 768:1024], s.YC2[lt * 128:(lt + 1) * 128, :], reads=[("YC2", s.name, lt, h) for h in range(4)], writes=["y"])
                        dma("sp", xr[:nt, :], lambda c, s=s, lt=lt: s.X1[:, g_of(c, lt) * 1024:(g_of(c, lt) + 1) * 1024], reads=allt("X1"), writes=["xr"], percore=True)
                    else:
                        dma("sp", y[:nt, 512:768], s.OB[:nt, t * 256:(t + 1) * 256], reads=[("OB", s.name, t)], writes=["y"])
                        dma("sp", y[:nt, 768:1024], s.YC[r0:r0 + nt, :], reads=[("YC", s.name, t, h) for h in range(4)], writes=["y"])
                        dma("sp", xr[:nt, :], s.x_in[r0:r0 + nt, :] if l == 0 else s.X1[:nt, t * 1024:(t + 1) * 1024],
                            reads=[("X1", s.name, t)] if l else [], writes=["xr"])
                    for k in range(8):
                        op("pe", lambda e, k=k: e.transpose(T0[:, k, :nt], y[:nt, k * 128:(k + 1) * 128], idb[:nt, :nt]), ["y", "idb"], ["T0"])
                    evac(yT[:, :, :nt], T0[:, :, :nt], ["T0"], ["yT"])
                    for cb in range(2):
                        for k in range(8):
                            op("pe", lambda e, k=k, cb=cb: e.matmul(FA[:nt, cb * 512:(cb + 1) * 512], lhsT=yT[:, k, :nt],
                                                                    rhs=WO[:, k, cb * 512:(cb + 1) * 512], start=(k == 0), stop=(k == 7)),
                               ["yT", "BIG"], ["FA%d" % cb])
                        op("dve", lambda e, cb=cb: e.tensor_tensor(out=xo[:nt, cb * 512:(cb + 1) * 512], in0=FA[:nt, cb * 512:(cb + 1) * 512],
                                                                   in1=xr[:nt, cb * 512:(cb + 1) * 512], op=ALU.add), ["FA%d" % cb, "xr"], ["xr"])
                    if l == 0:
                        dma("sp", s.X1[:nt, t * 1024:(t + 1) * 1024], xo[:nt, :], reads=["xr"], writes=[("X1", s.name, t)])
                    else:
                        op("pool", lambda e: e.tensor_tensor(out=sq[:nt, :], in0=xo[:nt, :], in1=xo[:nt, :], op=ALU.mult), ["xr"], ["sq"])
                        op("dve", lambda e: e.reduce_sum(out=st[:nt, 1:2], in_=sq[:nt, :], axis=AX.X), ["sq"], ["st1"])
                        op("dve", lambda e: e.tensor_scalar(out=st[:nt, 1:2], in0=st[:nt, 1:2], scalar1=1.0 / 1024, scalar2=1e-6,
                                                            op0=ALU.mult, op1=ALU.add), ["st1"], ["st1"])
                        op("act", lambda e: e.activation(out=st[:nt, 1:2], in_=st[:nt, 1:2], func=AF.Ln), ["st1"], ["st1"])
                        op("act", lambda e: e.activation(out=st[:nt, 1:2], in_=st[:nt, 1:2], func=AF.Exp, scale=-0.5), ["st1"], ["st1"])
                        op("dve", lambda e: e.scalar_tensor_tensor(out=xo[:nt, :], in0=xo[:nt, :], scalar=st[:nt, 1:2], in1=fing[:nt, :],
                                                                   op0=ALU.mult, op1=ALU.mult), ["xr", "st1", "fing"], ["xr"])
                        dst = O["yp"][lt * 128:(lt + 1) * 128, :] if s.prompt else O["ys"][s.si]
                        dma("sp", dst, xo[:nt, :], reads=["xr"], is_output=True)

        tr.finish()
        with nc.Block() as block:
            @block.tensor
            def _(e):
                for f in tr.ops["pe"]:
                    f(e)

            @block.scalar
            def _(e):
                for f in tr.ops["act"]:
                    f(e)

            @block.vector
            def _(e):
                for f in tr.ops["dve"]:
                    f(e)

            @block.gpsimd
            def _(e):
                for f in tr.ops["pool"]:
                    f(e)

            @block.sync
            def _(e):
                ops_sp = tr.ops["sp"]
                i_ = 0
                while i_ < len(ops_sp):
                    if getattr(ops_sp[i_], "percore", False):
                        j_ = i_
                        while j_ < len(ops_sp) and getattr(ops_sp[j_], "percore", False):
                            j_ += 1
                        for arm in e.switch_core_id(n=128):
                            for f in ops_sp[i_:j_]:
                                f(e, arm.logical % 8)
                        i_ = j_
                    else:
                        ops_sp[i_](e)
                        i_ += 1
    return nc


def _host_consts(rel_bias):
    import ml_dtypes
    kk = np.arange(128)[:, None]
    biasp = np.zeros((2, 128, 8, 5, 128), np.float32)
    for j in range(5):
        toff = j - 4
        q = np.arange(128)[None, :]
        dist = q - kk - 128 * toff
        idx = np.clip(dist, -128, 128) + 128
        vals = rel_bias[:, :, idx]
        qc, kc = (q // 64), ((kk + 128 * toff) // 64)
        ok = (kc <= qc) & (kc >= qc - 8)
        vals = np.where(ok[None, None], vals, np.float32(NEG))
        biasp[:, :, :, j, :] = vals.transpose(0, 2, 1, 3)
    biass = np.zeros((2, 128, 8, 5, T_S), np.float32)
    for j in range(5):
        r = j * 128 + kk
        q = np.arange(T_S)[None, :]
        dist = q + AWIN - r
        idx = np.clip(dist, -128, 128) + 128
        vals = rel_bias[:, :, idx]
        biass[:, :, :, j, :] = vals.transpose(0, 2, 1, 3)
    half = 4
    inv = (500000.0 ** (-np.arange(half, dtype=np.float32) * np.float32(2.0 / 8))).astype(np.float32)

    def rope_tab(pos):
        ang = pos.astype(np.float32)[:, None] * inv[None, :]
        c, s_ = np.cos(ang).astype(np.float32), np.sin(ang).astype(np.float32)
        return np.concatenate([np.tile(c, (1, 16)), np.tile(s_, (1, 16))], axis=1).astype(np.float32)
    lmc = np.zeros((2, 128), np.float32); lmc[0, :] = 1.0; lmc[1, 64:] = 1.0
    return dict(lmc=lmc.astype(ml_dtypes.bfloat16), biasp=biasp, biass=biass, ropep=rope_tab(np.arange(S_P)), ropes=rope_tab(PAST + np.arange(T_S)),
                idb=np.eye(128, dtype=np.float32).astype(ml_dtypes.bfloat16), idf=np.eye(128, dtype=np.float32))


def kernel(x_prompt, x_sample, cache_a_k, cache_a_v, state_conv, cache_c_k, cache_c_v,
           norm_g, w_in, w_out, rel_bias, conv_w, lam_q1, lam_k1, lam_q2, lam_k2, subln_g, final_g):
    f = lambda a: np.ascontiguousarray(np.asarray(a, dtype=np.float32))
    x_prompt, x_sample = f(x_prompt), f(x_sample)
    consts = _host_consts(f(rel_bias))
    shared = dict(
        xp=f(x_prompt[0]),
        ng=f(f(norm_g).reshape(2, 8, 128).transpose(0, 2, 1)),
        win=f(w_in), wout=f(w_out),
        convw=f(np.broadcast_to(f(conv_w)[:, :, None, :], (2, 3, 128, 256))),
        lam=f(np.broadcast_to(np.stack([f(lam_q1), f(lam_k1), f(lam_q2), f(lam_k2)], axis=1)[:, :, None, :], (2, 4, 128, 32))),
        subg=f(np.broadcast_to(f(subln_g)[:, None, :], (2, 128, 64))),
        fing=f(np.broadcast_to(f(final_g)[None, :], (128, 1024))),
        **consts,
    )
    import ml_dtypes
    in_maps = []
    blocks_of = []
    for c in range(NCORES):
        blk = [8 * j + (c if j % 2 == 0 else 7 - c) for j in range(NSLOT)]
        blocks_of.append(blk)
        cmask = np.zeros((NSLOT, 32, 2, 512), np.float32)
        padb = np.zeros((128, NSLOT * 20), np.float32)
        for j, b in enumerate(blk):
            for u in range(32):
                kt = 32 * j + u
                for sq_ in range(4):
                    gq = 4 * b + sq_
                    cols = slice(sq_ * 128, (sq_ + 1) * 128)
                    if kt > gq:
                        cmask[j, u, 0, cols] = NEG
                    elif kt == gq:
                        cmask[j, u, 1, sq_ * 128: sq_ * 128 + 64] = NEG
            for qt in range(4):
                g = 4 * b + qt
                for jj in range(5):
                    if g + jj - 4 < 0:
                        padb[:, (j * 4 + qt) * 5 + jj] = NEG
        shared_c = dict(cmask=cmask.astype(ml_dtypes.bfloat16), padb=padb)
        sl = slice(NSEQ * c, NSEQ * (c + 1))
        m = dict(shared)
        m.update(shared_c)
        m.update(
            xs=f(x_sample[sl]),
            cak=f(f(cache_a_k)[:, sl].reshape(2, NSEQ, AWIN, 512)), cav=f(f(cache_a_v)[:, sl].reshape(2, NSEQ, AWIN, 512)),
            scv=f(f(state_conv)[:, sl]),
            cck=f(f(cache_c_k)[:, sl].reshape(2, NSEQ, PAST, 256)), ccv=f(f(cache_c_v)[:, sl].reshape(2, NSEQ, PAST, 256)),
        )
        in_maps.append(m)
    nc = build_nc()
    res = run_bass_kernel_spmd(nc, in_maps, core_ids=list(range(NCORES)))
    R = res.results
    cat = lambda k, ax: np.concatenate([R[c][k] for c in range(NCORES)], axis=ax)
    r0 = R[0]
    yp_full = np.zeros((1, S_P, 1024), np.float32)
    for c in range(NCORES):
        for j, b in enumerate(blocks_of[c]):
            yp_full[0, b * 512:(b + 1) * 512] = R[c]["yp"][j * 512:(j + 1) * 512]
    return (
        yp_full,
        cat("ys", 0).reshape(16, T_S, 1024).astype(np.float32),
        r0["pak"].reshape(2, 1, 512, 8, 64), r0["pav"].reshape(2, 1, 512, 8, 64),
        r0["pconv"].reshape(2, 1, 2, 256),
        r0["pck"].reshape(2, 1, S_P, 4, 2, 32), r0["pcv"].reshape(2, 1, S_P, 4, 64),
        cat("sak", 1).reshape(2, 16, T_S, 8, 64), cat("sav", 1).reshape(2, 16, T_S, 8, 64),
        cat("sconv", 1).reshape(2, 16, 2, 256),
        cat("sck", 1).reshape(2, 16, T_S, 4, 2, 32), cat("scv2", 1).reshape(2, 16, T_S, 4, 64),
    )
```
